# Optimizing a Trainium2 kernel written in Bass

```python
import math
import jax, jax.numpy as jnp
from jax import lax
import numpy as np

D_MODEL = 1024
BATCH = 8
SEQ = 2048
DEPTH = 4
DEC_BATCH = 128
DEC_SEQ = 4
PAST_LEN = 16384
PAGE_SIZE = 128

GLA_HEADS = 4
GLA_DV = 3 * D_MODEL // (8 * GLA_HEADS)
GLA_DK = GLA_DV // 2
GLA_GATE_RANK = 16
GLA_GATE_TAU = 16.0
ML_HEADS = 4
ML_DH = 3 * D_MODEL // (8 * ML_HEADS)
ML_CONV = 4
HG_HEADS = 4
HG_DH = D_MODEL // (4 * HG_HEADS)

GLA_QK = GLA_HEADS * GLA_DK
GLA_V = GLA_HEADS * GLA_DV
ML_W = ML_HEADS * ML_DH
HG_W = HG_HEADS * HG_DH
N_BRANCH = 3
D_FF = 4 * D_MODEL
CHUNK = 64
DN_ALPHA = (2 * DEPTH) ** 0.25
DN_BETA = (8 * DEPTH) ** -0.25
LN_EPS = 1e-5
NORM_EPS = 1e-6

SEG_SIZES = (GLA_QK, GLA_QK, GLA_V, GLA_GATE_RANK, GLA_V,
             ML_W, ML_W, ML_HEADS, ML_HEADS, ML_W,
             HG_W, HG_W, HG_W, HG_W,
             N_BRANCH * D_MODEL)
N_IN = sum(SEG_SIZES)

kernel_name = "hybrid_gla_mlstm_hgrn2_decode_step"


def _split_cols(p):
    out = []
    off = 0
    for size in SEG_SIZES:
        out.append(p[..., off:off + size])
        off += size
    return out


def _layernorm(x, g, b):
    x32 = x.astype(jnp.float32)
    mu = jnp.mean(x32, axis=-1, keepdims=True)
    var = jnp.mean(jnp.square(x32 - mu), axis=-1, keepdims=True)
    return ((x32 - mu) * lax.rsqrt(var + LN_EPS) * g + b).astype(x.dtype)


def _head_rmsnorm(o, g):
    return o * lax.rsqrt(jnp.mean(jnp.square(o), axis=-1, keepdims=True) + NORM_EPS) * g


def _head_layernorm(o, g):
    mu = jnp.mean(o, axis=-1, keepdims=True)
    var = jnp.mean(jnp.square(o - mu), axis=-1, keepdims=True)
    return (o - mu) * lax.rsqrt(var + NORM_EPS) * g.reshape(o.shape[-2], o.shape[-1])


def _to_blocks(a, c):
    bsz, seq = a.shape[0], a.shape[1]
    return jnp.swapaxes(a.reshape((bsz, seq // c, c) + a.shape[2:]), 0, 1)


def _from_blocks(a):
    n, bsz, c = a.shape[0], a.shape[1], a.shape[2]
    return jnp.swapaxes(a, 0, 1).reshape((bsz, n * c) + a.shape[3:])


def _gated_linear_scan(q, k, v, g, s0):
    c = math.gcd(q.shape[1], CHUNK)
    causal = jnp.tril(jnp.ones((c, c), dtype=bool))[None, :, :, None, None]

    def step(s, blk):
        qc, kc, vc, gc = blk
        b = jnp.cumsum(gc, axis=1)
        o_inter = jnp.einsum("bthk,bhkv->bthv", qc * jnp.exp(b), s)
        rel = jnp.exp(jnp.where(causal, b[:, :, None] - b[:, None, :], -jnp.inf))
        att = jnp.sum(qc[:, :, None] * kc[:, None, :] * rel, axis=-1)
        o_intra = jnp.einsum("btsh,bshv->bthv", att, vc)
        b_last = b[:, -1]
        s_new = jnp.exp(b_last)[..., None] * s + jnp.einsum(
            "bshk,bshv->bhkv", kc * jnp.exp(b_last[:, None] - b), vc)
        return s_new, o_inter + o_intra

    xs = (_to_blocks(q, c), _to_blocks(k, c), _to_blocks(v, c), _to_blocks(g, c))
    s_end, o = lax.scan(step, s0.astype(jnp.float32), xs)
    return _from_blocks(o), s_end


def _mlstm_scan(q, k, v, log_i, log_f, c0, n0, m0):
    c = math.gcd(q.shape[1], CHUNK)
    causal = jnp.tril(jnp.ones((c, c), dtype=bool))[None, :, :, None]

    def step(carry, blk):
        cm, nv, m = carry
        qc, kc, vc, lic, lfc = blk
        b = jnp.cumsum(lfc, axis=1)
        m_t = b + jnp.maximum(m[:, None], lax.cummax(lic - b, axis=1))
        inter = jnp.exp(b + m[:, None] - m_t)
        logd = b[:, :, None] - b[:, None, :] + lic[:, None, :] - m_t[:, :, None]
        dmat = jnp.exp(jnp.where(causal, logd, -jnp.inf))
        sc = jnp.einsum("bthd,bshd->btsh", qc, kc) * dmat
        num = inter[..., None] * jnp.einsum("bthk,bhkv->bthv", qc, cm) + jnp.einsum("btsh,bshv->bthv", sc, vc)
        den = inter * jnp.einsum("bthk,bhk->bth", qc, nv) + jnp.sum(sc, axis=2)
        h = num / jnp.maximum(jnp.abs(den), 1.0)[..., None]
        m_end = m_t[:, -1]
        w = jnp.exp(b[:, -1:] - b + lic - m_end[:, None])
        carry_scale = jnp.exp(b[:, -1] + m - m_end)
        kw = kc * w[..., None]
        c_new = carry_scale[..., None, None] * cm + jnp.einsum("bshk,bshv->bhkv", kw, vc)
        n_new = carry_scale[..., None] * nv + jnp.sum(kw, axis=1)
        return (c_new, n_new, m_end), h

    xs = (_to_blocks(q, c), _to_blocks(k, c), _to_blocks(v, c), _to_blocks(log_i, c), _to_blocks(log_f, c))
    init = (c0.astype(jnp.float32), n0.astype(jnp.float32), m0.astype(jnp.float32))
    (c_end, n_end, m_end), h = lax.scan(step, init, xs)
    return _from_blocks(h), c_end, n_end, m_end


def _causal_conv(x, buf, w, bias):
    xp = jnp.concatenate([buf.astype(x.dtype), x], axis=1)
    seq = x.shape[1]
    y = bias + xp[:, 0:seq] * w[0]
    for j in range(1, ML_CONV):
        y = y + xp[:, j:j + seq] * w[j]
    return y, xp[:, xp.shape[1] - (ML_CONV - 1):]


def _layer(x, states, lp, lb):
    s_gla, s_c, s_n, s_m, s_conv, s_hg = states
    bsz, seq, _ = x.shape
    proj = (x @ lp["w_in"] + lp["b_in"]).astype(jnp.float32)
    (gq, gk, gv, ga, gr, mx, mv, mi, mf, mo, hq, hf, hi, hgt, mg) = _split_cols(proj)

    q = gq.reshape(bsz, seq, GLA_HEADS, GLA_DK) * (GLA_DK ** -0.5)
    k = gk.reshape(bsz, seq, GLA_HEADS, GLA_DK)
    v = gv.reshape(bsz, seq, GLA_HEADS, GLA_DV)
    g = (jax.nn.log_sigmoid(ga @ lp["w_gla_gate"] + lp["b_gla_gate"]) / GLA_GATE_TAU).reshape(bsz, seq, GLA_HEADS, GLA_DK)
    o, s_gla_new = _gated_linear_scan(q, k, v, g, s_gla)
    h_gla = _head_rmsnorm(o, lp["gla_norm_g"]).reshape(bsz, seq, GLA_V) * jax.nn.silu(gr)

    xc, s_conv_new = _causal_conv(mx, s_conv, lp["ml_conv_w"], lp["ml_conv_b"])
    xc = jax.nn.silu(xc).reshape(bsz, seq, ML_HEADS, ML_DH)
    q = jnp.einsum("blhd,hde->blhe", xc, lp["w_ml_q"])
    k = jnp.einsum("blhd,hde->blhe", xc, lp["w_ml_k"]) * (ML_DH ** -0.5)
    v = mv.reshape(bsz, seq, ML_HEADS, ML_DH)
    log_f = jax.nn.log_sigmoid(mf + lp["b_ml_f"])
    h, c_new, n_new, m_new = _mlstm_scan(q, k, v, mi, log_f, s_c, s_n, s_m)
    h_ml = _head_layernorm(h, lp["ml_norm_g"]).reshape(bsz, seq, ML_W) * jax.nn.sigmoid(mo)

    lbh = lb.reshape(HG_HEADS, HG_DH)
    hf = hf.reshape(bsz, seq, HG_HEADS, HG_DH)
    log_fh = jnp.logaddexp(jnp.log(lbh), jnp.log1p(-lbh) + jax.nn.log_sigmoid(hf))
    kh = (1.0 - lbh) * jax.nn.sigmoid(-hf)
    qh = jax.nn.silu(hq).reshape(bsz, seq, HG_HEADS, HG_DH) * (HG_DH ** -0.5)
    vh = hi.reshape(bsz, seq, HG_HEADS, HG_DH)
    o, s_hg_new = _gated_linear_scan(qh, kh, vh, log_fh, s_hg)
    h_hg = _head_rmsnorm(o, lp["hg_norm_g"]).reshape(bsz, seq, HG_W) * jax.nn.silu(hgt)

    gates = jax.nn.sigmoid(mg).reshape(bsz, seq, N_BRANCH, D_MODEL)
    merged = (gates[:, :, 0] * (h_gla @ lp["w_up_gla"])
              + gates[:, :, 1] * (h_ml @ lp["w_up_ml"])
              + gates[:, :, 2] * (h_hg @ lp["w_up_hg"]))
    y_mix = (merged @ lp["w_out"]).astype(x.dtype)
    x = _layernorm(DN_ALPHA * x + y_mix, lp["ln1_g"], lp["ln1_b"])

    hid = jnp.square(jax.nn.relu(x @ lp["w_ff_up"]))
    x = _layernorm(DN_ALPHA * x + (hid @ lp["w_ff_down"]).astype(x.dtype), lp["ln2_g"], lp["ln2_b"])
    return x, (s_gla_new, c_new, n_new, m_new, s_conv_new, s_hg_new)


def setup_inputs(seed: int = 0) -> dict:
    key = jax.random.key(seed)
    ks = jax.random.split(key, 40)

    def nrm(k, shape, scale):
        return jax.random.normal(k, shape, jnp.float32) * scale

    return {
        "x_prompt": nrm(ks[0], (BATCH, SEQ, D_MODEL), 1.0),
        "x_sample": nrm(ks[1], (DEC_BATCH, DEC_SEQ, D_MODEL), 1.0),
        "state_gla": nrm(ks[2], (DEPTH, DEC_BATCH, GLA_HEADS, GLA_DK, GLA_DV), 1.0),
        "state_mlstm_C": nrm(ks[3], (DEPTH, DEC_BATCH, ML_HEADS, ML_DH, ML_DH), 1.0),
        "state_mlstm_n": nrm(ks[4], (DEPTH, DEC_BATCH, ML_HEADS, ML_DH), 1.0),
        "state_mlstm_m": nrm(ks[5], (DEPTH, DEC_BATCH, ML_HEADS), 1.0),
        "state_mlstm_conv": nrm(ks[6], (DEPTH, DEC_BATCH, ML_CONV - 1, ML_W), 1.0),
        "state_hgrn": nrm(ks[7], (DEPTH, DEC_BATCH, HG_HEADS, HG_DH, HG_DH), 0.5),
        "w_in": nrm(ks[8], (DEPTH, D_MODEL, N_IN), D_MODEL ** -0.5),
        "b_in": nrm(ks[9], (DEPTH, N_IN), 0.02),
        "w_gla_gate": nrm(ks[10], (DEPTH, GLA_GATE_RANK, GLA_QK), GLA_GATE_RANK ** -0.5),
        "b_gla_gate": nrm(ks[11], (DEPTH, GLA_QK), 0.02),
        "gla_norm_g": 1.0 + nrm(ks[12], (DEPTH, GLA_DV), 0.02),
        "ml_conv_w": nrm(ks[13], (DEPTH, ML_CONV, ML_W), ML_CONV ** -0.5),
        "ml_conv_b": nrm(ks[14], (DEPTH, ML_W), 0.02),
        "w_ml_q": nrm(ks[15], (DEPTH, ML_HEADS, ML_DH, ML_DH), ML_DH ** -0.5),
        "w_ml_k": nrm(ks[16], (DEPTH, ML_HEADS, ML_DH, ML_DH), ML_DH ** -0.5),
        "b_ml_f": jnp.broadcast_to(jnp.linspace(3.0, 6.0, ML_HEADS, dtype=jnp.float32), (DEPTH, ML_HEADS)) + nrm(ks[17], (DEPTH, ML_HEADS), 0.02),
        "ml_norm_g": 1.0 + nrm(ks[18], (DEPTH, ML_W), 0.02),
        "hg_lb_logits": nrm(ks[19], (DEPTH, HG_W), 1.0),
        "hg_norm_g": 1.0 + nrm(ks[20], (DEPTH, HG_DH), 0.02),
        "w_up_gla": nrm(ks[21], (DEPTH, GLA_V, D_MODEL), GLA_V ** -0.5),
        "w_up_ml": nrm(ks[22], (DEPTH, ML_W, D_MODEL), ML_W ** -0.5),
        "w_up_hg": nrm(ks[23], (DEPTH, HG_W, D_MODEL), HG_W ** -0.5),
        "w_out": nrm(ks[24], (DEPTH, D_MODEL, D_MODEL), DN_BETA * D_MODEL ** -0.5),
        "ln1_g": 1.0 + nrm(ks[25], (DEPTH, D_MODEL), 0.02),
        "ln1_b": nrm(ks[26], (DEPTH, D_MODEL), 0.02),
        "ln2_g": 1.0 + nrm(ks[27], (DEPTH, D_MODEL), 0.02),
        "ln2_b": nrm(ks[28], (DEPTH, D_MODEL), 0.02),
        "w_ff_up": nrm(ks[29], (DEPTH, D_MODEL, D_FF), D_MODEL ** -0.5),
        "w_ff_down": nrm(ks[30], (DEPTH, D_FF, D_MODEL), DN_BETA * D_FF ** -0.5),
    }


def reference(x_prompt, x_sample, state_gla, state_mlstm_C, state_mlstm_n, state_mlstm_m, state_mlstm_conv, state_hgrn,
              w_in, b_in, w_gla_gate, b_gla_gate, gla_norm_g, ml_conv_w, ml_conv_b, w_ml_q, w_ml_k, b_ml_f, ml_norm_g,
              hg_lb_logits, hg_norm_g, w_up_gla, w_up_ml, w_up_hg, w_out, ln1_g, ln1_b, ln2_g, ln2_b, w_ff_up, w_ff_down):
    f32 = jnp.float32
    sm = jax.nn.softmax(hg_lb_logits.astype(f32), axis=0)
    lower_bounds = jnp.concatenate([jnp.zeros_like(sm[:1]), jnp.cumsum(sm[1:], axis=0)], axis=0)

    bp = x_prompt.shape[0]
    zero_states = (jnp.zeros((bp, GLA_HEADS, GLA_DK, GLA_DV), f32),
                   jnp.zeros((bp, ML_HEADS, ML_DH, ML_DH), f32),
                   jnp.zeros((bp, ML_HEADS, ML_DH), f32),
                   jnp.zeros((bp, ML_HEADS), f32),
                   jnp.zeros((bp, ML_CONV - 1, ML_W), f32),
                   jnp.zeros((bp, HG_HEADS, HG_DH, HG_DH), f32))

    xp, xs = x_prompt, x_sample
    new_p = [[] for _ in range(6)]
    new_s = [[] for _ in range(6)]
    for l in range(DEPTH):
        lp = {"w_in": w_in[l], "b_in": b_in[l], "w_gla_gate": w_gla_gate[l], "b_gla_gate": b_gla_gate[l],
              "gla_norm_g": gla_norm_g[l], "ml_conv_w": ml_conv_w[l], "ml_conv_b": ml_conv_b[l],
              "w_ml_q": w_ml_q[l], "w_ml_k": w_ml_k[l], "b_ml_f": b_ml_f[l], "ml_norm_g": ml_norm_g[l],
              "hg_norm_g": hg_norm_g[l], "w_up_gla": w_up_gla[l], "w_up_ml": w_up_ml[l], "w_up_hg": w_up_hg[l],
              "w_out": w_out[l], "ln1_g": ln1_g[l], "ln1_b": ln1_b[l], "ln2_g": ln2_g[l], "ln2_b": ln2_b[l],
              "w_ff_up": w_ff_up[l], "w_ff_down": w_ff_down[l]}
        lb = lower_bounds[l]
        xp, st_p = _layer(xp, zero_states, lp, lb)
        st_in = (state_gla[l], state_mlstm_C[l], state_mlstm_n[l], state_mlstm_m[l], state_mlstm_conv[l], state_hgrn[l])
        xs, st_s = _layer(xs, st_in, lp, lb)
        for j in range(6):
            new_p[j].append(st_p[j])
            new_s[j].append(st_s[j])

    gla_p, mc_p, mn_p, mm_p, mconv_p, hg_p = [jnp.stack(a, axis=0) for a in new_p]
    gla_s, mc_s, mn_s, mm_s, mconv_s, hg_s = [jnp.stack(a, axis=0) for a in new_s]
    return (xp, xs, gla_p, mc_p, mn_p, mm_p, mconv_p, hg_p, gla_s, mc_s, mn_s, mm_s, mconv_s, hg_s)
```

```python
import math
import os
import numpy as np
import concourse.bass as bass
import concourse.mybir as mybir
from concourse.bass_utils import run_bass_kernel_spmd

F32 = mybir.dt.float32
BF16 = mybir.dt.bfloat16
AF = mybir.ActivationFunctionType
ALU = mybir.AluOpType

SAME_ENG_SYNC = True
STAGE_LOG = []

D = 1024
DEPTH = 4
NCORES = 8
NP = 1024
NSEQ = 8
NS = NSEQ * 4
T = NP + NS
TTS = [(0, 512), (512, 512), (1024, NS)]
R128 = [(r * 128, 128) for r in range(8)] + [(NP, NS)]
NCH = 16
SEG = dict(gq=0, gk=192, gv=384, ga=768, gr=784, mx=1168, mv=1552, mi=1936, mf=1940, mo=1944,
           hq=2328, hf=2584, hi=2840, hgt=3096, mg=3352)
N_IN = 6424
DN_ALPHA = (2 * DEPTH) ** 0.25
LN_EPS = 1e-5
NORM_EPS = 1e-6

PLC = {}
_c = 0
for _n, _w in [("b_ga", 1), ("b_gq", 2), ("b_gk", 2), ("bg", 2), ("b_gr", 4), ("gng", 1),
               ("b_mx", 4), ("cw", 16), ("cb", 4), ("b_mi", 1), ("b_mf", 1), ("bmlf", 1), ("b_mo", 4), ("mng", 4),
               ("b_hq", 2), ("b_hf", 2), ("b_hgt", 2), ("hng", 1), ("b_mg", 24),
               ("ln1g", 8), ("ln1b", 8), ("ln2g", 8), ("ln2b", 8)]:
    PLC[_n] = _c
    _c += _w
NPC = _c


class _Op:
    __slots__ = ("eng", "fn", "deps", "is_dma", "dkey", "dcount", "ms", "msc", "fs")

    def __init__(self, eng, fn, is_dma=False, dkey=None):
        self.eng = eng
        self.fn = fn
        self.deps = []
        self.is_dma = is_dma
        self.dkey = dkey
        self.dcount = 0
        self.ms = False
        self.msc = 0
        self.fs = False


class Prog:
    ENGS = ("pe", "act", "dve", "pool", "sp")

    def __init__(self, nc):
        self.nc = nc
        self.eng_ops = {e: [] for e in self.ENGS}
        self.kw = {}
        self.kr = {}
        self.dcnt = {}
        self.bar_ops = []
        self.bar_done = set()

    def barrier(self):
        self.bar_ops = [self.eng_ops[e][-1] for e in self.ENGS if self.eng_ops[e] and not self.eng_ops[e][-1].is_dma]
        self.bar_done = set()

    def _deps(self, op, reads, writes, joins):
        deps = op.deps
        if self.bar_ops and op.eng not in self.bar_done:
            deps.extend(self.bar_ops)
            self.bar_done.add(op.eng)
        for k in reads:
            deps.extend(self.kw.get(k, ()))
        for k in writes:
            deps.extend(self.kw.get(k, ()))
            deps.extend(self.kr.get(k, ()))
        for k in joins:
            deps.extend(self.kr.get(k, ()))
        for k in reads:
            self.kr.setdefault(k, []).append(op)
        for k in writes:
            self.kw[k] = [op]
            self.kr[k] = []
        for k in joins:
            self.kw.setdefault(k, []).append(op)

    def add(self, eng, fn, reads=(), writes=(), joins=()):
        op = _Op(eng, fn)
        self._deps(op, reads, writes, joins)
        self.eng_ops[eng].append(op)
        return op

    def dma(self, eng, fn, dkey, reads=(), writes=(), joins=()):
        op = _Op(eng, fn, is_dma=True, dkey=dkey)
        self._deps(op, reads, writes, joins)
        self.dcnt[dkey] = self.dcnt.get(dkey, 0) + 16
        op.dcount = self.dcnt[dkey]
        self.eng_ops[eng].append(op)
        return op

    @staticmethod
    def _skip(d, op):
        return (not d.is_dma) and d.eng == op.eng and (not op.is_dma) and (not op.fs) and (d.eng == "pe" or not SAME_ENG_SYNC)

    def emit(self):
        nc = self.nc
        for e in self.ENGS:
            for op in self.eng_ops[e]:
                for d in op.deps:
                    if d.is_dma or self._skip(d, op):
                        continue
                    d.ms = True
        for e in self.ENGS:
            c = 0
            for op in self.eng_ops[e]:
                if op.ms and not op.is_dma:
                    c += 1
                    op.msc = c
        esem = {e: nc.alloc_semaphore("es_" + e) for e in self.ENGS}
        dsem = {k: nc.alloc_semaphore("ds_%d" % i) for i, k in enumerate(self.dcnt)}
        prog = self

        def run(e, eng):
            waited = {}
            for op in prog.eng_ops[e]:
                need = {}
                for d in op.deps:
                    if d.is_dma:
                        key = ("d", d.dkey)
                        val = d.dcount
                    else:
                        if prog._skip(d, op):
                            continue
                        key = ("e", d.eng)
                        val = d.msc
                    if val > need.get(key, 0):
                        need[key] = val
                for key, val in need.items():
                    if waited.get(key, 0) >= val:
                        continue
                    waited[key] = val
                    eng.wait_ge(dsem[key[1]] if key[0] == "d" else esem[key[1]], val)
                ins = op.fn(eng)
                if op.is_dma:
                    ins.then_inc(dsem[op.dkey], 16)
                elif op.ms:
                    ins.then_inc(esem[e], 1)
            last = {}
            for op in prog.eng_ops[e]:
                if op.is_dma:
                    last[op.dkey] = max(last.get(op.dkey, 0), op.dcount)
            for k, v in last.items():
                if waited.get(("d", k), 0) < v:
                    eng.wait_ge(dsem[k], v)

        with nc.Block() as block:
            @block.sync
            def _(eng):
                run("sp", eng)

            @block.tensor
            def _(eng):
                run("pe", eng)

            @block.scalar
            def _(eng):
                run("act", eng)

            @block.vector
            def _(eng):
                run("dve", eng)

            @block.gpsimd
            def _(eng):
                run("pool", eng)


class Ring:
    def __init__(self, aps, name):
        self.aps = aps
        self.name = name
        self.i = 0

    def next(self):
        i = self.i % len(self.aps)
        self.i += 1
        return self.aps[i], "%s%d" % (self.name, i)


def build_program(n_layers=DEPTH, halves=(0, 1), dbg=None):
    nc = bass.Bass("TRN2", target_bir_lowering=False)
    P = Prog(nc)
    dbg_outs = {}

    def din(name, shape, dt=F32):
        return nc.dram_tensor(name, list(shape), dt, kind="ExternalInput").ap()

    def dout(name, shape):
        return nc.dram_tensor(name, list(shape), F32, kind="ExternalOutput").ap()

    def sb(name, shape, dt=F32):
        return nc.alloc_sbuf_tensor(name, list(shape), dt).ap()

    xT = din("xT", [2, 128, 8, T])
    d_sgla = din("sgla", [4, 16, 4, 48, 96])
    d_smC = din("smC", [4, 16, 4, 96, 96])
    d_smn = din("smn", [4, 16, 4, 96])
    d_smm = din("smm", [4, 16, 4])
    d_smconv = din("smconv", [4, 16, 3, 384])
    d_shg = din("shg", [4, 16, 4, 64, 64])
    w_in = din("w_in", [4, D, N_IN])
    b_in = din("b_in", [4, N_IN])
    w_out = din("w_out", [4, D, D])
    w_ffu = din("w_ff_up", [4, D, 4 * D])
    w_ffd = din("w_ff_down", [4, 4 * D, D])
    w_upg = din("w_up_gla", [4, 384, D])
    w_upm = din("w_up_ml", [4, 384, D])
    w_uph = din("w_up_hg", [4, 256, D])
    w_gg = din("w_gla_gate", [4, 16, 192])
    w_mq = din("w_ml_q", [4, 4, 96, 96])
    w_mk = din("w_ml_k", [4, 4, 96, 96])
    d_PL = din("PL", [4, 128, NPC])
    d_hglog = din("hglog", [128, 2, 4])
    d_identF = din("c_identF", [128, 128])
    d_ones128 = din("c_ones128", [128, 128])
    d_ones96 = din("c_ones96", [128, 96])
    d_ones64b = din("c_ones64b", [128, 128])
    d_emean = din("c_emean", [128, 97])
    d_e96 = din("c_e96", [128, 97])
    d_maskP = din("c_maskP", [128, 256])
    d_maskS = din("c_maskS", [32, 128])
    d_blkS = din("c_blkS", [32, 8])
    d_sel4 = din("c_sel4", [4, 4 * 97])
    d_rmask = din("c_rmask", [128, T])

    yT = dout("yT", [2, 128, 8, T])
    o_gla_p = dout("gla_p", [4, 1, 4, 48, 96])
    o_mC_p = dout("mC_p", [4, 1, 4, 96, 96])
    o_mn_p = dout("mn_p", [4, 1, 4, 96])
    o_mm_p = dout("mm_p", [4, 1, 4])
    o_mconv_p = dout("mconv_p", [4, 1, 3, 384])
    o_hg_p = dout("hg_p", [4, 1, 4, 64, 64])
    o_gla_s = dout("gla_s", [4, 16, 4, 48, 96])
    o_mC_s = dout("mC_s", [4, 16, 4, 96, 96])
    o_mn_s = dout("mn_s", [4, 16, 4, 96])
    o_mm_s = dout("mm_s", [4, 16, 4])
    o_mconv_s = dout("mconv_s", [4, 16, 3, 384])
    o_hg_s = dout("hg_s", [4, 16, 4, 64, 64])

    xf = sb("xf", [128, 8, T])
    xb = sb("xb", [128, 8, T], BF16)
    bigraw = sb("bigraw", [128, 4 * T])
    big = bigraw.bitcast(BF16).rearrange("p (k t) -> p k t", k=8)
    obuf = bigraw.rearrange("p (h t) -> p h t", h=4)
    rows = bigraw[0:4, :].rearrange("p (r t) -> p r t", r=4)
    hG = sb("hG", [128, 4, T], BF16)
    hM = sb("hM", [128, 4, T], BF16)
    hH = sb("hH", [128, 2, T], BF16)
    WSLOT = 4352
    wslots = [sb("wslot%d" % i, [128, WSLOT], BF16) for i in range(3)]
    qbuf = sb("qbuf", [128, 4, T], BF16)
    kbuf = sb("kbuf", [128, 4, T], BF16)
    sA = sb("sA", [128, 2 * T])
    sB = sb("sB", [128, 2 * T])
    vTM = sb("vTM", [128, 9, 388], BF16)
    kTM = sb("kTM", [128, 9, 384], BF16)
    xcb = sb("xcb", [128, T], BF16)
    tmpF = Ring([sb("tmpF%d" % i, [128, 512]) for i in range(4)], "tf")
    attb = Ring([sb("attb%d" % i, [128, 256], BF16) for i in range(2)], "ab")
    Sg = [sb("Sg%d" % l, [128, 2, 96]) for l in range(DEPTH)]
    Cm = [sb("Cm%d" % l, [128, 4, 97]) for l in range(DEPTH)]
    Sh = [sb("Sh%d" % l, [128, 2, 64]) for l in range(DEPTH)]
    Sbf = sb("Sbf", [128, 4 * 97], BF16)
    Sbf2 = sb("Sbf2", [128, 4 * 97], BF16)
    sstR = [sb("sst0", [128, 8, 97]), sb("sst1", [128, 8, 97])]
    nin = sb("nin", [128, 4, 8])
    nout = sb("nout", [128, 4, 8])
    sstb = sb("sstb", [128, 8, 97], BF16)
    vblk = sb("vblk", [32, 8, 97], BF16)
    PLs = sb("PLs", [128, 4, NPC])
    DPs = sb("DPs", [128, 4, 4])
    lbs = sb("lbs", [128, 2, 4])
    omlb = sb("omlb", [128, 2, 4])
    hgl = sb("hgl", [128, 2, 4])
    hgt_ = sb("hgt_", [128, 2, 4])
    eA = sb("eA", [128, 2, 24])
    eB = sb("eB", [128, 2, 24])
    eC = sb("eC", [128, 2, 24])
    carry = sb("carry", [128, 4, 24])
    crow = sb("crow", [4, 24])
    crow2 = sb("crow2", [4, 24])
    m0s = sb("m0s", [4, 8])
    msout = sb("msout", [4, 8])
    msave = [sb("msave%d" % l, [4, 1]) for l in range(DEPTH)]
    mi0 = sb("mi0", [4, 1])
    ctail = [sb("ctail%d" % l, [128, 4, 3]) for l in range(DEPTH)]
    sext = sb("sext", [128, 8, 7])
    cst = sb("cst", [32, 384])
    wg_s = sb("wg_s", [16, 256], BF16)
    wq_s = sb("wq_s", [128, 4, 96], BF16)
    wk_s = sb("wk_s", [128, 4, 96], BF16)
    gaT = sb("gaT", [16, T], BF16)
    identF = sb("identF", [32, 32])
    identB = sb("identB", [128, 128], BF16)
    ones128b = sb("ones128b", [128, 128], BF16)
    ones96 = sb("ones96", [128, 96])
    ones64b = sb("ones64b", [128, 128])
    emean = sb("emean", [128, 97])
    e96 = sb("e96", [128, 97])
    maskP = sb("maskP", [128, 256], BF16)
    maskS = sb("maskS", [32, 128], BF16)
    blkS = sb("blkS", [32, 8])
    sel4 = sb("sel4", [4, 4, 97])
    rmask = sb("rmask", [128, T], BF16)
    PS = Ring([nc.alloc_psum_tensor("psb%d" % i, [128, 512], F32).ap() for i in range(8)], "ps")

    def A(fn, r=(), w=(), j=()):
        return P.add("act", fn, reads=r, writes=w, joins=j)

    def V(fn, r=(), w=(), j=()):
        return P.add("dve", fn, reads=r, writes=w, joins=j)

    pe_last = {}

    def emit_bank(seq, pok):
        seen = set()
        for idx, (oo, lh, rh, base, qs, rd, last) in enumerate(seq):
            st_ = qs not in seen
            seen.add(qs)
            PE(lambda e, oo=oo, lh=lh, rh=rh, st_=st_, last=last: e.matmul(oo, lhsT=lh, rhs=rh, start=st_, stop=last, skip_group_check=True),
               r=rd, w=[pok] if idx == 0 else (), j=() if idx == 0 else [pok], base=base)

    def G(fn, r=(), w=(), j=()):
        return P.add("pool", fn, reads=r, writes=w, joins=j)

    def PE(fn, r=(), w=(), j=(), base=None):
        op = P.add("pe", fn, reads=r, writes=w, joins=j)
        if base is not None:
            for bk in list(w) + list(j):
                prev = pe_last.get(bk)
                if prev is not None and bk in j and prev[1] != base:
                    op.deps.append(prev[0])
                    op.fs = True
                pe_last[bk] = (op, base)
        return op

    def act(out, in_, func, bias=0.0, scale=1.0):
        return lambda e: e.activation(out=out, in_=in_, func=func, bias=bias, scale=scale)

    def tt(out, in0, in1, op):
        return lambda e: e.tensor_tensor(out=out, in0=in0, in1=in1, op=op)

    def ts(out, in0, s1, op0, s2=None, op1=None):
        if op1 is None:
            return lambda e: e.tensor_scalar(out=out, in0=in0, scalar1=s1, scalar2=None, op0=op0)
        return lambda e: e.tensor_scalar(out=out, in0=in0, scalar1=s1, scalar2=s2, op0=op0, op1=op1)

    def stt(out, in0, scalar, in1, op0, op1):
        return lambda e: e.scalar_tensor_tensor(out=out, in0=in0, scalar=scalar, in1=in1, op0=op0, op1=op1)

    def mm(out, lhsT, rhs, start=True, stop=True):
        return lambda e: e.matmul(out, lhsT=lhsT, rhs=rhs, start=start, stop=stop)

    def tr(out, in_, ident):
        return lambda e: e.transpose(out=out, in_=in_, identity=ident)

    def ld(q, out, in_, dkey, w=(), j=(), r=(), nonc=False):
        if nonc:
            return P.dma(q, lambda e: e.dma_start(out=out, in_=in_, allow_slow_non_contiguous=True), dkey, reads=r, writes=w, joins=j)
        return P.dma(q, lambda e: e.dma_start(out=out, in_=in_), dkey, reads=r, writes=w, joins=j)

    def dump(name, ap, keys, shape):
        if dbg is None or name not in dbg:
            return
        t = sb("dbgs_" + name, shape)
        V(lambda e: e.tensor_copy(out=t, in_=ap), r=keys, w=["dbg_" + name])
        o = dout("dbg_" + name, shape)
        dbg_outs[name] = o
        ld("sp", o, t, "dbg_" + name, r=["dbg_" + name])

    for i_, t_ in enumerate(wslots + [wg_s, qbuf, kbuf, sA, sB, bigraw, vTM, kTM, xcb, Sbf, Sbf2, hG, hM, hH, wq_s, wk_s, gaT,
                                      eA, eB, eC, carry, cst] + tmpF.aps + attb.aps
                            + [sstR[0].rearrange("p b v -> p (b v)"), sstR[1].rearrange("p b v -> p (b v)"), sstb.rearrange("p b v -> p (b v)"),
                               sext.rearrange("p b v -> p (b v)"), vblk.rearrange("p b v -> p (b v)")]):
        nm_ = "z%d" % i_
        V(lambda e, t_=t_: e.memset(t_, 0.0), w=[nm_])
    for i_, t_ in enumerate(PS.aps):
        V(lambda e, t_=t_: e.memset(t_, 0.0), w=["ps%d" % i_])
    P.barrier()
    for dst, src in [(identF, d_identF[0:32, 0:32]), (ones96, d_ones96), (ones64b, d_ones64b), (emean, d_emean), (e96, d_e96),
                     (blkS, d_blkS), (sel4, d_sel4.rearrange("p (h n) -> p h n", h=4)),
                     (hgl, d_hglog), (PLs, d_PL.rearrange("l p n -> p l n"))]:
        ld("sp", dst, src, "const", j=["const"])
    for dst, src in [(identB, d_identF), (ones128b, d_ones128), (rmask, d_rmask), (maskP, d_maskP), (maskS, d_maskS)]:
        ld("pool", dst, src, "constb", j=["constb"])
    C = ["const"]
    CB = ["constb"]

    def plc(l, name, i=0, rows_=(0, 128)):
        c = PLC[name] + i
        return PLs[rows_[0]:rows_[1], l, c:c + 1]

    for l in range(DEPTH):
        V(ts(DPs[:, l, 0:2], PLs[:, l, PLC["bg"]:PLC["bg"] + 2], -1.0, ALU.mult), r=C, j=["dp"])
        V(tt(DPs[0:4, l, 2:3], PLs[0:4, l, PLC["b_mf"]:PLC["b_mf"] + 1], PLs[0:4, l, PLC["bmlf"]:PLC["bmlf"] + 1], ALU.add), r=C, j=["dp"])
        V(ts(DPs[0:4, l, 2:3], DPs[0:4, l, 2:3], -1.0, ALU.mult), r=["dp"], w=["dp"])
    V(tt(hgt_[:, :, 0:1], hgl[:, :, 0:1], hgl[:, :, 1:2], ALU.max), r=C, w=["lbw"])
    V(tt(hgt_[:, :, 0:1], hgt_[:, :, 0:1], hgl[:, :, 2:3], ALU.max), r=C + ["lbw"], w=["lbw"])
    V(tt(hgt_[:, :, 0:1], hgt_[:, :, 0:1], hgl[:, :, 3:4], ALU.max), r=C + ["lbw"], w=["lbw"])
    V(tt(hgl, hgl, hgt_[:, :, 0:1].to_broadcast([128, 2, 4]), ALU.subtract), r=C + ["lbw"], w=["lbe"])
    A(act(hgl, hgl, AF.Exp), r=["lbe"], w=["lbe"])
    V(tt(hgt_[:, :, 0:1], hgl[:, :, 0:1], hgl[:, :, 1:2], ALU.add), r=["lbe"], w=["lbw"])
    V(tt(hgt_[:, :, 0:1], hgt_[:, :, 0:1], hgl[:, :, 2:3], ALU.add), r=["lbe", "lbw"], w=["lbw"])
    V(tt(hgt_[:, :, 0:1], hgt_[:, :, 0:1], hgl[:, :, 3:4], ALU.add), r=["lbe", "lbw"], w=["lbw"])
    V(lambda e: e.reciprocal(out=hgt_[:, :, 0:1], in_=hgt_[:, :, 0:1]), r=["lbw"], w=["lbw"])
    V(tt(hgl, hgl, hgt_[:, :, 0:1].to_broadcast([128, 2, 4]), ALU.mult), r=["lbe", "lbw"], w=["lbe"])
    V(lambda e: e.memset(lbs[:, :, 0:1], 0.0), w=["lb"])
    V(lambda e: e.tensor_copy(out=lbs[:, :, 1:2], in_=hgl[:, :, 1:2]), r=["lbe"], j=["lb"])
    V(tt(lbs[:, :, 2:3], lbs[:, :, 1:2], hgl[:, :, 2:3], ALU.add), r=["lbe", "lb"], w=["lb"])
    V(tt(lbs[:, :, 3:4], lbs[:, :, 2:3], hgl[:, :, 3:4], ALU.add), r=["lbe", "lb"], w=["lb"])
    V(ts(omlb, lbs, -1.0, ALU.mult, 1.0, ALU.add), r=["lb"], w=["omlb"])
    LB = ["lb", "omlb"]

    stages = []
    wn = [0]

    def wload(dmas):
        i = wn[0] % 3
        wn[0] += 1
        key = "w%d" % i
        slot = wslots[i]
        first = True
        for dstf, src, okey in dmas:
            if okey is None:
                ld("pool", dstf(slot), src, key, w=[key] if first else (), j=() if first else [key])
                first = False
            else:
                ld("pool", dstf, src, okey, w=[okey])
        return slot, key

    def wv(slot, ncols, off=0, k=8):
        return slot[:, off:off + k * ncols].rearrange("p (k n) -> p k n", k=k)

    def win_cols(l, c0, n):
        return w_in[l].rearrange("(k p) n -> p k n", p=128)[:, :, c0:c0 + n]

    xkeys = lambda pre, ti: "%s_t%d" % (pre, ti)
    S1K = ["S1_%d" % u for u in range(4)] + ["mxh0", "sext0"]
    S2K = ["S2_%d" % u for u in range(4)] + ["mxh1", "sext1"]
    SAK = ["sA%d_%d" % (j, ti) for j in range(2) for ti in range(3)]
    SBK = ["sB%d_%d" % (j, ti) for j in range(2) for ti in range(3)]
    VTK = ["vT%d" % r for r in range(9)]
    KTK = ["kT%d" % r for r in range(9)]

    def proj_x(lhs_fn, M, wkey, evac):
        for ti, (c0, n) in enumerate(TTS):
            ps, pk = PS.next()
            for k in range(8):
                PE(mm(ps[0:M, 0:n], lhs_fn(k), xb[:, k, c0:c0 + n], k == 0, k == 7),
                   r=[wkey, xkeys("xb", ti)], w=[pk] if k == 0 else (), j=() if k == 0 else [pk])
            evac(ti, c0, n, ps, pk)

    def pair_branch(l, half, br):
        gla = br == "gla"
        dk = 48 if gla else 64
        dv = 96 if gla else 64
        Sst = Sg[l] if gla else Sh[l]
        dscale = (-1.0 / 16.0) if gla else 1.0
        qscale = dk ** -0.5
        bs0 = half * NSEQ
        sAv = sA.rearrange("p (j t) -> p j t", j=2)
        sBv = sB.rearrange("p (j t) -> p j t", j=2)
        sCv = bigraw[:, 0:2 * T].rearrange("p (j t) -> p j t", j=2)
        st = {}

        def rows_h(h):
            return (h % 2) * 64, (h % 2) * 64 + dk

        def stage1(slot, wkey):
            if gla:
                U = wv(slot, 528)
                def ev_ga(ti, c0, n, ps, pk):
                    A(act(gaT[0:16, c0:c0 + n], ps[0:16, 0:n], AF.Identity, bias=plc(l, "b_ga", 0, (0, 16))),
                      r=[pk] + C, w=[xkeys("gaT", ti)])
                proj_x(lambda k: U[:, k, 0:16], 16, wkey, ev_ga)
                for j in range(2):
                    for ti, (c0, n) in enumerate(TTS):
                        ps, pk = PS.next()
                        PE(mm(ps[:, 0:n], wg_s[0:16, j * 128:(j + 1) * 128], gaT[0:16, c0:c0 + n]),
                           r=["wg", xkeys("gaT", ti)], w=[pk])
                        A(act(sAv[:, j, c0:c0 + n], ps[:, 0:n], AF.Exp, bias=DPs[:, l, j:j + 1], scale=-1.0),
                          r=[pk, "dp"], w=["sA%d_%d" % (j, ti)] + S1K)
                        A(act(sAv[:, j, c0:c0 + n], sAv[:, j, c0:c0 + n], AF.Ln, bias=1.0),
                          r=["sA%d_%d" % (j, ti)], w=["sA%d_%d" % (j, ti)])
                gsrc, gk_ = sAv, "sA"
                cum, ck = sBv, "sB"
                dd, dk_ = sAv, "sA"
            else:
                U = wv(slot, 512)
                for j in range(2):
                    def ev_hf(ti, c0, n, ps, pk, j=j):
                        tf, tk = tmpF.next()
                        A(act(tf[:, 0:n], ps[:, 0:n], AF.Sigmoid, bias=plc(l, "b_hf", j)), r=[pk] + C, w=[tk])
                        V(ts(sBv[:, j, c0:c0 + n], tf[:, 0:n], omlb[:, j, l:l + 1], ALU.mult, lbs[:, j, l:l + 1], ALU.add),
                          r=[tk] + LB, w=["sB%d_%d" % (j, ti)] + S2K)
                    proj_x(lambda k, j=j: U[:, k, 256 + j * 128:256 + (j + 1) * 128], 128, wkey, ev_hf)
                allsb = ["sB%d_%d" % (j, ti) for j in range(2) for ti in range(3)]
                allsa = ["sA%d_%d" % (j, ti) for j in range(2) for ti in range(3)]
                A(act(sAv[:, :, :], sBv[:, :, :], AF.Ln), r=allsb, w=allsa + S1K)
                V(ts(sBv[:, :, :], sBv[:, :, :], -1.0, ALU.mult, 1.0, ALU.add), r=allsa, w=allsb)
                gsrc, gk_ = sAv, "sA"
                cum, ck = sCv, "BIG"
                dd, dk_ = sAv, "sA"
            allt = lambda pre, j: (["big_t0", "big_t1", "big_t2"] if pre == "BIG" else ["%s%d_%d" % (pre, j, ti) for ti in range(3)])
            for j in range(2):
                V(lambda e, j=j: e.tensor_tensor_scan(out=cum[:, j, :], data0=rmask[:, :], data1=gsrc[:, j, :], initial=0.0,
                                                      op0=ALU.mult, op1=ALU.add),
                  r=allt(gk_, j) + CB, w=allt(ck, j) + (S2K if gla else []))
                cp = cum[:, j, 0:NP].rearrange("p (c s) -> p c s", s=64)
                cs = cum[:, j, NP:T].rearrange("p (c s) -> p c s", s=4)
                dp_ = dd[:, j, 0:NP].rearrange("p (c s) -> p c s", s=64)
                ds_ = dd[:, j, NP:T].rearrange("p (c s) -> p c s", s=4)
                V(tt(dp_, cp, cp[:, :, 31:32].to_broadcast([128, 16, 64]), ALU.subtract), r=allt(ck, j), w=allt(dk_, j)[0:2])
                V(tt(ds_, cs, cs[:, :, 1:2].to_broadcast([128, 8, 4]), ALU.subtract), r=allt(ck, j), w=allt(dk_, j)[2:3])
                A(act(eA[:, j, 0:16], cp[:, :, 31], AF.Exp, scale=dscale), r=allt(ck, j), w=["eA%d" % j])
                A(act(eA[:, j, 16:24], cs[:, :, 1], AF.Exp, scale=dscale), r=allt(ck, j), j=["eA%d" % j])
                A(act(eB[:, j, 0:16], dp_[:, :, 63], AF.Exp, scale=dscale), r=allt(dk_, j), w=["eB%d" % j])
                A(act(eB[:, j, 16:24], ds_[:, :, 3], AF.Exp, scale=dscale), r=allt(dk_, j), j=["eB%d" % j])
                V(tt(eC[:, j, :], eA[:, j, :], eB[:, j, :], ALU.mult), r=["eA%d" % j, "eB%d" % j], w=["eC%d" % j])
            allck = allt(ck, 0) + [k_ for k_ in allt(ck, 1) if k_ not in allt(ck, 0)]
            alldk = allt(dk_, 0) + allt(dk_, 1)
            A(act(cum[:, :, :], dd[:, :, :], AF.Exp, bias=math.log(qscale), scale=dscale), r=alldk + ["eA0", "eA1"], w=allck)
            A(act(dd[:, :, :], dd[:, :, :], AF.Exp, scale=-dscale), r=allck + ["eB0", "eB1"], w=alldk)
            for j in range(2):
                def ev_q(ti, c0, n, ps, pk, j=j):
                    if gla:
                        V(stt(qbuf[:, j, c0:c0 + n], ps[:, 0:n], plc(l, "b_gq", j), cum[:, j, c0:c0 + n], ALU.add, ALU.mult),
                          r=[pk] + allck + C, w=["q%d_%d" % (j, ti)])
                    else:
                        sq_, sqk = tmpF.next()
                        A(act(sq_[:, 0:n], ps[:, 0:n], AF.Silu, bias=plc(l, "b_hq", j)), r=[pk] + C, w=[sqk])
                        V(tt(qbuf[:, j, c0:c0 + n], sq_[:, 0:n], cum[:, j, c0:c0 + n], ALU.mult), r=[sqk] + allck, w=["q%d_%d" % (j, ti)])
                if gla:
                    proj_x(lambda k, j=j: U[:, k, 16 + j * 128:16 + (j + 1) * 128], 128, wkey, ev_q)
                else:
                    proj_x(lambda k, j=j: U[:, k, j * 128:(j + 1) * 128], 128, wkey, ev_q)
            for j in range(2):
                if gla:
                    def ev_k(ti, c0, n, ps, pk, j=j):
                        V(stt(kbuf[:, j, c0:c0 + n], ps[:, 0:n], plc(l, "b_gk", j), dd[:, j, c0:c0 + n], ALU.add, ALU.mult),
                          r=[pk] + alldk + C, w=["k%d_%d" % (j, ti)])
                    proj_x(lambda k, j=j: U[:, k, 272 + j * 128:272 + (j + 1) * 128], 128, wkey, ev_k)
                else:
                    V(tt(kbuf[:, j, :], sBv[:, j, :], dd[:, j, :], ALU.mult),
                      r=allsb + alldk, w=["k%d_%d" % (j, ti) for ti in range(3)])
            for r, (c0, w_) in enumerate(R128):
                ps, pk = PS.next()
                psb = ps.bitcast(BF16)
                ti = min(c0 // 512, 2)
                for j in range(2):
                    PE(tr(psb[0:w_, j * 128:(j + 1) * 128], kbuf[:, j, c0:c0 + w_], identB),
                       r=["k%d_%d" % (j, ti)] + CB, w=[pk] if j == 0 else (), j=() if j == 0 else [pk])
                V(lambda e, psb=psb, r=r, w_=w_: e.tensor_copy(out=kTM[0:w_, r, 0:256], in_=psb[0:w_, 0:256]),
                  r=[pk], w=["kT%d" % r] + (["xcb1"] if r == 0 else []))

        def stage2(slot, wkey):
            if gla:
                U = wv(slot, 384)
                vcol0, nvc, bcol = 0, 384, SEG["gv"]
            else:
                U = wv(slot, 512)
                vcol0, nvc, bcol = 0, 256, SEG["hi"]
            bias_bc, bbk = tmpF.next()
            ld("sp", bias_bc[:, 0:nvc], b_in[l:l + 1, bcol:bcol + nvc].partition_broadcast(128), "bbc", w=[bbk])
            for r, (c0, w_) in enumerate(R128):
                ps, pk = PS.next()
                ti = min(c0 // 512, 2)
                for k in range(8):
                    PE(mm(ps[0:w_, 0:nvc], xb[:, k, c0:c0 + w_], U[:, k, vcol0:vcol0 + nvc], k == 0, k == 7),
                       r=[wkey, xkeys("xb", ti)], w=[pk] if k == 0 else (), j=() if k == 0 else [pk])
                V(tt(vTM[0:w_, r, 0:nvc], ps[0:w_, 0:nvc], bias_bc[0:w_, 0:nvc], ALU.add), r=[pk, bbk], w=["vT%d" % r] + (["acc"] if r == 0 else []))
            DL = int(os.environ.get("DBG_S2", "99"))
            if DL < 2:
                return
            if half == 0:
                V(lambda e: e.memset(Sst, 0.0), w=["S_" + br])
            Sb2 = [Sbf[:, 0:2 * dv].rearrange("p (j v) -> p j v", j=2), Sbf2[:, 0:2 * dv].rearrange("p (j v) -> p j v", j=2)]
            okeys = ["big_t0", "big_t1", "big_t2"]
            for j in range(2):
                V(ts(Sb2[0][:, j, :], Sst[:, j, :], eA[:, j, 0:1], ALU.mult), r=["S_" + br, "eA%d" % j],
                  w=["Sbf0"] if j == 0 else (), j=() if j == 0 else ["Sbf0"])
            for c in range(NCH):
                c0 = c * 64
                r = c // 2
                p0 = (c % 2) * 64
                ti = c // 8
                Sbv = Sb2[c % 2]
                sbk = "Sbf%d" % (c % 2)
                psa, pak = PS.next()
                for hi_, h in enumerate((0, 2, 1, 3)):
                    j = h // 2
                    a, b = rows_h(h)
                    PE(mm(psa[p0:p0 + 64, h * 64:(h + 1) * 64], kbuf[a:b, j, c0:c0 + 64], qbuf[a:b, j, c0:c0 + 64]),
                       r=["k%d_%d" % (j, ti), "q%d_%d" % (j, ti)], w=[pak] if hi_ == 0 else (), j=() if hi_ == 0 else [pak], base=a)
                pss, psk = PS.next()
                for h in range(4):
                    j = h // 2
                    i = h % 2
                    PE(mm(pss[i * 64:i * 64 + dk, j * dv:(j + 1) * dv], kTM[p0:p0 + 64, r, j * 128 + i * 64:j * 128 + i * 64 + dk],
                          vTM[p0:p0 + 64, r, h * dv:(h + 1) * dv]),
                       r=["kT%d" % r, "vT%d" % r], w=[psk] if h == 0 else (), j=() if h == 0 else [psk], base=p0)
                ab, abk = attb.next()
                V(tt(ab[p0:p0 + 64, :], psa[p0:p0 + 64, 0:256], maskP[p0:p0 + 64, :], ALU.mult), r=[pak] + CB, w=[abk])
                pso, pok = PS.next()
                inter, intra = {}, {}
                for h in range(4):
                    j = h // 2
                    a, b = rows_h(h)
                    if gla:
                        oo = pso[0:96, h * 64:(h + 1) * 64]
                        qs = "q012"
                    else:
                        oo = pso[(h % 2) * 64:(h % 2) * 64 + 64, j * 64:(j + 1) * 64]
                        qs = "lo" if h % 2 == 0 else "hi"
                    inter[h] = (oo, Sbv[a:b, j, :], qbuf[a:b, j, c0:c0 + 64], a, qs, [sbk, "q%d_%d" % (j, ti)])
                    intra[h] = (oo, vTM[p0:p0 + 64, r, h * dv:(h + 1) * dv], ab[p0:p0 + 64, h * 64:(h + 1) * 64], p0, qs, ["vT%d" % r, abk])
                if p0 == 0:
                    order = [inter[0] + (False,), intra[0] + (True,), inter[2] + (False,), intra[2] + (True,),
                             intra[1] + (False,), intra[3] + (False,), inter[1] + (True,), inter[3] + (True,)]
                else:
                    order = [inter[0] + (False,), inter[2] + (False,), intra[0] + (True,), intra[2] + (True,),
                             inter[1] + (False,), intra[1] + (True,), inter[3] + (False,), intra[3] + (True,)]
                emit_bank(order, pok)
                if gla:
                    A(act(obuf[0:96, :, c0:c0 + 64], pso[0:96, 0:256].rearrange("p (h t) -> p h t", h=4), AF.Copy),
                      r=[pok], j=[okeys[ti]])
                else:
                    A(act(obuf[:, 0:2, c0:c0 + 64], pso[:, 0:128].rearrange("p (h t) -> p h t", h=2), AF.Copy),
                      r=[pok], j=[okeys[ti]])
                for j in range(2):
                    tf, tk = tmpF.next()
                    V(ts(tf[:, 0:dv], pss[:, j * dv:(j + 1) * dv], eB[:, j, c:c + 1], ALU.mult), r=[psk, "eB%d" % j], w=[tk])
                    V(stt(Sst[:, j, :], Sst[:, j, :], eC[:, j, c:c + 1], tf[:, 0:dv], ALU.mult, ALU.add),
                      r=[tk, "eC%d" % j], w=["S_" + br] if j == 0 else (), j=() if j == 0 else ["S_" + br])
                if c + 1 < NCH:
                    nk = "Sbf%d" % ((c + 1) % 2)
                    for j in range(2):
                        V(ts(Sb2[(c + 1) % 2][:, j, :], Sst[:, j, :], eA[:, j, c + 1:c + 2], ALU.mult), r=["S_" + br, "eA%d" % j],
                          w=[nk] if j == 0 else (), j=() if j == 0 else [nk])
            if DL < 3:
                return
            c0 = NP
            psa, pak = PS.next()
            for h in range(4):
                j = h // 2
                a, b = rows_h(h)
                PE(mm(psa[0:32, h * 32:(h + 1) * 32], kbuf[a:b, j, c0:c0 + 32], qbuf[a:b, j, c0:c0 + 32]),
                   r=["k%d_2" % j, "q%d_2" % j], w=[pak] if h == 0 else (), j=() if h == 0 else [pak], base=a)
            ab, abk = attb.next()
            V(tt(ab[0:32, 0:128], psa[0:32, 0:128], maskS[:, :], ALU.mult), r=[pak] + CB, w=[abk])
            src_d = d_sgla if gla else d_shg
            dst_d = o_gla_s if gla else o_hg_s

            def sload(h):
                a_, b__ = rows_h(h)
                ld("sp", sstR[h % 2][a_:b__, :, 0:dv], src_d[l, bs0:bs0 + NSEQ, h].rearrange("b k v -> k b v"), "sst%d" % (h % 2), w=["sst%d" % (h % 2)])
            sload(0)
            for h in range(4):
                j = h // 2
                i = h % 2
                a, b = rows_h(h)
                if h + 1 < 4:
                    sload(h + 1)
                sst = sstR[h % 2]
                ssk = "sst%d" % (h % 2)
                pso, pok = PS.next()
                if gla:
                    oo = pso[0:96, h * 32:(h + 1) * 32]
                else:
                    oo = pso[i * 64:i * 64 + 64, j * 32:(j + 1) * 32]
                if DL < 4:
                    return
                V(tt(sstb[a:b, :, 0:dv], sst[a:b, :, 0:dv], eA[a:b, j, 16:24].unsqueeze(2).to_broadcast([dk, 8, dv]), ALU.mult),
                  r=[ssk, "eA%d" % j], w=["sstb"])
                PE(mm(oo, vTM[0:32, 8, h * dv:(h + 1) * dv], ab[0:32, h * 32:(h + 1) * 32], True, False),
                   r=["vT8", abk], w=[pok], base=0)
                for b_ in range(NSEQ):
                    if gla:
                        ob = pso[0:96, h * 32 + 4 * b_:h * 32 + 4 * b_ + 4]
                    else:
                        ob = pso[i * 64:i * 64 + 64, j * 32 + 4 * b_:j * 32 + 4 * b_ + 4]
                    PE(mm(ob, sstb[a:b, b_, 0:dv], qbuf[a:b, j, c0 + 4 * b_:c0 + 4 * b_ + 4], False, b_ == NSEQ - 1),
                       r=["sstb", "q%d_2" % j], j=[pok], base=a)
                if gla:
                    A(act(obuf[0:96, h, c0:c0 + 32], oo, AF.Copy), r=[pok], j=["big_t2"])
                else:
                    A(act(obuf[i * 64:i * 64 + 64, j, c0:c0 + 32], oo, AF.Copy), r=[pok], j=["big_t2"])
                if DL < 6:
                    return
                V(tt(vblk[:, :, 0:dv], vTM[0:32, 8, h * dv:(h + 1) * dv].unsqueeze(1).to_broadcast([32, 8, dv]),
                     blkS[:, :].unsqueeze(2).to_broadcast([32, 8, dv]), ALU.mult), r=["vT8"] + C, w=["vblk"])
                for g in range(2):
                    psu, puk = PS.next()
                    PE(mm(psu[a:b, 0:4 * dv], kTM[0:32, 8, j * 128 + i * 64:j * 128 + i * 64 + dk],
                          vblk[:, 4 * g:4 * g + 4, 0:dv]), r=["kT8", "vblk"], w=[puk])
                    tf, tk = tmpF.next()
                    tfv = tf[a:b, 0:4 * dv].rearrange("p (b v) -> p b v", b=4)
                    V(tt(tfv, psu[a:b, 0:4 * dv].rearrange("p (b v) -> p b v", b=4),
                         eB[a:b, j, 16 + 4 * g:20 + 4 * g].unsqueeze(2).to_broadcast([dk, 4, dv]), ALU.mult),
                      r=[puk, "eB%d" % j], w=[tk])
                    V(tt(sst[a:b, 4 * g:4 * g + 4, 0:dv], sst[a:b, 4 * g:4 * g + 4, 0:dv],
                         eC[a:b, j, 16 + 4 * g:20 + 4 * g].unsqueeze(2).to_broadcast([dk, 4, dv]), ALU.mult),
                      r=["eC%d" % j, "sstb"], w=[ssk])
                    V(tt(sst[a:b, 4 * g:4 * g + 4, 0:dv], sst[a:b, 4 * g:4 * g + 4, 0:dv], tfv, ALU.add), r=[tk], w=[ssk])
                ld("sp", dst_d[l, bs0:bs0 + NSEQ, h].rearrange("b k v -> k b v"), sst[a:b, :, 0:dv], ssk, r=[ssk])
            if gla:
                dump("og", obuf[0:96, :, NP:T], ["big_t2"], [96, 4, 32])
                dump("ogp", obuf[0:96, :, 0:128], ["big_t0"], [96, 4, 128])
            if half == 1:
                if gla:
                    for h in range(4):
                        a, b = rows_h(h)
                        ld("sp", o_gla_p[l, 0, h], Sst[a:b, h // 2, :], "S_out_" + br, r=["S_" + br])
                else:
                    for j in range(2):
                        ld("sp", o_hg_p[l, 0, 2 * j:2 * j + 2].rearrange("i k v -> (i k) v"), Sst[:, j, :], "S_out_" + br, r=["S_" + br])

        def stage3(slot, wkey):
            if gla:
                U = wv(slot, 384)
            else:
                U = wv(slot, 512)
            nu = 4 if gla else 2
            pr = 96 if gla else 128
            S1 = sA[0:pr, 0:nu * 512].rearrange("p (u n) -> p u n", u=nu)
            S2 = sB[0:pr, 0:nu * 512].rearrange("p (u n) -> p u n", u=nu)
            for ti, (c0, n) in enumerate(TTS):
                bk = "big_t%d" % ti
                osl = obuf[0:pr, 0:nu, c0:c0 + n]
                A(act(S1[:, :, 0:n], osl, AF.Square), r=[bk], w=S1K + SAK)
                pss_ = []
                for u in range(nu):
                    ps, pk = PS.next()
                    PE(mm(ps[0:pr, 0:n], ones96[0:96, 0:96] if gla else ones64b[:, :], S1[:, u, 0:n]), r=["S1_%d" % u] + C, w=[pk])
                    pss_.append((ps, pk))
                for u in range(nu):
                    ps, pk = pss_[u]
                    A(act(S1[:, u, 0:n], ps[0:pr, 0:n], AF.Ln, bias=NORM_EPS), r=[pk], w=["S1_%d" % u])
                A(act(S1[:, :, 0:n], S1[:, :, 0:n], AF.Exp, scale=-0.5), r=S1K, w=S1K)
                V(tt(osl, osl, S1[:, :, 0:n], ALU.mult), r=S1K, w=[bk])
                pss_ = []
                for u in range(nu):
                    ps2, pk2 = PS.next()
                    for k in range(8):
                        lhs = U[:, k, u * 96:(u + 1) * 96] if gla else U[:, k, 256 + u * 128:256 + (u + 1) * 128]
                        PE(mm(ps2[0:pr, 0:n], lhs, xb[:, k, c0:c0 + n], k == 0, k == 7), r=[wkey, xkeys("xb", ti)],
                           w=[pk2] if k == 0 else (), j=() if k == 0 else [pk2])
                    pss_.append((ps2, pk2))
                for u in range(nu):
                    ps2, pk2 = pss_[u]
                    A(act(S2[:, u, 0:n], ps2[0:pr, 0:n], AF.Silu, bias=plc(l, "b_gr" if gla else "b_hgt", u, (0, pr))),
                      r=[pk2] + C, w=["S2_%d" % u] + (SBK if u == 0 else []))
                hdst = hG[0:96, :, c0:c0 + n] if gla else hH[:, :, c0:c0 + n]
                V(stt(hdst, osl, plc(l, "gng" if gla else "hng", 0, (0, pr)), S2[:, :, 0:n], ALU.mult, ALU.mult),
                  r=[bk] + S2K[0:nu] + C, w=["h%s_%d_%d" % (br, u, ti) for u in range(nu)])

        if gla:
            gqv = lambda s: wv(s, 528)
            d1 = [(lambda s: gqv(s)[:, :, 0:16], win_cols(l, SEG["ga"], 16), None)]
            for h in range(4):
                d1.append((lambda s, h=h: gqv(s)[:, :, 16 + (h // 2) * 128 + (h % 2) * 64:16 + (h // 2) * 128 + (h % 2) * 64 + 48],
                           win_cols(l, SEG["gq"] + 48 * h, 48), None))
                d1.append((lambda s, h=h: gqv(s)[:, :, 272 + (h // 2) * 128 + (h % 2) * 64:272 + (h // 2) * 128 + (h % 2) * 64 + 48],
                           win_cols(l, SEG["gk"] + 48 * h, 48), None))
            for h in range(4):
                d1.append((wg_s[0:16, (h // 2) * 128 + (h % 2) * 64:(h // 2) * 128 + (h % 2) * 64 + 48], w_gg[l, :, 48 * h:48 * h + 48], "wg"))
            stages.append((d1, stage1))
            stages.append(([(lambda s: wv(s, 384), win_cols(l, SEG["gv"], 384), None)], stage2))
            stages.append(([(lambda s: wv(s, 384), win_cols(l, SEG["gr"], 384), None)], stage3))
        else:
            stages.append(([(lambda s: wv(s, 512), win_cols(l, SEG["hq"], 512), None)], stage1))
            stages.append(([(lambda s: wv(s, 512), win_cols(l, SEG["hi"], 512), None)], stage2))
            stages.append((None, lambda slot, wkey: stage3(st["s2slot"], st["s2key"])))
            old2 = stages[-2][1]

            def s2wrap(slot, wkey, old2=old2):
                st["s2slot"], st["s2key"] = slot, wkey
                old2(slot, wkey)
            stages[-2] = (stages[-2][0], s2wrap)

    def ml_branch(l, half):
        bs0 = half * NSEQ
        R0, R1, R2, R3 = rows[:, 0, :], rows[:, 1, :], rows[:, 2, :], rows[:, 3, :]
        RK = ["big_t0", "big_t1", "big_t2"]
        sextv = sext

        def stageA(slot, wkey):
            U = wv(slot, 392)
            ld("sp", m0s[0:4, :], d_smm[l, bs0:bs0 + NSEQ, :].rearrange("b h -> h b"), "m0s", w=["m0s"], nonc=True)
            for h_ in range(4):
                ld("sp", nin[0:96, h_, :], d_smn[l, bs0:bs0 + NSEQ, h_].rearrange("b d -> d b"), "nin", w=["nin"] if h_ == 0 else (),
                   j=() if h_ == 0 else ["nin"], nonc=True)
            ld("sp", cst[0:24, :], d_smconv[l, bs0:bs0 + NSEQ].rearrange("b j c -> (b j) c"), "cst", w=["cst"])
            mxhs = [sA, sB]
            sexts = [sA[:, 1064:1120].rearrange("p (b v) -> p b v", v=7), sB[:, 1064:1120].rearrange("p (b v) -> p b v", v=7)]
            accf = vTM.rearrange("p r c -> p (r c)").bitcast(F32)
            accP = accf[0:96, 0:NP]
            accS = accf[0:96, NP:T].rearrange("p (b s) -> p b s", s=4)
            xcbs = [xcb, kTM.rearrange("p r c -> p (r c)")[:, 0:T]]
            def do_mx(h):
                mxh = mxhs[h % 2]
                sxv = sexts[h % 2]
                mk, sk = "mxh%d" % (h % 2), "sext%d" % (h % 2)
                mfence = (S1K + SAK) if h % 2 == 0 else (S2K + SBK)

                def ev_mx(ti, c0, n, ps, pk, h=h, mxh=mxh, sxv=sxv, mk=mk, sk=sk, mfence=mfence):
                    if ti < 2:
                        A(act(mxh[0:96, 3 + c0:3 + c0 + n], ps[0:96, 0:n], AF.Identity, bias=plc(l, "b_mx", h, (0, 96))),
                          r=[pk] + C, w=([mk] + mfence) if ti == 0 else (), j=() if ti == 0 else [mk])
                    else:
                        A(act(sxv[0:96, :, 3:7], ps[0:96, 0:n].rearrange("p (b s) -> p b s", s=4), AF.Identity,
                              bias=plc(l, "b_mx", h, (0, 96))), r=[pk] + C, w=[sk])
                proj_x(lambda k, h=h: U[:, k, h * 96:(h + 1) * 96], 96, wkey, ev_mx)

            def ev_i(ti, c0, n, ps, pk):
                A(act(R0[:, c0:c0 + n], ps[0:4, 0:n], AF.Identity, bias=plc(l, "b_mi", 0, (0, 4))), r=[pk] + C,
                  w=RK if ti == 0 else (), j=() if ti == 0 else RK)
            proj_x(lambda k: U[:, k, 384:388], 4, wkey, ev_i)

            def ev_f(ti, c0, n, ps, pk):
                A(act(R1[:, c0:c0 + n], ps[0:4, 0:n], AF.Exp, bias=DPs[0:4, l, 2:3], scale=-1.0), r=[pk, "dp"], j=RK)
            proj_x(lambda k: U[:, k, 388:392], 4, wkey, ev_f)
            A(act(R1, R1, AF.Ln, bias=1.0), r=RK, w=RK)
            do_mx(0)
            do_mx(1)
            V(lambda e: e.tensor_tensor_scan(out=R2, data0=rmask[0:4, :], data1=R1, initial=0.0, op0=ALU.mult, op1=ALU.add),
              r=RK + CB, w=RK)
            V(ts(R1, R1, -1.0, ALU.mult), r=RK, w=RK)
            if half == 0:
                V(lambda e: e.memset(mi0, 0.0), w=["mi0"])
            else:
                V(lambda e: e.tensor_copy(out=mi0, in_=msave[l]), r=["msave%d" % l], w=["mi0"])
            V(lambda e: e.tensor_tensor_scan(out=R3[:, 0:NP], data0=R1[:, 0:NP], data1=R0[:, 0:NP], initial=mi0[0:4, 0:1],
                                             op0=ALU.add, op1=ALU.max), r=RK + ["mi0"], w=RK)
            for b_ in range(NSEQ):
                s0 = NP + 4 * b_
                V(lambda e, s0=s0, b_=b_: e.tensor_tensor_scan(out=R3[:, s0:s0 + 4], data0=R1[:, s0:s0 + 4], data1=R0[:, s0:s0 + 4],
                                                                initial=m0s[0:4, b_:b_ + 1], op0=ALU.add, op1=ALU.max),
                  r=RK + ["m0s"], w=RK)
            V(lambda e: e.tensor_copy(out=msave[l], in_=R3[:, NP - 1:NP]), r=RK, w=["msave%d" % l])
            R3s = R3[:, NP:T].rearrange("p (b s) -> p b s", s=4)
            V(lambda e: e.tensor_copy(out=msout, in_=R3s[:, :, 3]), r=RK, w=["msout"])
            ld("sp", o_mm_s[l, bs0:bs0 + NSEQ, :].rearrange("b h -> h b"), msout[0:4, :], "msout", r=["msout"], nonc=True)
            if half == 1:
                ld("sp", o_mm_p[l, 0:1, :].rearrange("b h -> h b"), msave[l][0:4, 0:1], "msave", r=["msave%d" % l], nonc=True)
            V(tt(R1, R3, R2, ALU.add), r=RK, w=RK)
            V(tt(R0, R0, R2, ALU.add), r=RK, w=RK)
            R1p = R1[:, 0:NP].rearrange("p (c s) -> p c s", s=64)
            R1s = R1[:, NP:T].rearrange("p (b s) -> p b s", s=4)
            R3p = R3[:, 0:NP].rearrange("p (c s) -> p c s", s=64)
            V(tt(crow[:, 1:16], R3p[:, 0:15, 63], R1p[:, 1:16, 63], ALU.subtract), r=RK, w=["crow"])
            V(tt(crow[:, 0:1], mi0[0:4, 0:1], R1[:, 63:64], ALU.subtract), r=RK + ["mi0"], j=["crow"])
            V(tt(crow[:, 16:24], m0s[0:4, :], R1s[:, :, 3], ALU.subtract), r=RK + ["m0s"], j=["crow"])
            A(act(crow2, crow, AF.Exp), r=["crow"], w=["crow2"])
            ps, pk = PS.next()
            for h in range(4):
                PE(mm(ps[0:96, h * 24:(h + 1) * 24], sel4[0:4, h, 0:96], crow2[0:4, :]), r=["crow2"] + C,
                   w=[pk] if h == 0 else (), j=() if h == 0 else [pk])
            A(act(carry[0:96, :, :], ps[0:96, 0:96].rearrange("p (h c) -> p h c", h=4), AF.Copy), r=[pk], w=["carry"])
            R2p = R2[:, 0:NP].rearrange("p (c s) -> p c s", s=64)
            R2s = R2[:, NP:T].rearrange("p (b s) -> p b s", s=4)
            R0p = R0[:, 0:NP].rearrange("p (c s) -> p c s", s=64)
            R0s = R0[:, NP:T].rearrange("p (b s) -> p b s", s=4)
            V(tt(R2p, R1p[:, :, 63:64].to_broadcast([4, 16, 64]), R1p, ALU.subtract), r=RK, w=RK)
            V(tt(R2s, R1s[:, :, 3:4].to_broadcast([4, 8, 4]), R1s, ALU.subtract), r=RK, w=RK)
            V(tt(R0p, R0p, R1p[:, :, 63:64].to_broadcast([4, 16, 64]), ALU.subtract), r=RK, w=RK)
            V(tt(R0s, R0s, R1s[:, :, 3:4].to_broadcast([4, 8, 4]), ALU.subtract), r=RK, w=RK)
            A(act(R2, R2, AF.Exp), r=RK, w=RK)
            A(act(R0, R0, AF.Exp), r=RK, w=RK)
            bias_bc, bbk = tmpF.next()
            ld("sp", bias_bc[:, 0:384], b_in[l:l + 1, SEG["mx"]:SEG["mx"] + 384].partition_broadcast(128), "bbc", w=[bbk])
            ps, pk = PS.next()
            for k in range(8):
                PE(mm(ps[0:32, 0:384], xb[:, k, NP:T], U[:, k, 0:384], k == 0, k == 7), r=[wkey, "xb_t2"],
                   w=[pk] if k == 0 else (), j=() if k == 0 else [pk])
            cTM, ctk = tmpF.next()
            V(tt(cTM[0:32, 0:384], ps[0:32, 0:384], bias_bc[0:32, 0:384], ALU.add), r=[pk, bbk], w=[ctk])
            for b_ in range(NSEQ):
                ld("sp", o_mconv_s[l, bs0 + b_], cTM[4 * b_ + 1:4 * b_ + 4, 0:384], ctk, r=[ctk])
            if half == 1:
                ps, pk = PS.next()
                for k in range(8):
                    PE(mm(ps[0:3, 0:384], xb[:, k, NP - 3:NP], U[:, k, 0:384], k == 0, k == 7), r=[wkey, "xb_t1"],
                       w=[pk] if k == 0 else (), j=() if k == 0 else [pk])
                cTM, ctk = tmpF.next()
                V(tt(cTM[0:3, 0:384], ps[0:3, 0:384], bias_bc[0:3, 0:384], ALU.add), r=[pk, bbk], w=[ctk])
                ld("sp", o_mconv_p[l, 0], cTM[0:3, 0:384], ctk, r=[ctk])
            def do_rest(h):
                mxh = mxhs[h % 2]
                sxv = sexts[h % 2]
                xc = xcbs[h % 2]
                mk, sk, xk = "mxh%d" % (h % 2), "sext%d" % (h % 2), "xcb%d" % (h % 2)
                if half == 0:
                    V(lambda e, mxh=mxh: e.memset(mxh[0:96, 0:3], 0.0), j=[mk])
                else:
                    V(lambda e, h=h, mxh=mxh: e.tensor_copy(out=mxh[0:96, 0:3], in_=ctail[l][0:96, h, :]), r=["ctail%d" % l], j=[mk])
                ps, pk = PS.next()
                PE(tr(ps[0:96, 0:24], cst[0:24, h * 96:(h + 1) * 96], identF[0:24, 0:24]), r=["cst"] + C, w=[pk])
                A(act(sxv[0:96, :, 0:3], ps[0:96, 0:24].rearrange("p (b j) -> p b j", j=3), AF.Copy), r=[pk], j=[sk])
                cwc = lambda j_, h=h: PLs[0:96, l, PLC["cw"] + 4 * h + j_:PLC["cw"] + 4 * h + j_ + 1]
                cbc = plc(l, "cb", h, (0, 96))
                V(ts(accP, mxh[0:96, 3:3 + NP], cwc(3), ALU.mult, cbc, ALU.add), r=[mk] + C, w=["acc"] + VTK)
                V(ts(accS, sxv[0:96, :, 3:7], cwc(3), ALU.mult, cbc, ALU.add), r=[sk] + C, j=["acc"])
                for j_ in range(3):
                    V(stt(accP, mxh[0:96, j_:j_ + NP], cwc(j_), accP, ALU.mult, ALU.add), r=[mk, "acc"] + C, w=["acc"])
                    V(stt(accS, sxv[0:96, :, j_:j_ + 4], cwc(j_), accS, ALU.mult, ALU.add), r=[sk, "acc"] + C, w=["acc"])
                if half == 0:
                    V(lambda e, h=h, mxh=mxh: e.tensor_copy(out=ctail[l][0:96, h, :], in_=mxh[0:96, NP:NP + 3]), r=[mk],
                      w=["ctail%d" % l] if h == 0 else (), j=() if h == 0 else ["ctail%d" % l])
                A(act(xc[0:96, :], accf[0:96, 0:T], AF.Silu), r=["acc"], w=[xk] + (KTK if h % 2 == 1 else []))

            def do_qk(h):
                xc = xcbs[h % 2]
                xk = "xcb%d" % (h % 2)
                for ti, (c0, n) in enumerate(TTS):
                    ps1, pk1 = PS.next()
                    PE(mm(ps1[0:96, 0:n], sel4[0:4, h, 0:96], R2[0:4, c0:c0 + n]), r=RK + C, w=[pk1])
                    f1, f1k = tmpF.next()
                    A(act(f1[0:96, 0:n], ps1[0:96, 0:n], AF.Copy), r=[pk1], w=[f1k])
                    ps2, pk2 = PS.next()
                    PE(mm(ps2[0:96, 0:n], sel4[0:4, h, 0:96], R0[0:4, c0:c0 + n]), r=RK + C, w=[pk2])
                    f2, f2k = tmpF.next()
                    A(act(f2[0:96, 0:n], ps2[0:96, 0:n], AF.Copy), r=[pk2], w=[f2k])
                    psq, pqk = PS.next()
                    PE(mm(psq[0:96, 0:n], wq_s[0:96, h, :], xc[0:96, c0:c0 + n]), r=["wq", xk], w=[pqk])
                    V(tt(qbuf[0:96, h, c0:c0 + n], psq[0:96, 0:n], f1[0:96, 0:n], ALU.mult), r=[pqk, f1k], w=["q%d_%d" % (h, ti)])
                    psk_, pkk = PS.next()
                    PE(mm(psk_[0:96, 0:n], wk_s[0:96, h, :], xc[0:96, c0:c0 + n]), r=["wk", xk], w=[pkk])
                    V(stt(kbuf[0:96, h, c0:c0 + n], psk_[0:96, 0:n], 96.0 ** -0.5, f2[0:96, 0:n], ALU.mult, ALU.mult),
                      r=[pkk, f2k], w=["k%d_%d" % (h, ti)])

            do_rest(0)
            for h in range(4):
                if h + 2 < 4:
                    do_mx(h + 2)
                if h + 1 < 4:
                    do_rest(h + 1)
                do_qk(h)
            for r, (c0, w_) in enumerate(R128):
                ps, pk = PS.next()
                psb = ps.bitcast(BF16)
                ti = min(c0 // 512, 2)
                for h in range(4):
                    PE(tr(psb[0:w_, h * 96:(h + 1) * 96], kbuf[0:96, h, c0:c0 + w_], identB[0:96, 0:96]),
                       r=["k%d_%d" % (h, ti)] + CB, w=[pk] if h == 0 else (), j=() if h == 0 else [pk])
                V(lambda e, psb=psb, r=r, w_=w_: e.tensor_copy(out=kTM[0:w_, r, 0:384], in_=psb[0:w_, 0:384]), r=[pk],
                  w=["kT%d" % r] + (["xcb1"] if r == 0 else []))

        def stageB(slot, wkey):
            U = wv(slot, 384)
            bias_bc, bbk = tmpF.next()
            ld("sp", bias_bc[:, 0:384], b_in[l:l + 1, SEG["mv"]:SEG["mv"] + 384].partition_broadcast(128), "bbc", w=[bbk])
            vT4 = vTM.rearrange("p r (h v) -> p r h v", h=4)
            for r, (c0, w_) in enumerate(R128):
                ps, pk = PS.next()
                ti = min(c0 // 512, 2)
                for k in range(8):
                    PE(mm(ps[0:w_, 0:384], xb[:, k, c0:c0 + w_], U[:, k, 0:384], k == 0, k == 7),
                       r=[wkey, xkeys("xb", ti)], w=[pk] if k == 0 else (), j=() if k == 0 else [pk])
                V(tt(vT4[0:w_, r, :, 0:96], ps[0:w_, 0:384].rearrange("p (h v) -> p h v", h=4),
                     bias_bc[0:w_, 0:384].rearrange("p (h v) -> p h v", h=4), ALU.add), r=[pk, bbk], w=["vT%d" % r] + (["acc"] if r == 0 else []))
                V(lambda e, r=r, w_=w_: e.memset(vT4[0:w_, r, :, 96:97], 1.0), j=["vT%d" % r])
            Cst = Cm[l]
            if half == 0:
                V(lambda e: e.memset(Cst, 0.0), w=["S_ml"])
            Cb2 = [Sbf[:, 0:388].rearrange("p (h v) -> p h v", h=4), Sbf2[:, 0:388].rearrange("p (h v) -> p h v", h=4)]
            okeys = ["big_t0", "big_t1", "big_t2"]
            V(tt(Cb2[0][0:96, :, :], Cst[0:96, :, :], carry[0:96, :, 0:1].to_broadcast([96, 4, 97]), ALU.mult), r=["S_ml", "carry"], w=["Sbf0"])
            for c in range(NCH):
                c0 = c * 64
                r = c // 2
                p0 = (c % 2) * 64
                ti = c // 8
                Cbv = Cb2[c % 2]
                sbk = "Sbf%d" % (c % 2)
                psa, pak = PS.next()
                for h in range(4):
                    PE(mm(psa[p0:p0 + 64, h * 64:(h + 1) * 64], kbuf[0:96, h, c0:c0 + 64], qbuf[0:96, h, c0:c0 + 64]),
                       r=["k%d_%d" % (h, ti), "q%d_%d" % (h, ti)], w=[pak] if h == 0 else (), j=() if h == 0 else [pak])
                pss, psk = PS.next()
                for h in range(4):
                    PE(mm(pss[0:96, h * 97:(h + 1) * 97], kTM[p0:p0 + 64, r, h * 96:(h + 1) * 96], vTM[p0:p0 + 64, r, h * 97:(h + 1) * 97]),
                       r=["kT%d" % r, "vT%d" % r], w=[psk] if h == 0 else (), j=() if h == 0 else [psk])
                ab, abk = attb.next()
                V(tt(ab[p0:p0 + 64, :], psa[p0:p0 + 64, 0:256], maskP[p0:p0 + 64, :], ALU.mult), r=[pak] + CB, w=[abk])
                pso, pok = PS.next()
                order = []
                for h in range(4):
                    order.append((pso[0:97, h * 64:(h + 1) * 64], Cbv[0:96, h, :], qbuf[0:96, h, c0:c0 + 64], 0, "all",
                                  [sbk, "q%d_%d" % (h, ti)], False))
                for h in range(4):
                    order.append((pso[0:97, h * 64:(h + 1) * 64], vTM[p0:p0 + 64, r, h * 97:(h + 1) * 97], ab[p0:p0 + 64, h * 64:(h + 1) * 64],
                                  p0, "all", ["vT%d" % r, abk], True))
                emit_bank(order, pok)
                A(act(obuf[0:97, :, c0:c0 + 64], pso[0:97, 0:256].rearrange("p (h t) -> p h t", h=4), AF.Copy),
                  r=[pok], w=okeys if c == 0 else (), j=() if c == 0 else [okeys[ti]])
                V(tt(Cst[0:96, :, :], Cst[0:96, :, :], carry[0:96, :, c:c + 1].to_broadcast([96, 4, 97]), ALU.mult),
                  r=["carry"], w=["S_ml"])
                V(tt(Cst[0:96, :, :], Cst[0:96, :, :], pss[0:96, 0:388].rearrange("p (h v) -> p h v", h=4), ALU.add),
                  r=[psk], w=["S_ml"])
                if c + 1 < NCH:
                    V(tt(Cb2[(c + 1) % 2][0:96, :, :], Cst[0:96, :, :], carry[0:96, :, c + 1:c + 2].to_broadcast([96, 4, 97]), ALU.mult),
                      r=["S_ml", "carry"], w=["Sbf%d" % ((c + 1) % 2)])
            c0 = NP
            psa, pak = PS.next()
            for h in range(4):
                PE(mm(psa[0:32, h * 32:(h + 1) * 32], kbuf[0:96, h, c0:c0 + 32], qbuf[0:96, h, c0:c0 + 32]),
                   r=["k%d_2" % h, "q%d_2" % h], w=[pak] if h == 0 else (), j=() if h == 0 else [pak])
            ab, abk = attb.next()
            V(tt(ab[0:32, 0:128], psa[0:32, 0:128], maskS[:, :], ALU.mult), r=[pak] + CB, w=[abk])
            def mload(h):
                ld("sp", sstR[h % 2][0:96, :, 0:96], d_smC[l, bs0:bs0 + NSEQ, h].rearrange("b d e -> d b e"), "sst%d" % (h % 2), w=["sst%d" % (h % 2)])
            mload(0)
            for h in range(4):
                if h + 1 < 4:
                    mload(h + 1)
                sst = sstR[h % 2]
                ssk = "sst%d" % (h % 2)
                pso, pok = PS.next()
                oo = pso[0:97, h * 32:(h + 1) * 32]
                V(lambda e, sst=sst, h=h: e.tensor_copy(out=sst[0:96, :, 96], in_=nin[0:96, h, :]), r=["nin"], j=[ssk])
                V(tt(sst[0:96, :, :], sst[0:96, :, :], carry[0:96, h, 16:24].unsqueeze(2).to_broadcast([96, 8, 97]), ALU.mult),
                  r=["carry"], w=[ssk])
                V(lambda e, sst=sst: e.tensor_copy(out=sstb[0:96, :, :], in_=sst[0:96, :, :]), r=[ssk], w=["sstb"])
                PE(mm(oo, vTM[0:32, 8, h * 97:(h + 1) * 97], ab[0:32, h * 32:(h + 1) * 32], True, False),
                   r=["vT8", abk], w=[pok])
                for b_ in range(NSEQ):
                    PE(mm(pso[0:97, h * 32 + 4 * b_:h * 32 + 4 * b_ + 4], sstb[0:96, b_, :], qbuf[0:96, h, c0 + 4 * b_:c0 + 4 * b_ + 4],
                          False, b_ == NSEQ - 1), r=["sstb", "q%d_2" % h], j=[pok])
                A(act(obuf[0:97, h, c0:c0 + 32], oo, AF.Copy), r=[pok], j=["big_t2"])
                V(tt(vblk[:, :, :], vTM[0:32, 8, h * 97:(h + 1) * 97].unsqueeze(1).to_broadcast([32, 8, 97]),
                     blkS[:, :].unsqueeze(2).to_broadcast([32, 8, 97]), ALU.mult), r=["vT8"] + C, w=["vblk"])
                for g in range(2):
                    psu, puk = PS.next()
                    PE(mm(psu[0:96, 0:388], kTM[0:32, 8, h * 96:(h + 1) * 96], vblk[:, 4 * g:4 * g + 4, :]), r=["kT8", "vblk"], w=[puk])
                    V(tt(sst[0:96, 4 * g:4 * g + 4, :], sst[0:96, 4 * g:4 * g + 4, :],
                         psu[0:96, 0:388].rearrange("p (b v) -> p b v", b=4), ALU.add), r=[puk, "sstb"], w=[ssk])
                ld("sp", o_mC_s[l, bs0:bs0 + NSEQ, h].rearrange("b d e -> d b e"), sst[0:96, :, 0:96], ssk, r=[ssk])
                V(lambda e, sst=sst, h=h: e.tensor_copy(out=nout[0:96, h, :], in_=sst[0:96, :, 96]), r=[ssk],
                  w=["nout"] if h == 0 else (), j=() if h == 0 else ["nout"])
            for h_ in range(4):
                ld("sp", o_mn_s[l, bs0:bs0 + NSEQ, h_].rearrange("b d -> d b"), nout[0:96, h_, :], "nout", r=["nout"], nonc=True)
            if half == 1:
                ld("sp", o_mC_p[l, 0].rearrange("h d e -> d h e"), Cst[0:96, :, 0:96], "S_out_ml", r=["S_ml"])
                ld("sp", o_mn_p[l, 0].rearrange("h d -> d h"), Cst[0:96, :, 96], "S_out_ml", r=["S_ml"], nonc=True)

        def stageC(slot, wkey):
            U = wv(slot, 384)
            t2buf, t2key = tmpF.next()
            S1s = [sA[0:97, 0:4 * 512].rearrange("p (u n) -> p u n", u=4), sB[0:97, 0:4 * 512].rearrange("p (u n) -> p u n", u=4),
                   t2buf[0:97, 0:4 * 32].rearrange("p (u n) -> p u n", u=4)]
            fence = [SAK, SBK, [t2key]]
            SK = lambda ti, h: "S1_%d" % h if ti == 0 else ("S2_%d" % h if ti == 1 else t2key)
            SKall = lambda ti: [SK(ti, h) for h in range(4)] if ti < 2 else [t2key]
            TT3 = list(enumerate(TTS))
            for ti, (c0, n) in TT3:
                bk = "big_t%d" % ti
                S1 = S1s[ti]
                for h in range(4):
                    ps, pk = PS.next()
                    PE(mm(ps[0:97, 0:n], e96[0:97, 0:97], obuf[0:97, h, c0:c0 + n]), r=[bk] + C, w=[pk])
                    A(act(S1[:, h, 0:n], ps[0:97, 0:n], AF.Abs), r=[pk], w=([SK(ti, h)] if ti < 2 else []) + (fence[ti] if h == 0 else []),
                      j=[t2key] if (ti == 2 and h > 0) else ())
            for ti, (c0, n) in TT3:
                bk = "big_t%d" % ti
                S1 = S1s[ti]
                osl = obuf[0:97, :, c0:c0 + n]
                V(ts(S1[:, :, 0:n], S1[:, :, 0:n], 1.0, ALU.max), r=SKall(ti), w=SKall(ti))
                V(lambda e, S1=S1, n=n: e.reciprocal(out=S1[:, :, 0:n], in_=S1[:, :, 0:n]), r=SKall(ti), w=SKall(ti))
                V(tt(osl, osl, S1[:, :, 0:n], ALU.mult), r=SKall(ti), w=[bk])
            for ti, (c0, n) in TT3:
                bk = "big_t%d" % ti
                pss_ = []
                for h in range(4):
                    ps1, pk1 = PS.next()
                    PE(mm(ps1[0:97, 0:n], emean[0:97, 0:97], obuf[0:97, h, c0:c0 + n]), r=[bk] + C, w=[pk1])
                    pss_.append((ps1, pk1))
                for h in range(4):
                    ps1, pk1 = pss_[h]
                    V(tt(obuf[0:97, h, c0:c0 + n], obuf[0:97, h, c0:c0 + n], ps1[0:97, 0:n], ALU.subtract), r=[pk1],
                      w=[bk] if h == 0 else (), j=() if h == 0 else [bk])
            for ti, (c0, n) in TT3:
                bk = "big_t%d" % ti
                A(act(S1s[ti][:, :, 0:n], obuf[0:97, :, c0:c0 + n], AF.Square), r=[bk], w=SKall(ti))
            for ti, (c0, n) in TT3:
                S1 = S1s[ti]
                pss_ = []
                for h in range(4):
                    ps2, pk2 = PS.next()
                    PE(mm(ps2[0:97, 0:n], emean[0:97, 0:97], S1[:, h, 0:n]), r=[SK(ti, h)] + C, w=[pk2])
                    pss_.append((ps2, pk2))
                for h in range(4):
                    ps2, pk2 = pss_[h]
                    A(act(S1[:, h, 0:n], ps2[0:97, 0:n], AF.Ln, bias=NORM_EPS), r=[pk2], w=[SK(ti, h)] if ti < 2 else (),
                      j=() if ti < 2 else [t2key])
            for ti, (c0, n) in TT3:
                bk = "big_t%d" % ti
                S1 = S1s[ti]
                osl = obuf[0:97, :, c0:c0 + n]
                A(act(S1[:, :, 0:n], S1[:, :, 0:n], AF.Exp, scale=-0.5), r=SKall(ti), w=SKall(ti))
                V(tt(osl, osl, S1[:, :, 0:n], ALU.mult), r=SKall(ti), w=[bk])
            for ti, (c0, n) in TT3:
                S1 = S1s[ti]
                pss_ = []
                for h in range(4):
                    ps3, pk3 = PS.next()
                    for k in range(8):
                        PE(mm(ps3[0:96, 0:n], U[:, k, h * 96:(h + 1) * 96], xb[:, k, c0:c0 + n], k == 0, k == 7),
                           r=[wkey, xkeys("xb", ti)], w=[pk3] if k == 0 else (), j=() if k == 0 else [pk3])
                    pss_.append((ps3, pk3))
                for h in range(4):
                    ps3, pk3 = pss_[h]
                    A(act(S1[0:96, h, 0:n], ps3[0:96, 0:n], AF.Sigmoid, bias=plc(l, "b_mo", h, (0, 96))), r=[pk3] + C,
                      w=[SK(ti, h)] if ti < 2 else (), j=() if ti < 2 else [t2key])
            for ti, (c0, n) in TT3:
                bk = "big_t%d" % ti
                S1 = S1s[ti]
                for h in range(4):
                    V(stt(hM[0:96, h, c0:c0 + n], obuf[0:96, h, c0:c0 + n], plc(l, "mng", h, (0, 96)), S1[0:96, h, 0:n], ALU.mult, ALU.mult),
                      r=[bk, SK(ti, h)] + C, w=["hml_%d_%d" % (h, ti)])

        dA = [(lambda s: wv(s, 392)[:, :, 0:384], win_cols(l, SEG["mx"], 384), None),
              (lambda s: wv(s, 392)[:, :, 384:392], win_cols(l, SEG["mi"], 8), None),
              (wq_s[0:96, :, :], w_mq[l].rearrange("h d e -> d h e"), "wq"),
              (wk_s[0:96, :, :], w_mk[l].rearrange("h d e -> d h e"), "wk")]
        stages.append((dA, stageA))
        stages.append(([(lambda s: wv(s, 384), win_cols(l, SEG["mv"], 384), None)], stageB))
        stages.append(([(lambda s: wv(s, 384), win_cols(l, SEG["mo"], 384), None)], stageC))

    def layernorm(l, gname, bname):
        st = {}

        def stats(ti):
            c0, n = TTS[ti]
            xk, bk = xkeys("xf", ti), xkeys("xb", ti)
            psm, pmk = PS.next()
            pss, psk = PS.next()
            st[ti] = (psm, pmk, pss, psk)
            for k in range(8):
                A(act(xb[:, k, c0:c0 + n], xf[:, k, c0:c0 + n], AF.Copy), r=[xk], w=[bk] if k == 0 else (), j=() if k == 0 else [bk])
            for k in range(8):
                PE(mm(psm[:, 0:n], ones128b[:, :], xb[:, k, c0:c0 + n], k == 0, k == 7), r=[bk] + CB,
                   w=[pmk] if k == 0 else (), j=() if k == 0 else [pmk])
            for k in range(8):
                sq, sqk = tmpF.next()
                sqb = sq.bitcast(BF16)
                A(act(sqb[:, 0:n], xf[:, k, c0:c0 + n], AF.Square), r=[xk], w=[sqk])
                PE(mm(pss[:, 0:n], ones128b[:, :], sqb[:, 0:n], k == 0, k == 7), r=[sqk] + CB,
                   w=[psk] if k == 0 else (), j=() if k == 0 else [psk])
            m2, m2k = tmpF.next()
            A(act(m2[:, 0:n], psm[:, 0:n], AF.Square), r=[pmk], w=[m2k])
            st[ti] = (psm, pmk, pss, psk, m2, m2k)

        def statsB(ti):
            c0, n = TTS[ti]
            psm, pmk, pss, psk, m2, m2k = st[ti]
            V(tt(pss[:, 0:n], pss[:, 0:n], m2[:, 0:n], ALU.subtract), r=[m2k], w=[psk])
            A(act(pss[:, 0:n], pss[:, 0:n], AF.Ln, bias=LN_EPS), r=[psk], w=[psk])
            A(act(pss[:, 0:n], pss[:, 0:n], AF.Exp, scale=-0.5), r=[psk], w=[psk])

        def norm(ti):
            c0, n = TTS[ti]
            xk, bk = xkeys("xf", ti), xkeys("xb", ti)
            psm, pmk, pss, psk = st[ti][0:4]
            for k in range(8):
                V(tt(xf[:, k, c0:c0 + n], xf[:, k, c0:c0 + n], psm[:, 0:n], ALU.subtract), r=[pmk], w=[xk] if k == 0 else (), j=() if k == 0 else [xk])
            for k in range(8):
                V(tt(xf[:, k, c0:c0 + n], xf[:, k, c0:c0 + n], pss[:, 0:n], ALU.mult), r=[psk, xk], w=[xk] if k == 0 else (), j=() if k == 0 else [xk])
            for k in range(8):
                V(ts(xf[:, k, c0:c0 + n], xf[:, k, c0:c0 + n], plc(l, gname, k), ALU.mult, plc(l, bname, k), ALU.add),
                  r=[xk] + C, w=[xk] if k == 0 else (), j=() if k == 0 else [xk])

        def fcopy(ti):
            c0, n = TTS[ti]
            xk, bk = xkeys("xf", ti), xkeys("xb", ti)
            for k in range(8):
                A(act(xb[:, k, c0:c0 + n], xf[:, k, c0:c0 + n], AF.Copy), r=[xk], w=[bk] if k == 0 else (), j=() if k == 0 else [bk])

        stats(0)
        statsB(0)
        stats(1)
        norm(0)
        statsB(1)
        stats(2)
        fcopy(0)
        norm(1)
        statsB(2)
        fcopy(1)
        norm(2)
        fcopy(2)

    def mix_out(l):
        hkeys = lambda br, nu, ti: ["h%s_%d_%d" % (br, u, ti) for u in range(nu)]
        for f in range(8):
            def mstage(slot, wkey, f=f):
                U = slot[:, 0:34 * 128].rearrange("p (b n) -> p b n", b=34)
                specs = [("gla", 4, 96, hG, 0), ("ml", 4, 96, hM, 4), ("hg", 2, 128, hH, 8)]
                for ti, (c0, n) in enumerate(TTS):
                    accb, acck = tmpF.next()
                    gs = []
                    for bi in range(3):
                        psg, pgk = PS.next()
                        for k in range(8):
                            PE(mm(psg[:, 0:n], U[:, 10 + bi * 8 + k, :], xb[:, k, c0:c0 + n], k == 0, k == 7),
                               r=[wkey, xkeys("xb", ti)], w=[pgk] if k == 0 else (), j=() if k == 0 else [pgk])
                        g, gk = tmpF.next()
                        A(act(g[:, 0:n], psg[:, 0:n], AF.Sigmoid, bias=plc(l, "b_mg", bi * 8 + f)), r=[pgk] + C, w=[gk])
                        gs.append((g, gk))
                    for bi, (br, nu, kr, hb, ub) in enumerate(specs):
                        g, gk = gs[bi]
                        psu, puk = PS.next()
                        for u in range(nu):
                            PE(mm(psu[:, 0:n], U[0:kr, ub + u, :], hb[0:kr, u, c0:c0 + n], u == 0, u == nu - 1),
                               r=[wkey] + hkeys(br, nu, ti), w=[puk] if u == 0 else (), j=() if u == 0 else [puk])
                        if bi == 0:
                            V(tt(accb[:, 0:n], psu[:, 0:n], g[:, 0:n], ALU.mult), r=[puk, gk], w=[acck])
                        else:
                            V(tt(g[:, 0:n], psu[:, 0:n], g[:, 0:n], ALU.mult), r=[puk, gk], w=[gk])
                            if bi == 1:
                                V(tt(accb[:, 0:n], accb[:, 0:n], g[:, 0:n], ALU.add), r=[gk, acck], w=[acck])
                            else:
                                V(tt(big[:, f, c0:c0 + n], accb[:, 0:n], g[:, 0:n], ALU.add), r=[gk, acck], w=["mg%d_%d" % (f, ti)], j=["big_t%d" % ti])
            blk = lambda s: s[:, 0:34 * 128].rearrange("p (b n) -> p b n", b=34)
            dm = [(lambda s: blk(s)[0:96, 0:4, :], w_upg[l].rearrange("(h p) n -> p h n", p=96)[:, :, f * 128:(f + 1) * 128], None),
                  (lambda s: blk(s)[0:96, 4:8, :], w_upm[l].rearrange("(h p) n -> p h n", p=96)[:, :, f * 128:(f + 1) * 128], None),
                  (lambda s: blk(s)[:, 8:10, :], w_uph[l].rearrange("(h p) n -> p h n", p=128)[:, :, f * 128:(f + 1) * 128], None)]
            for bi in range(3):
                dm.append((lambda s, bi=bi: blk(s)[:, 10 + bi * 8:18 + bi * 8, :],
                           win_cols(l, SEG["mg"] + bi * 1024 + f * 128, 128), None))
            stages.append((dm, mstage))
        for part in range(2):
            def ostage(slot, wkey, part=part):
                U = wv(slot, 512)
                loop = [(fo, ti) for fo in range(4) for ti in range(3)] if part == 0 else [(fo, ti) for ti in range(3) for fo in range(4)]
                for fo, ti in loop:
                    f = part * 4 + fo
                    c0, n = TTS[ti]
                    if True:
                        ps, pk = PS.next()
                        for k in range(8):
                            PE(mm(ps[:, 0:n], U[:, k, fo * 128:(fo + 1) * 128], big[:, k, c0:c0 + n], k == 0, k == 7),
                               r=[wkey, "mg%d_%d" % (k, ti), "big_t%d" % ti], w=[pk] if k == 0 else (), j=() if k == 0 else [pk])
                        V(stt(xf[:, f, c0:c0 + n], xf[:, f, c0:c0 + n], DN_ALPHA, ps[:, 0:n], ALU.mult, ALU.add),
                          r=[pk, xkeys("xb", ti)], w=[xkeys("xf", ti)] if f == 0 else (), j=() if f == 0 else [xkeys("xf", ti)])
                if part == 1:
                    layernorm(l, "ln1g", "ln1b")
            stages.append(([(lambda s: wv(s, 512), w_out[l].rearrange("(k p) n -> p k n", p=128)[:, :, part * 512:(part + 1) * 512], None)], ostage))

    def ffn(l):
        for jb in range(4):
            for part in range(2):
                def ustage(slot, wkey, jb=jb, part=part):
                    U = wv(slot, 512)
                    for fo in range(4):
                        fh = part * 4 + fo
                        for ti, (c0, n) in enumerate(TTS):
                            ps, pk = PS.next()
                            for k in range(8):
                                PE(mm(ps[:, 0:n], U[:, k, fo * 128:(fo + 1) * 128], xb[:, k, c0:c0 + n], k == 0, k == 7),
                                   r=[wkey, xkeys("xb", ti)], w=[pk] if k == 0 else (), j=() if k == 0 else [pk])
                            rl, rlk = tmpF.next()
                            A(act(rl[:, 0:n], ps[:, 0:n], AF.Relu), r=[pk], w=[rlk])
                            V(tt(big[:, fh, c0:c0 + n], rl[:, 0:n], rl[:, 0:n], ALU.mult), r=[rlk], w=["hid%d_%d" % (fh, ti)], j=["big_t%d" % ti])
                stages.append(([(lambda s: wv(s, 512),
                                 w_ffu[l].rearrange("(k p) n -> p k n", p=128)[:, :, jb * 1024 + part * 512:jb * 1024 + (part + 1) * 512], None)], ustage))
            for part in range(2):
                def dstage(slot, wkey, jb=jb, part=part):
                    U = wv(slot, 512)
                    last = (jb == 3 and part == 1)
                    loop = [(fo, ti) for ti in range(3) for fo in range(4)] if last else [(fo, ti) for fo in range(4) for ti in range(3)]
                    for fo, ti in loop:
                        f = part * 4 + fo
                        c0, n = TTS[ti]
                        if True:
                            ps, pk = PS.next()
                            for k in range(8):
                                PE(mm(ps[:, 0:n], U[:, k, fo * 128:(fo + 1) * 128], big[:, k, c0:c0 + n], k == 0, k == 7),
                                   r=[wkey, "hid%d_%d" % (k, ti), "big_t%d" % ti], w=[pk] if k == 0 else (), j=() if k == 0 else [pk])
                            if jb == 0:
                                V(stt(xf[:, f, c0:c0 + n], xf[:, f, c0:c0 + n], DN_ALPHA, ps[:, 0:n], ALU.mult, ALU.add),
                                  r=[pk], w=[xkeys("xf", ti)] if f == 0 else (), j=() if f == 0 else [xkeys("xf", ti)])
                            else:
                                V(tt(xf[:, f, c0:c0 + n], xf[:, f, c0:c0 + n], ps[:, 0:n], ALU.add),
                                  r=[pk], w=[xkeys("xf", ti)] if f == 0 else (), j=() if f == 0 else [xkeys("xf", ti)])
                    if jb == 3 and part == 1:
                        layernorm(l, "ln2g", "ln2b")
                stages.append(([(lambda s: wv(s, 512),
                                 w_ffd[l].rearrange("(k p) n -> p k n", p=128)[jb * 8:(jb + 1) * 8].rearrange("k p n -> p k n")[:, :, part * 512:(part + 1) * 512]
                                 if False else
                                 w_ffd[l, jb * 1024:(jb + 1) * 1024, :].rearrange("(k p) n -> p k n", p=128)[:, :, part * 512:(part + 1) * 512], None)], dstage))

    for half in halves:
        def load_x(slot, wkey, half=half):
            for ti, (c0, n) in enumerate(TTS):
                ld("sp", xf[:, :, c0:c0 + n], xT[half, :, :, c0:c0 + n], "xin%d" % ti, w=[xkeys("xf", ti)])
                for k in range(8):
                    A(act(xb[:, k, c0:c0 + n], xf[:, k, c0:c0 + n], AF.Copy), r=[xkeys("xf", ti)],
                      w=[xkeys("xb", ti)] if k == 0 else (), j=() if k == 0 else [xkeys("xb", ti)])
        stages.append((None, load_x))
        for l in range(n_layers):
            pair_branch(l, half, "gla")
            ml_branch(l, half)
            pair_branch(l, half, "hg")
            mix_out(l)
            ffn(l)

        def store_x(slot, wkey, half=half):
            for ti, (c0, n) in enumerate(TTS):
                ld("sp", yT[half, :, :, c0:c0 + n], xf[:, :, c0:c0 + n], "xin%d" % ti, r=[xkeys("xf", ti)])
        stages.append((None, store_x))

    if os.environ.get("MAXST"):
        stages = stages[:int(os.environ["MAXST"])]
    loaded = {}
    LOOK = 2
    for i in range(len(stages)):
        for j in range(i, min(i + LOOK + 1, len(stages))):
            if j not in loaded:
                loaded[j] = wload(stages[j][0]) if stages[j][0] is not None else (None, None)
        slot, key = loaded.pop(i)
        STAGE_LOG.append((i, getattr(stages[i][1], "__name__", "?"), len(P.eng_ops["pe"]), len(P.eng_ops["act"]), len(P.eng_ops["dve"])))
        stages[i][1](slot, key)
    P.emit()
    return nc, dbg_outs


def _consts():
    c = {}
    c["c_identF"] = np.eye(128, dtype=np.float32)
    c["c_ones128"] = np.full((128, 128), 1.0 / 1024.0, np.float32)
    o96 = np.zeros((128, 96), np.float32)
    o96[0:96, :] = 1.0 / 96.0
    c["c_ones96"] = o96
    o64 = np.zeros((128, 128), np.float32)
    o64[0:64, 0:64] = 1.0 / 64.0
    o64[64:128, 64:128] = 1.0 / 64.0
    c["c_ones64b"] = o64
    em = np.zeros((128, 97), np.float32)
    em[0:96, :] = 1.0 / 96.0
    c["c_emean"] = em
    e96 = np.zeros((128, 97), np.float32)
    e96[96, :] = 1.0
    c["c_e96"] = e96
    s = np.arange(64)
    causal = (s[:, None] <= s[None, :]).astype(np.float32)
    c["c_maskP"] = np.tile(np.tile(causal, (1, 4)), (2, 1))
    i = np.arange(32)
    ms = ((i[:, None] // 4 == i[None, :] // 4) & (i[:, None] <= i[None, :])).astype(np.float32)
    c["c_maskS"] = np.tile(ms, (1, 4))
    c["c_blkS"] = (i[:, None] // 4 == np.arange(8)[None, :]).astype(np.float32)
    sel = np.zeros((4, 4, 97), np.float32)
    for h in range(4):
        sel[h, h, :] = 1.0
    c["c_sel4"] = sel.reshape(4, 4 * 97)
    rm = np.ones((128, T), np.float32)
    rm[:, 0:NP:64] = 0.0
    rm[:, NP:T:4] = 0.0
    c["c_rmask"] = rm
    return c


def _pack_params(inp):
    PL = np.zeros((DEPTH, 128, NPC), np.float32)
    b_in = np.asarray(inp["b_in"], np.float32)

    def put(name, i, vec, r0=0):
        PL[:, r0:r0 + vec.shape[1], PLC[name] + i] = vec

    put("b_ga", 0, b_in[:, SEG["ga"]:SEG["ga"] + 16])
    bg = np.asarray(inp["b_gla_gate"], np.float32)
    for h in range(4):
        put("b_gq", h // 2, b_in[:, SEG["gq"] + 48 * h:SEG["gq"] + 48 * h + 48], (h % 2) * 64)
        put("b_gk", h // 2, b_in[:, SEG["gk"] + 48 * h:SEG["gk"] + 48 * h + 48], (h % 2) * 64)
        put("bg", h // 2, bg[:, 48 * h:48 * h + 48], (h % 2) * 64)
        put("b_gr", h, b_in[:, SEG["gr"] + 96 * h:SEG["gr"] + 96 * h + 96])
        put("b_mx", h, b_in[:, SEG["mx"] + 96 * h:SEG["mx"] + 96 * h + 96])
        put("b_mo", h, b_in[:, SEG["mo"] + 96 * h:SEG["mo"] + 96 * h + 96])
        put("mng", h, np.asarray(inp["ml_norm_g"], np.float32)[:, 96 * h:96 * h + 96])
        put("cb", h, np.asarray(inp["ml_conv_b"], np.float32)[:, 96 * h:96 * h + 96])
        for j in range(4):
            put("cw", 4 * h + j, np.asarray(inp["ml_conv_w"], np.float32)[:, j, 96 * h:96 * h + 96])
    put("gng", 0, np.asarray(inp["gla_norm_g"], np.float32))
    put("b_mi", 0, b_in[:, SEG["mi"]:SEG["mi"] + 4])
    put("b_mf", 0, b_in[:, SEG["mf"]:SEG["mf"] + 4])
    put("bmlf", 0, np.asarray(inp["b_ml_f"], np.float32))
    for j in range(2):
        put("b_hq", j, b_in[:, SEG["hq"] + 128 * j:SEG["hq"] + 128 * j + 128])
        put("b_hf", j, b_in[:, SEG["hf"] + 128 * j:SEG["hf"] + 128 * j + 128])
        put("b_hgt", j, b_in[:, SEG["hgt"] + 128 * j:SEG["hgt"] + 128 * j + 128])
    hn = np.asarray(inp["hg_norm_g"], np.float32)
    put("hng", 0, np.concatenate([hn, hn], axis=1))
    for bi in range(3):
        for f in range(8):
            put("b_mg", bi * 8 + f, b_in[:, SEG["mg"] + bi * 1024 + f * 128:SEG["mg"] + bi * 1024 + f * 128 + 128])
    for nm in ("ln1g", "ln1b", "ln2g", "ln2b"):
        src = np.asarray(inp[{"ln1g": "ln1_g", "ln1b": "ln1_b", "ln2g": "ln2_g", "ln2b": "ln2_b"}[nm]], np.float32)
        for k in range(8):
            put(nm, k, src[:, 128 * k:128 * k + 128])
    hglog = np.ascontiguousarray(np.asarray(inp["hg_lb_logits"], np.float32).reshape(DEPTH, 2, 128).transpose(2, 1, 0))
    return PL, hglog


def make_in_maps(inp):
    f = lambda k: np.ascontiguousarray(np.asarray(inp[k], np.float32))
    PL, hglog = _pack_params(inp)
    consts = _consts()
    shared = {"w_in": f("w_in"), "b_in": f("b_in"), "w_out": f("w_out"), "w_ff_up": f("w_ff_up"), "w_ff_down": f("w_ff_down"),
              "w_up_gla": f("w_up_gla"), "w_up_ml": f("w_up_ml"), "w_up_hg": f("w_up_hg"), "w_gla_gate": f("w_gla_gate"),
              "w_ml_q": f("w_ml_q"), "w_ml_k": f("w_ml_k"), "PL": PL, "hglog": hglog}
    shared.update(consts)
    xp, xs = f("x_prompt"), f("x_sample")
    st = {k: f(k) for k in ("state_gla", "state_mlstm_C", "state_mlstm_n", "state_mlstm_m", "state_mlstm_conv", "state_hgrn")}
    maps = []
    for c in range(NCORES):
        xt = np.empty((2, T, D), np.float32)
        for h in range(2):
            xt[h, 0:NP] = xp[c, h * NP:(h + 1) * NP]
            xt[h, NP:T] = xs[c * 16 + h * NSEQ:c * 16 + (h + 1) * NSEQ].reshape(NS, D)
        xTc = np.ascontiguousarray(xt.reshape(2, T, 8, 128).transpose(0, 3, 2, 1))
        m = dict(shared)
        m["xT"] = xTc
        sl = slice(c * 16, (c + 1) * 16)
        m["sgla"] = np.ascontiguousarray(st["state_gla"][:, sl])
        m["smC"] = np.ascontiguousarray(st["state_mlstm_C"][:, sl])
        m["smn"] = np.ascontiguousarray(st["state_mlstm_n"][:, sl])
        m["smm"] = np.ascontiguousarray(st["state_mlstm_m"][:, sl])
        m["smconv"] = np.ascontiguousarray(st["state_mlstm_conv"][:, sl])
        m["shg"] = np.ascontiguousarray(st["state_hgrn"][:, sl])
        maps.append(m)
    return maps


def assemble(results):
    yp = np.empty((NCORES, 2 * NP, D), np.float32)
    ys = np.empty((NCORES * 16, 4, D), np.float32)
    for c, r in enumerate(results):
        y = np.asarray(r["yT"]).transpose(0, 3, 2, 1).reshape(2, T, D)
        for h in range(2):
            yp[c, h * NP:(h + 1) * NP] = y[h, 0:NP]
            ys[c * 16 + h * NSEQ:c * 16 + (h + 1) * NSEQ] = y[h, NP:T].reshape(NSEQ, 4, D)
    cat = lambda k: np.ascontiguousarray(np.concatenate([np.asarray(r[k]) for r in results], axis=1)).astype(np.float32)
    return (yp, ys, cat("gla_p"), cat("mC_p"), cat("mn_p"), cat("mm_p"), cat("mconv_p"), cat("hg_p"),
            cat("gla_s"), cat("mC_s"), cat("mn_s"), cat("mm_s"), cat("mconv_s"), cat("hg_s"))


def kernel(**inputs):
    nc, _ = build_program()
    in_maps = make_in_maps(inputs)
    res = run_bass_kernel_spmd(nc, in_maps, core_ids=list(range(NCORES)))
    return assemble(res.results)
```

```python
import math
import os
import numpy as np
import concourse.bass as bass
import concourse.mybir as mybir
from concourse.bass_utils import run_bass_kernel_spmd

F32 = mybir.dt.float32
BF16 = mybir.dt.bfloat16
AF = mybir.ActivationFunctionType
ALU = mybir.AluOpType

SAME_ENG_SYNC = True
STAGE_LOG = []

D = 1024
DEPTH = 4
NCORES = 8
NP = 1024
NSEQ = 8
NS = NSEQ * 4
T = NP + NS
TTS = [(0, 512), (512, 512), (1024, NS)]
R128 = [(r * 128, 128) for r in range(8)] + [(NP, NS)]
NCH = 16
SEG = dict(gq=0, gk=192, gv=384, ga=768, gr=784, mx=1168, mv=1552, mi=1936, mf=1940, mo=1944,
           hq=2328, hf=2584, hi=2840, hgt=3096, mg=3352)
N_IN = 6424
DN_ALPHA = (2 * DEPTH) ** 0.25
LN_EPS = 1e-5
NORM_EPS = 1e-6

PLC = {}
_c = 0
for _n, _w in [("b_ga", 1), ("b_gq", 2), ("b_gk", 2), ("bg", 2), ("b_gr", 4), ("gng", 1),
               ("b_mx", 4), ("cw", 16), ("cb", 4), ("b_mi", 1), ("b_mf", 1), ("bmlf", 1), ("b_mo", 4), ("mng", 4),
               ("b_hq", 2), ("b_hf", 2), ("b_hgt", 2), ("hng", 1), ("b_mg", 24),
               ("ln1g", 8), ("ln1b", 8), ("ln2g", 8), ("ln2b", 8)]:
    PLC[_n] = _c
    _c += _w
NPC = _c


class _Op:
    __slots__ = ("eng", "fn", "deps", "is_dma", "dkey", "dcount", "ms", "msc", "fs")

    def __init__(self, eng, fn, is_dma=False, dkey=None):
        self.eng = eng
        self.fn = fn
        self.deps = []
        self.is_dma = is_dma
        self.dkey = dkey
        self.dcount = 0
        self.ms = False
        self.msc = 0
        self.fs = False


class Prog:
    ENGS = ("pe", "act", "dve", "pool", "sp")

    def __init__(self, nc):
        self.nc = nc
        self.eng_ops = {e: [] for e in self.ENGS}
        self.kw = {}
        self.kr = {}
        self.dcnt = {}
        self.bar_ops = []
        self.bar_done = set()

    def barrier(self):
        self.bar_ops = [self.eng_ops[e][-1] for e in self.ENGS if self.eng_ops[e] and not self.eng_ops[e][-1].is_dma]
        self.bar_done = set()

    def _deps(self, op, reads, writes, joins):
        deps = op.deps
        if self.bar_ops and op.eng not in self.bar_done:
            deps.extend(self.bar_ops)
            self.bar_done.add(op.eng)
        for k in reads:
            deps.extend(self.kw.get(k, ()))
        for k in writes:
            deps.extend(self.kw.get(k, ()))
            deps.extend(self.kr.get(k, ()))
        for k in joins:
            deps.extend(self.kr.get(k, ()))
        for k in reads:
            self.kr.setdefault(k, []).append(op)
        for k in writes:
            self.kw[k] = [op]
            self.kr[k] = []
        for k in joins:
            self.kw.setdefault(k, []).append(op)

    def add(self, eng, fn, reads=(), writes=(), joins=()):
        op = _Op(eng, fn)
        self._deps(op, reads, writes, joins)
        self.eng_ops[eng].append(op)
        return op

    def dma(self, eng, fn, dkey, reads=(), writes=(), joins=()):
        op = _Op(eng, fn, is_dma=True, dkey=dkey)
        self._deps(op, reads, writes, joins)
        self.dcnt[dkey] = self.dcnt.get(dkey, 0) + 16
        op.dcount = self.dcnt[dkey]
        self.eng_ops[eng].append(op)
        return op

    @staticmethod
    def _skip(d, op):
        return (not d.is_dma) and d.eng == op.eng and (not op.is_dma) and (not op.fs) and (d.eng == "pe" or not SAME_ENG_SYNC)

    def emit(self):
        nc = self.nc
        for e in self.ENGS:
            for op in self.eng_ops[e]:
                for d in op.deps:
                    if d.is_dma or self._skip(d, op):
                        continue
                    d.ms = True
        for e in self.ENGS:
            c = 0
            for op in self.eng_ops[e]:
                if op.ms and not op.is_dma:
                    c += 1
                    op.msc = c
        esem = {e: nc.alloc_semaphore("es_" + e) for e in self.ENGS}
        dsem = {k: nc.alloc_semaphore("ds_%d" % i) for i, k in enumerate(self.dcnt)}
        prog = self

        def run(e, eng):
            waited = {}
            for op in prog.eng_ops[e]:
                need = {}
                for d in op.deps:
                    if d.is_dma:
                        key = ("d", d.dkey)
                        val = d.dcount
                    else:
                        if prog._skip(d, op):
                            continue
                        key = ("e", d.eng)
                        val = d.msc
                    if val > need.get(key, 0):
                        need[key] = val
                for key, val in need.items():
                    if waited.get(key, 0) >= val:
                        continue
                    waited[key] = val
                    eng.wait_ge(dsem[key[1]] if key[0] == "d" else esem[key[1]], val)
                ins = op.fn(eng)
                if op.is_dma:
                    ins.then_inc(dsem[op.dkey], 16)
                elif op.ms:
                    ins.then_inc(esem[e], 1)
            last = {}
            for op in prog.eng_ops[e]:
                if op.is_dma:
                    last[op.dkey] = max(last.get(op.dkey, 0), op.dcount)
            for k, v in last.items():
                if waited.get(("d", k), 0) < v:
                    eng.wait_ge(dsem[k], v)

        with nc.Block() as block:
            @block.sync
            def _(eng):
                run("sp", eng)

            @block.tensor
            def _(eng):
                run("pe", eng)

            @block.scalar
            def _(eng):
                run("act", eng)

            @block.vector
            def _(eng):
                run("dve", eng)

            @block.gpsimd
            def _(eng):
                run("pool", eng)


class Ring:
    def __init__(self, aps, name):
        self.aps = aps
        self.name = name
        self.i = 0

    def next(self):
        i = self.i % len(self.aps)
        self.i += 1
        return self.aps[i], "%s%d" % (self.name, i)


def build_program(n_layers=DEPTH, halves=(0, 1), dbg=None):
    nc = bass.Bass("TRN2", target_bir_lowering=False)
    P = Prog(nc)
    dbg_outs = {}

    def din(name, shape, dt=F32):
        return nc.dram_tensor(name, list(shape), dt, kind="ExternalInput").ap()

    def dout(name, shape):
        return nc.dram_tensor(name, list(shape), F32, kind="ExternalOutput").ap()

    def sb(name, shape, dt=F32):
        return nc.alloc_sbuf_tensor(name, list(shape), dt).ap()

    xT = din("xT", [2, 128, 8, T])
    d_sgla = din("sgla", [4, 16, 4, 48, 96])
    d_smC = din("smC", [4, 16, 4, 96, 96])
    d_smn = din("smn", [4, 16, 4, 96])
    d_smm = din("smm", [4, 16, 4])
    d_smconv = din("smconv", [4, 16, 3, 384])
    d_shg = din("shg", [4, 16, 4, 64, 64])
    w_in = din("w_in", [4, D, N_IN])
    b_in = din("b_in", [4, N_IN])
    w_out = din("w_out", [4, D, D])
    w_ffu = din("w_ff_up", [4, D, 4 * D])
    w_ffd = din("w_ff_down", [4, 4 * D, D])
    w_upg = din("w_up_gla", [4, 384, D])
    w_upm = din("w_up_ml", [4, 384, D])
    w_uph = din("w_up_hg", [4, 256, D])
    w_gg = din("w_gla_gate", [4, 16, 192])
    w_mq = din("w_ml_q", [4, 4, 96, 96])
    w_mk = din("w_ml_k", [4, 4, 96, 96])
    d_PL = din("PL", [4, 128, NPC])
    d_hglog = din("hglog", [128, 2, 4])
    d_identF = din("c_identF", [128, 128])
    d_ones128 = din("c_ones128", [128, 128])
    d_ones96 = din("c_ones96", [128, 96])
    d_ones64b = din("c_ones64b", [128, 128])
    d_emean = din("c_emean", [128, 97])
    d_e96 = din("c_e96", [128, 97])
    d_maskP = din("c_maskP", [128, 256])
    d_maskS = din("c_maskS", [32, 128])
    d_blkS = din("c_blkS", [32, 8])
    d_sel4 = din("c_sel4", [4, 4 * 97])
    d_rmask = din("c_rmask", [128, T])

    yT = dout("yT", [2, 128, 8, T])
    o_gla_p = dout("gla_p", [4, 1, 4, 48, 96])
    o_mC_p = dout("mC_p", [4, 1, 4, 96, 96])
    o_mn_p = dout("mn_p", [4, 1, 4, 96])
    o_mm_p = dout("mm_p", [4, 1, 4])
    o_mconv_p = dout("mconv_p", [4, 1, 3, 384])
    o_hg_p = dout("hg_p", [4, 1, 4, 64, 64])
    o_gla_s = dout("gla_s", [4, 16, 4, 48, 96])
    o_mC_s = dout("mC_s", [4, 16, 4, 96, 96])
    o_mn_s = dout("mn_s", [4, 16, 4, 96])
    o_mm_s = dout("mm_s", [4, 16, 4])
    o_mconv_s = dout("mconv_s", [4, 16, 3, 384])
    o_hg_s = dout("hg_s", [4, 16, 4, 64, 64])

    xf = sb("xf", [128, 8, T])
    xb = sb("xb", [128, 8, T], BF16)
    bigraw = sb("bigraw", [128, 4 * T])
    big = bigraw.bitcast(BF16).rearrange("p (k t) -> p k t", k=8)
    obuf = bigraw.rearrange("p (h t) -> p h t", h=4)
    rows = bigraw[0:4, :].rearrange("p (r t) -> p r t", r=4)
    hG = sb("hG", [128, 4, T], BF16)
    hM = sb("hM", [128, 4, T], BF16)
    hH = sb("hH", [128, 2, T], BF16)
    WSLOT = 4352
    wslots = [sb("wslot%d" % i, [128, WSLOT], BF16) for i in range(3)]
    qbuf = sb("qbuf", [128, 4, T], BF16)
    kbuf = sb("kbuf", [128, 4, T], BF16)
    sA = sb("sA", [128, 2 * T])
    sB = sb("sB", [128, 2 * T])
    vTM = sb("vTM", [128, 9, 388], BF16)
    kTM = sb("kTM", [128, 9, 384], BF16)
    xcb = sb("xcb", [128, T], BF16)
    tmpF = Ring([sb("tmpF%d" % i, [128, 512]) for i in range(4)], "tf")
    attb = Ring([sb("attb%d" % i, [128, 256], BF16) for i in range(2)], "ab")
    Sg = [sb("Sg%d" % l, [128, 2, 96]) for l in range(DEPTH)]
    Cm = [sb("Cm%d" % l, [128, 4, 97]) for l in range(DEPTH)]
    Sh = [sb("Sh%d" % l, [128, 2, 64]) for l in range(DEPTH)]
    Sbf = sb("Sbf", [128, 4 * 97], BF16)
    Sbf2 = sb("Sbf2", [128, 4 * 97], BF16)
    sstR = [sb("sst0", [128, 8, 97]), sb("sst1", [128, 8, 97])]
    nin = sb("nin", [128, 4, 8])
    nout = sb("nout", [128, 4, 8])
    sstb = sb("sstb", [128, 8, 97], BF16)
    vblk = sb("vblk", [32, 8, 97], BF16)
    PLs = sb("PLs", [128, 4, NPC])
    DPs = sb("DPs", [128, 4, 4])
    lbs = sb("lbs", [128, 2, 4])
    omlb = sb("omlb", [128, 2, 4])
    hgl = sb("hgl", [128, 2, 4])
    hgt_ = sb("hgt_", [128, 2, 4])
    eA = sb("eA", [128, 2, 24])
    eB = sb("eB", [128, 2, 24])
    eC = sb("eC", [128, 2, 24])
    carry = sb("carry", [128, 4, 24])
    crow = sb("crow", [4, 24])
    crow2 = sb("crow2", [4, 24])
    m0s = sb("m0s", [4, 8])
    msout = sb("msout", [4, 8])
    msave = [sb("msave%d" % l, [4, 1]) for l in range(DEPTH)]
    mi0 = sb("mi0", [4, 1])
    ctail = [sb("ctail%d" % l, [128, 4, 3]) for l in range(DEPTH)]
    sext = sb("sext", [128, 8, 7])
    cst = sb("cst", [32, 384])
    wg_s = sb("wg_s", [16, 256], BF16)
    wq_s = sb("wq_s", [128, 4, 96], BF16)
    wk_s = sb("wk_s", [128, 4, 96], BF16)
    gaT = sb("gaT", [16, T], BF16)
    identF = sb("identF", [32, 32])
    identB = sb("identB", [128, 128], BF16)
    ones128b = sb("ones128b", [128, 128], BF16)
    ones96 = sb("ones96", [128, 96])
    ones64b = sb("ones64b", [128, 128])
    emean = sb("emean", [128, 97])
    e96 = sb("e96", [128, 97])
    maskP = sb("maskP", [128, 256], BF16)
    maskS = sb("maskS", [32, 128], BF16)
    blkS = sb("blkS", [32, 8])
    sel4 = sb("sel4", [4, 4, 97])
    rmask = sb("rmask", [128, T], BF16)
    PS = Ring([nc.alloc_psum_tensor("psb%d" % i, [128, 512], F32).ap() for i in range(8)], "ps")

    def A(fn, r=(), w=(), j=()):
        return P.add("act", fn, reads=r, writes=w, joins=j)

    def V(fn, r=(), w=(), j=()):
        return P.add("dve", fn, reads=r, writes=w, joins=j)

    pe_last = {}

    def emit_bank(seq, pok):
        seen = set()
        for idx, (oo, lh, rh, base, qs, rd, last) in enumerate(seq):
            st_ = qs not in seen
            seen.add(qs)
            PE(lambda e, oo=oo, lh=lh, rh=rh, st_=st_, last=last: e.matmul(oo, lhsT=lh, rhs=rh, start=st_, stop=last, skip_group_check=True),
               r=rd, w=[pok] if idx == 0 else (), j=() if idx == 0 else [pok], base=base)

    def G(fn, r=(), w=(), j=()):
        return P.add("pool", fn, reads=r, writes=w, joins=j)

    def PE(fn, r=(), w=(), j=(), base=None):
        op = P.add("pe", fn, reads=r, writes=w, joins=j)
        if base is not None:
            for bk in list(w) + list(j):
                prev = pe_last.get(bk)
                if prev is not None and bk in j and prev[1] != base:
                    op.deps.append(prev[0])
                    op.fs = True
                pe_last[bk] = (op, base)
        return op

    def act(out, in_, func, bias=0.0, scale=1.0):
        return lambda e: e.activation(out=out, in_=in_, func=func, bias=bias, scale=scale)

    def tt(out, in0, in1, op):
        return lambda e: e.tensor_tensor(out=out, in0=in0, in1=in1, op=op)

    def ts(out, in0, s1, op0, s2=None, op1=None):
        if op1 is None:
            return lambda e: e.tensor_scalar(out=out, in0=in0, scalar1=s1, scalar2=None, op0=op0)
        return lambda e: e.tensor_scalar(out=out, in0=in0, scalar1=s1, scalar2=s2, op0=op0, op1=op1)

    def stt(out, in0, scalar, in1, op0, op1):
        return lambda e: e.scalar_tensor_tensor(out=out, in0=in0, scalar=scalar, in1=in1, op0=op0, op1=op1)

    def mm(out, lhsT, rhs, start=True, stop=True):
        return lambda e: e.matmul(out, lhsT=lhsT, rhs=rhs, start=start, stop=stop)

    def tr(out, in_, ident):
        return lambda e: e.transpose(out=out, in_=in_, identity=ident)

    def ld(q, out, in_, dkey, w=(), j=(), r=(), nonc=False):
        if nonc:
            return P.dma(q, lambda e: e.dma_start(out=out, in_=in_, allow_slow_non_contiguous=True), dkey, reads=r, writes=w, joins=j)
        return P.dma(q, lambda e: e.dma_start(out=out, in_=in_), dkey, reads=r, writes=w, joins=j)

    def dump(name, ap, keys, shape):
        if dbg is None or name not in dbg:
            return
        t = sb("dbgs_" + name, shape)
        V(lambda e: e.tensor_copy(out=t, in_=ap), r=keys, w=["dbg_" + name])
        o = dout("dbg_" + name, shape)
        dbg_outs[name] = o
        ld("sp", o, t, "dbg_" + name, r=["dbg_" + name])

    for i_, t_ in enumerate(wslots + [wg_s, qbuf, kbuf, sA, sB, bigraw, vTM, kTM, xcb, Sbf, Sbf2, hG, hM, hH, wq_s, wk_s, gaT,
                                      eA, eB, eC, carry, cst] + tmpF.aps + attb.aps
                            + [sstR[0].rearrange("p b v -> p (b v)"), sstR[1].rearrange("p b v -> p (b v)"), sstb.rearrange("p b v -> p (b v)"),
                               sext.rearrange("p b v -> p (b v)"), vblk.rearrange("p b v -> p (b v)")]):
        nm_ = "z%d" % i_
        V(lambda e, t_=t_: e.memset(t_, 0.0), w=[nm_])
    for i_, t_ in enumerate(PS.aps):
        V(lambda e, t_=t_: e.memset(t_, 0.0), w=["ps%d" % i_])
    P.barrier()
    for dst, src in [(identF, d_identF[0:32, 0:32]), (ones96, d_ones96), (ones64b, d_ones64b), (emean, d_emean), (e96, d_e96),
                     (blkS, d_blkS), (sel4, d_sel4.rearrange("p (h n) -> p h n", h=4)),
                     (hgl, d_hglog), (PLs, d_PL.rearrange("l p n -> p l n"))]:
        ld("sp", dst, src, "const", j=["const"])
    for dst, src in [(identB, d_identF), (ones128b, d_ones128), (rmask, d_rmask), (maskP, d_maskP), (maskS, d_maskS)]:
        ld("pool", dst, src, "constb", j=["constb"])
    C = ["const"]
    CB = ["constb"]

    def plc(l, name, i=0, rows_=(0, 128)):
        c = PLC[name] + i
        return PLs[rows_[0]:rows_[1], l, c:c + 1]

    for l in range(DEPTH):
        V(ts(DPs[:, l, 0:2], PLs[:, l, PLC["bg"]:PLC["bg"] + 2], -1.0, ALU.mult), r=C, j=["dp"])
        V(tt(DPs[0:4, l, 2:3], PLs[0:4, l, PLC["b_mf"]:PLC["b_mf"] + 1], PLs[0:4, l, PLC["bmlf"]:PLC["bmlf"] + 1], ALU.add), r=C, j=["dp"])
        V(ts(DPs[0:4, l, 2:3], DPs[0:4, l, 2:3], -1.0, ALU.mult), r=["dp"], w=["dp"])
    V(tt(hgt_[:, :, 0:1], hgl[:, :, 0:1], hgl[:, :, 1:2], ALU.max), r=C, w=["lbw"])
    V(tt(hgt_[:, :, 0:1], hgt_[:, :, 0:1], hgl[:, :, 2:3], ALU.max), r=C + ["lbw"], w=["lbw"])
    V(tt(hgt_[:, :, 0:1], hgt_[:, :, 0:1], hgl[:, :, 3:4], ALU.max), r=C + ["lbw"], w=["lbw"])
    V(tt(hgl, hgl, hgt_[:, :, 0:1].to_broadcast([128, 2, 4]), ALU.subtract), r=C + ["lbw"], w=["lbe"])
    A(act(hgl, hgl, AF.Exp), r=["lbe"], w=["lbe"])
    V(tt(hgt_[:, :, 0:1], hgl[:, :, 0:1], hgl[:, :, 1:2], ALU.add), r=["lbe"], w=["lbw"])
    V(tt(hgt_[:, :, 0:1], hgt_[:, :, 0:1], hgl[:, :, 2:3], ALU.add), r=["lbe", "lbw"], w=["lbw"])
    V(tt(hgt_[:, :, 0:1], hgt_[:, :, 0:1], hgl[:, :, 3:4], ALU.add), r=["lbe", "lbw"], w=["lbw"])
    V(lambda e: e.reciprocal(out=hgt_[:, :, 0:1], in_=hgt_[:, :, 0:1]), r=["lbw"], w=["lbw"])
    V(tt(hgl, hgl, hgt_[:, :, 0:1].to_broadcast([128, 2, 4]), ALU.mult), r=["lbe", "lbw"], w=["lbe"])
    V(lambda e: e.memset(lbs[:, :, 0:1], 0.0), w=["lb"])
    V(lambda e: e.tensor_copy(out=lbs[:, :, 1:2], in_=hgl[:, :, 1:2]), r=["lbe"], j=["lb"])
    V(tt(lbs[:, :, 2:3], lbs[:, :, 1:2], hgl[:, :, 2:3], ALU.add), r=["lbe", "lb"], w=["lb"])
    V(tt(lbs[:, :, 3:4], lbs[:, :, 2:3], hgl[:, :, 3:4], ALU.add), r=["lbe", "lb"], w=["lb"])
    V(ts(omlb, lbs, -1.0, ALU.mult, 1.0, ALU.add), r=["lb"], w=["omlb"])
    LB = ["lb", "omlb"]

    stages = []
    wn = [0]

    def wload(dmas):
        i = wn[0] % 3
        wn[0] += 1
        key = "w%d" % i
        slot = wslots[i]
        first = True
        for dstf, src, okey in dmas:
            if okey is None:
                ld("pool", dstf(slot), src, key, w=[key] if first else (), j=() if first else [key])
                first = False
            else:
                ld("pool", dstf, src, okey, w=[okey])
        return slot, key

    def wv(slot, ncols, off=0, k=8):
        return slot[:, off:off + k * ncols].rearrange("p (k n) -> p k n", k=k)

    def win_cols(l, c0, n):
        return w_in[l].rearrange("(k p) n -> p k n", p=128)[:, :, c0:c0 + n]

    xkeys = lambda pre, ti: "%s_t%d" % (pre, ti)
    S1K = ["S1_%d" % u for u in range(4)] + ["mxh0", "sext0"]
    S2K = ["S2_%d" % u for u in range(4)] + ["mxh1", "sext1"]
    SAK = ["sA%d_%d" % (j, ti) for j in range(2) for ti in range(3)]
    SBK = ["sB%d_%d" % (j, ti) for j in range(2) for ti in range(3)]
    VTK = ["vT%d" % r for r in range(9)]
    KTK = ["kT%d" % r for r in range(9)]

    def proj_x(lhs_fn, M, wkey, evac):
        for ti, (c0, n) in enumerate(TTS):
            ps, pk = PS.next()
            for k in range(8):
                PE(mm(ps[0:M, 0:n], lhs_fn(k), xb[:, k, c0:c0 + n], k == 0, k == 7),
                   r=[wkey, xkeys("xb", ti)], w=[pk] if k == 0 else (), j=() if k == 0 else [pk])
            evac(ti, c0, n, ps, pk)

    def pair_branch(l, half, br):
        gla = br == "gla"
        dk = 48 if gla else 64
        dv = 96 if gla else 64
        Sst = Sg[l] if gla else Sh[l]
        dscale = (-1.0 / 16.0) if gla else 1.0
        qscale = dk ** -0.5
        bs0 = half * NSEQ
        sAv = sA.rearrange("p (j t) -> p j t", j=2)
        sBv = sB.rearrange("p (j t) -> p j t", j=2)
        sCv = bigraw[:, 0:2 * T].rearrange("p (j t) -> p j t", j=2)
        st = {}

        def rows_h(h):
            return (h % 2) * 64, (h % 2) * 64 + dk

        def stage1(slot, wkey):
            if gla:
                U = wv(slot, 528)
                def ev_ga(ti, c0, n, ps, pk):
                    A(act(gaT[0:16, c0:c0 + n], ps[0:16, 0:n], AF.Identity, bias=plc(l, "b_ga", 0, (0, 16))),
                      r=[pk] + C, w=[xkeys("gaT", ti)])
                proj_x(lambda k: U[:, k, 0:16], 16, wkey, ev_ga)
                for j in range(2):
                    for ti, (c0, n) in enumerate(TTS):
                        ps, pk = PS.next()
                        PE(mm(ps[:, 0:n], wg_s[0:16, j * 128:(j + 1) * 128], gaT[0:16, c0:c0 + n]),
                           r=["wg", xkeys("gaT", ti)], w=[pk])
                        A(act(sAv[:, j, c0:c0 + n], ps[:, 0:n], AF.Exp, bias=DPs[:, l, j:j + 1], scale=-1.0),
                          r=[pk, "dp"], w=["sA%d_%d" % (j, ti)] + S1K)
                        A(act(sAv[:, j, c0:c0 + n], sAv[:, j, c0:c0 + n], AF.Ln, bias=1.0),
                          r=["sA%d_%d" % (j, ti)], w=["sA%d_%d" % (j, ti)])
                gsrc, gk_ = sAv, "sA"
                cum, ck = sBv, "sB"
                dd, dk_ = sAv, "sA"
            else:
                U = wv(slot, 512)
                for j in range(2):
                    def ev_hf(ti, c0, n, ps, pk, j=j):
                        tf, tk = tmpF.next()
                        A(act(tf[:, 0:n], ps[:, 0:n], AF.Sigmoid, bias=plc(l, "b_hf", j)), r=[pk] + C, w=[tk])
                        V(ts(sBv[:, j, c0:c0 + n], tf[:, 0:n], omlb[:, j, l:l + 1], ALU.mult, lbs[:, j, l:l + 1], ALU.add),
                          r=[tk] + LB, w=["sB%d_%d" % (j, ti)] + S2K)
                    proj_x(lambda k, j=j: U[:, k, 256 + j * 128:256 + (j + 1) * 128], 128, wkey, ev_hf)
                allsb = ["sB%d_%d" % (j, ti) for j in range(2) for ti in range(3)]
                allsa = ["sA%d_%d" % (j, ti) for j in range(2) for ti in range(3)]
                A(act(sAv[:, :, :], sBv[:, :, :], AF.Ln), r=allsb, w=allsa + S1K)
                V(ts(sBv[:, :, :], sBv[:, :, :], -1.0, ALU.mult, 1.0, ALU.add), r=allsa, w=allsb)
                gsrc, gk_ = sAv, "sA"
                cum, ck = sCv, "BIG"
                dd, dk_ = sAv, "sA"
            allt = lambda pre, j: (["big_t0", "big_t1", "big_t2"] if pre == "BIG" else ["%s%d_%d" % (pre, j, ti) for ti in range(3)])
            for j in range(2):
                V(lambda e, j=j: e.tensor_tensor_scan(out=cum[:, j, :], data0=rmask[:, :], data1=gsrc[:, j, :], initial=0.0,
                                                      op0=ALU.mult, op1=ALU.add),
                  r=allt(gk_, j) + CB, w=allt(ck, j) + (S2K if gla else []))
                cp = cum[:, j, 0:NP].rearrange("p (c s) -> p c s", s=64)
                cs = cum[:, j, NP:T].rearrange("p (c s) -> p c s", s=4)
                dp_ = dd[:, j, 0:NP].rearrange("p (c s) -> p c s", s=64)
                ds_ = dd[:, j, NP:T].rearrange("p (c s) -> p c s", s=4)
                V(tt(dp_, cp, cp[:, :, 31:32].to_broadcast([128, 16, 64]), ALU.subtract), r=allt(ck, j), w=allt(dk_, j)[0:2])
                V(tt(ds_, cs, cs[:, :, 1:2].to_broadcast([128, 8, 4]), ALU.subtract), r=allt(ck, j), w=allt(dk_, j)[2:3])
                A(act(eA[:, j, 0:16], cp[:, :, 31], AF.Exp, scale=dscale), r=allt(ck, j), w=["eA%d" % j])
                A(act(eA[:, j, 16:24], cs[:, :, 1], AF.Exp, scale=dscale), r=allt(ck, j), j=["eA%d" % j])
                A(act(eB[:, j, 0:16], dp_[:, :, 63], AF.Exp, scale=dscale), r=allt(dk_, j), w=["eB%d" % j])
                A(act(eB[:, j, 16:24], ds_[:, :, 3], AF.Exp, scale=dscale), r=allt(dk_, j), j=["eB%d" % j])
                V(tt(eC[:, j, :], eA[:, j, :], eB[:, j, :], ALU.mult), r=["eA%d" % j, "eB%d" % j], w=["eC%d" % j])
            allck = allt(ck, 0) + [k_ for k_ in allt(ck, 1) if k_ not in allt(ck, 0)]
            alldk = allt(dk_, 0) + allt(dk_, 1)
            A(act(cum[:, :, :], dd[:, :, :], AF.Exp, bias=math.log(qscale), scale=dscale), r=alldk + ["eA0", "eA1"], w=allck)
            A(act(dd[:, :, :], dd[:, :, :], AF.Exp, scale=-dscale), r=allck + ["eB0", "eB1"], w=alldk)
            for j in range(2):
                def ev_q(ti, c0, n, ps, pk, j=j):
                    if gla:
                        V(stt(qbuf[:, j, c0:c0 + n], ps[:, 0:n], plc(l, "b_gq", j), cum[:, j, c0:c0 + n], ALU.add, ALU.mult),
                          r=[pk] + allck + C, w=["q%d_%d" % (j, ti)])
                    else:
                        sq_, sqk = tmpF.next()
                        A(act(sq_[:, 0:n], ps[:, 0:n], AF.Silu, bias=plc(l, "b_hq", j)), r=[pk] + C, w=[sqk])
                        V(tt(qbuf[:, j, c0:c0 + n], sq_[:, 0:n], cum[:, j, c0:c0 + n], ALU.mult), r=[sqk] + allck, w=["q%d_%d" % (j, ti)])
                if gla:
                    proj_x(lambda k, j=j: U[:, k, 16 + j * 128:16 + (j + 1) * 128], 128, wkey, ev_q)
                else:
                    proj_x(lambda k, j=j: U[:, k, j * 128:(j + 1) * 128], 128, wkey, ev_q)
            for j in range(2):
                if gla:
                    def ev_k(ti, c0, n, ps, pk, j=j):
                        V(stt(kbuf[:, j, c0:c0 + n], ps[:, 0:n], plc(l, "b_gk", j), dd[:, j, c0:c0 + n], ALU.add, ALU.mult),
                          r=[pk] + alldk + C, w=["k%d_%d" % (j, ti)])
                    proj_x(lambda k, j=j: U[:, k, 272 + j * 128:272 + (j + 1) * 128], 128, wkey, ev_k)
                else:
                    V(tt(kbuf[:, j, :], sBv[:, j, :], dd[:, j, :], ALU.mult),
                      r=allsb + alldk, w=["k%d_%d" % (j, ti) for ti in range(3)])
            for r, (c0, w_) in enumerate(R128):
                ps, pk = PS.next()
                psb = ps.bitcast(BF16)
                ti = min(c0 // 512, 2)
                for j in range(2):
                    PE(tr(psb[0:w_, j * 128:(j + 1) * 128], kbuf[:, j, c0:c0 + w_], identB),
                       r=["k%d_%d" % (j, ti)] + CB, w=[pk] if j == 0 else (), j=() if j == 0 else [pk])
                V(lambda e, psb=psb, r=r, w_=w_: e.tensor_copy(out=kTM[0:w_, r, 0:256], in_=psb[0:w_, 0:256]),
                  r=[pk], w=["kT%d" % r] + (["xcb1"] if r == 0 else []))

        def stage2(slot, wkey):
            if gla:
                U = wv(slot, 384)
                vcol0, nvc, bcol = 0, 384, SEG["gv"]
            else:
                U = wv(slot, 512)
                vcol0, nvc, bcol = 0, 256, SEG["hi"]
            bias_bc, bbk = tmpF.next()
            ld("sp", bias_bc[:, 0:nvc], b_in[l:l + 1, bcol:bcol + nvc].partition_broadcast(128), "bbc", w=[bbk])
            for r, (c0, w_) in enumerate(R128):
                ps, pk = PS.next()
                ti = min(c0 // 512, 2)
                for k in range(8):
                    PE(mm(ps[0:w_, 0:nvc], xb[:, k, c0:c0 + w_], U[:, k, vcol0:vcol0 + nvc], k == 0, k == 7),
                       r=[wkey, xkeys("xb", ti)], w=[pk] if k == 0 else (), j=() if k == 0 else [pk])
                V(tt(vTM[0:w_, r, 0:nvc], ps[0:w_, 0:nvc], bias_bc[0:w_, 0:nvc], ALU.add), r=[pk, bbk], w=["vT%d" % r] + (["acc"] if r == 0 else []))
            DL = int(os.environ.get("DBG_S2", "99"))
            if DL < 2:
                return
            if half == 0:
                V(lambda e: e.memset(Sst, 0.0), w=["S_" + br])
            Sb2 = [Sbf[:, 0:2 * dv].rearrange("p (j v) -> p j v", j=2), Sbf2[:, 0:2 * dv].rearrange("p (j v) -> p j v", j=2)]
            okeys = ["big_t0", "big_t1", "big_t2"]
            for j in range(2):
                V(ts(Sb2[0][:, j, :], Sst[:, j, :], eA[:, j, 0:1], ALU.mult), r=["S_" + br, "eA%d" % j],
                  w=["Sbf0"] if j == 0 else (), j=() if j == 0 else ["Sbf0"])
            for c in range(NCH):
                c0 = c * 64
                r = c // 2
                p0 = (c % 2) * 64
                ti = c // 8
                Sbv = Sb2[c % 2]
                sbk = "Sbf%d" % (c % 2)
                psa, pak = PS.next()
                for hi_, h in enumerate((0, 2, 1, 3)):
                    j = h // 2
                    a, b = rows_h(h)
                    PE(mm(psa[p0:p0 + 64, h * 64:(h + 1) * 64], kbuf[a:b, j, c0:c0 + 64], qbuf[a:b, j, c0:c0 + 64]),
                       r=["k%d_%d" % (j, ti), "q%d_%d" % (j, ti)], w=[pak] if hi_ == 0 else (), j=() if hi_ == 0 else [pak], base=a)
                pss, psk = PS.next()
                for h in range(4):
                    j = h // 2
                    i = h % 2
                    PE(mm(pss[i * 64:i * 64 + dk, j * dv:(j + 1) * dv], kTM[p0:p0 + 64, r, j * 128 + i * 64:j * 128 + i * 64 + dk],
                          vTM[p0:p0 + 64, r, h * dv:(h + 1) * dv]),
                       r=["kT%d" % r, "vT%d" % r], w=[psk] if h == 0 else (), j=() if h == 0 else [psk], base=p0)
                ab, abk = attb.next()
                V(tt(ab[p0:p0 + 64, :], psa[p0:p0 + 64, 0:256], maskP[p0:p0 + 64, :], ALU.mult), r=[pak] + CB, w=[abk])
                pso, pok = PS.next()
                inter, intra = {}, {}
                for h in range(4):
                    j = h // 2
                    a, b = rows_h(h)
                    if gla:
                        oo = pso[0:96, h * 64:(h + 1) * 64]
                        qs = "q012"
                    else:
                        oo = pso[(h % 2) * 64:(h % 2) * 64 + 64, j * 64:(j + 1) * 64]
                        qs = "lo" if h % 2 == 0 else "hi"
                    inter[h] = (oo, Sbv[a:b, j, :], qbuf[a:b, j, c0:c0 + 64], a, qs, [sbk, "q%d_%d" % (j, ti)])
                    intra[h] = (oo, vTM[p0:p0 + 64, r, h * dv:(h + 1) * dv], ab[p0:p0 + 64, h * 64:(h + 1) * 64], p0, qs, ["vT%d" % r, abk])
                if p0 == 0:
                    order = [inter[0] + (False,), intra[0] + (True,), inter[2] + (False,), intra[2] + (True,),
                             intra[1] + (False,), intra[3] + (False,), inter[1] + (True,), inter[3] + (True,)]
                else:
                    order = [inter[0] + (False,), inter[2] + (False,), intra[0] + (True,), intra[2] + (True,),
                             inter[1] + (False,), intra[1] + (True,), inter[3] + (False,), intra[3] + (True,)]
                emit_bank(order, pok)
                if gla:
                    A(act(obuf[0:96, :, c0:c0 + 64], pso[0:96, 0:256].rearrange("p (h t) -> p h t", h=4), AF.Copy),
                      r=[pok], j=[okeys[ti]])
                else:
                    A(act(obuf[:, 0:2, c0:c0 + 64], pso[:, 0:128].rearrange("p (h t) -> p h t", h=2), AF.Copy),
                      r=[pok], j=[okeys[ti]])
                for j in range(2):
                    tf, tk = tmpF.next()
                    V(ts(tf[:, 0:dv], pss[:, j * dv:(j + 1) * dv], eB[:, j, c:c + 1], ALU.mult), r=[psk, "eB%d" % j], w=[tk])
                    V(stt(Sst[:, j, :], Sst[:, j, :], eC[:, j, c:c + 1], tf[:, 0:dv], ALU.mult, ALU.add),
                      r=[tk, "eC%d" % j], w=["S_" + br] if j == 0 else (), j=() if j == 0 else ["S_" + br])
                if c + 1 < NCH:
                    nk = "Sbf%d" % ((c + 1) % 2)
                    for j in range(2):
                        V(ts(Sb2[(c + 1) % 2][:, j, :], Sst[:, j, :], eA[:, j, c + 1:c + 2], ALU.mult), r=["S_" + br, "eA%d" % j],
                          w=[nk] if j == 0 else (), j=() if j == 0 else [nk])
            if DL < 3:
                return
            c0 = NP
            psa, pak = PS.next()
            for h in range(4):
                j = h // 2
                a, b = rows_h(h)
                PE(mm(psa[0:32, h * 32:(h + 1) * 32], kbuf[a:b, j, c0:c0 + 32], qbuf[a:b, j, c0:c0 + 32]),
                   r=["k%d_2" % j, "q%d_2" % j], w=[pak] if h == 0 else (), j=() if h == 0 else [pak], base=a)
            ab, abk = attb.next()
            V(tt(ab[0:32, 0:128], psa[0:32, 0:128], maskS[:, :], ALU.mult), r=[pak] + CB, w=[abk])
            src_d = d_sgla if gla else d_shg
            dst_d = o_gla_s if gla else o_hg_s

            def sload(h):
                a_, b__ = rows_h(h)
                ld("sp", sstR[h % 2][a_:b__, :, 0:dv], src_d[l, bs0:bs0 + NSEQ, h].rearrange("b k v -> k b v"), "sst%d" % (h % 2), w=["sst%d" % (h % 2)])
            sload(0)
            for h in range(4):
                j = h // 2
                i = h % 2
                a, b = rows_h(h)
                if h + 1 < 4:
                    sload(h + 1)
                sst = sstR[h % 2]
                ssk = "sst%d" % (h % 2)
                pso, pok = PS.next()
                if gla:
                    oo = pso[0:96, h * 32:(h + 1) * 32]
                else:
                    oo = pso[i * 64:i * 64 + 64, j * 32:(j + 1) * 32]
                if DL < 4:
                    return
                V(tt(sstb[a:b, :, 0:dv], sst[a:b, :, 0:dv], eA[a:b, j, 16:24].unsqueeze(2).to_broadcast([dk, 8, dv]), ALU.mult),
                  r=[ssk, "eA%d" % j], w=["sstb"])
                PE(mm(oo, vTM[0:32, 8, h * dv:(h + 1) * dv], ab[0:32, h * 32:(h + 1) * 32], True, False),
                   r=["vT8", abk], w=[pok], base=0)
                for b_ in range(NSEQ):
                    if gla:
                        ob = pso[0:96, h * 32 + 4 * b_:h * 32 + 4 * b_ + 4]
                    else:
                        ob = pso[i * 64:i * 64 + 64, j * 32 + 4 * b_:j * 32 + 4 * b_ + 4]
                    PE(mm(ob, sstb[a:b, b_, 0:dv], qbuf[a:b, j, c0 + 4 * b_:c0 + 4 * b_ + 4], False, b_ == NSEQ - 1),
                       r=["sstb", "q%d_2" % j], j=[pok], base=a)
                if gla:
                    A(act(obuf[0:96, h, c0:c0 + 32], oo, AF.Copy), r=[pok], j=["big_t2"])
                else:
                    A(act(obuf[i * 64:i * 64 + 64, j, c0:c0 + 32], oo, AF.Copy), r=[pok], j=["big_t2"])
                if DL < 6:
                    return
                V(tt(vblk[:, :, 0:dv], vTM[0:32, 8, h * dv:(h + 1) * dv].unsqueeze(1).to_broadcast([32, 8, dv]),
                     blkS[:, :].unsqueeze(2).to_broadcast([32, 8, dv]), ALU.mult), r=["vT8"] + C, w=["vblk"])
                for g in range(2):
                    psu, puk = PS.next()
                    PE(mm(psu[a:b, 0:4 * dv], kTM[0:32, 8, j * 128 + i * 64:j * 128 + i * 64 + dk],
                          vblk[:, 4 * g:4 * g + 4, 0:dv]), r=["kT8", "vblk"], w=[puk])
                    tf, tk = tmpF.next()
                    tfv = tf[a:b, 0:4 * dv].rearrange("p (b v) -> p b v", b=4)
                    V(tt(tfv, psu[a:b, 0:4 * dv].rearrange("p (b v) -> p b v", b=4),
                         eB[a:b, j, 16 + 4 * g:20 + 4 * g].unsqueeze(2).to_broadcast([dk, 4, dv]), ALU.mult),
                      r=[puk, "eB%d" % j], w=[tk])
                    V(tt(sst[a:b, 4 * g:4 * g + 4, 0:dv], sst[a:b, 4 * g:4 * g + 4, 0:dv],
                         eC[a:b, j, 16 + 4 * g:20 + 4 * g].unsqueeze(2).to_broadcast([dk, 4, dv]), ALU.mult),
                      r=["eC%d" % j, "sstb"], w=[ssk])
                    V(tt(sst[a:b, 4 * g:4 * g + 4, 0:dv], sst[a:b, 4 * g:4 * g + 4, 0:dv], tfv, ALU.add), r=[tk], w=[ssk])
                ld("sp", dst_d[l, bs0:bs0 + NSEQ, h].rearrange("b k v -> k b v"), sst[a:b, :, 0:dv], ssk, r=[ssk])
            if gla:
                dump("og", obuf[0:96, :, NP:T], ["big_t2"], [96, 4, 32])
                dump("ogp", obuf[0:96, :, 0:128], ["big_t0"], [96, 4, 128])
            if half == 1:
                if gla:
                    for h in range(4):
                        a, b = rows_h(h)
                        ld("sp", o_gla_p[l, 0, h], Sst[a:b, h // 2, :], "S_out_" + br, r=["S_" + br])
                else:
                    for j in range(2):
                        ld("sp", o_hg_p[l, 0, 2 * j:2 * j + 2].rearrange("i k v -> (i k) v"), Sst[:, j, :], "S_out_" + br, r=["S_" + br])

        def stage3(slot, wkey):
            if gla:
                U = wv(slot, 384)
            else:
                U = wv(slot, 512)
            nu = 4 if gla else 2
            pr = 96 if gla else 128
            t2buf, t2key = tmpF.next()
            S1s = [sA[0:pr, 0:nu * 512].rearrange("p (u n) -> p u n", u=nu), sB[0:pr, 0:nu * 512].rearrange("p (u n) -> p u n", u=nu),
                   t2buf[0:pr, 0:nu * 32].rearrange("p (u n) -> p u n", u=nu)]
            fence = [SAK, SBK, []]
            SK = lambda ti, u: "S1_%d" % u if ti == 0 else ("S2_%d" % u if ti == 1 else t2key)
            SKall = lambda ti: [SK(ti, u) for u in range(nu)] if ti < 2 else [t2key]
            TT3 = list(enumerate(TTS))
            ones_ = ones96[0:96, 0:96] if gla else ones64b[:, :]
            for ti, (c0, n) in TT3:
                A(act(S1s[ti][:, :, 0:n], obuf[0:pr, 0:nu, c0:c0 + n], AF.Square), r=["big_t%d" % ti], w=SKall(ti) + fence[ti])
            for ti, (c0, n) in TT3:
                S1 = S1s[ti]
                pss_ = []
                for u in range(nu):
                    ps, pk = PS.next()
                    PE(mm(ps[0:pr, 0:n], ones_, S1[:, u, 0:n]), r=[SK(ti, u)] + C, w=[pk])
                    pss_.append((ps, pk))
                for u in range(nu):
                    ps, pk = pss_[u]
                    A(act(S1[:, u, 0:n], ps[0:pr, 0:n], AF.Ln, bias=NORM_EPS), r=[pk], w=[SK(ti, u)] if ti < 2 else (),
                      j=() if ti < 2 else [t2key])
            for ti, (c0, n) in TT3:
                S1 = S1s[ti]
                osl = obuf[0:pr, 0:nu, c0:c0 + n]
                A(act(S1[:, :, 0:n], S1[:, :, 0:n], AF.Exp, scale=-0.5), r=SKall(ti), w=SKall(ti))
                V(tt(osl, osl, S1[:, :, 0:n], ALU.mult), r=SKall(ti), w=["big_t%d" % ti])
            for ti, (c0, n) in TT3:
                S1 = S1s[ti]
                pss_ = []
                for u in range(nu):
                    ps2, pk2 = PS.next()
                    for k in range(8):
                        lhs = U[:, k, u * 96:(u + 1) * 96] if gla else U[:, k, 256 + u * 128:256 + (u + 1) * 128]
                        PE(mm(ps2[0:pr, 0:n], lhs, xb[:, k, c0:c0 + n], k == 0, k == 7), r=[wkey, xkeys("xb", ti)],
                           w=[pk2] if k == 0 else (), j=() if k == 0 else [pk2])
                    pss_.append((ps2, pk2))
                for u in range(nu):
                    ps2, pk2 = pss_[u]
                    A(act(S1[:, u, 0:n], ps2[0:pr, 0:n], AF.Silu, bias=plc(l, "b_gr" if gla else "b_hgt", u, (0, pr))),
                      r=[pk2] + C, w=[SK(ti, u)] if ti < 2 else (), j=() if ti < 2 else [t2key])
            for ti, (c0, n) in TT3:
                S1 = S1s[ti]
                osl = obuf[0:pr, 0:nu, c0:c0 + n]
                hdst = hG[0:96, :, c0:c0 + n] if gla else hH[:, :, c0:c0 + n]
                V(stt(hdst, osl, plc(l, "gng" if gla else "hng", 0, (0, pr)), S1[:, :, 0:n], ALU.mult, ALU.mult),
                  r=["big_t%d" % ti] + SKall(ti) + C, w=["h%s_%d_%d" % (br, u, ti) for u in range(nu)])

        if gla:
            gqv = lambda s: wv(s, 528)
            d1 = [(lambda s: gqv(s)[:, :, 0:16], win_cols(l, SEG["ga"], 16), None)]
            for h in range(4):
                d1.append((lambda s, h=h: gqv(s)[:, :, 16 + (h // 2) * 128 + (h % 2) * 64:16 + (h // 2) * 128 + (h % 2) * 64 + 48],
                           win_cols(l, SEG["gq"] + 48 * h, 48), None))
                d1.append((lambda s, h=h: gqv(s)[:, :, 272 + (h // 2) * 128 + (h % 2) * 64:272 + (h // 2) * 128 + (h % 2) * 64 + 48],
                           win_cols(l, SEG["gk"] + 48 * h, 48), None))
            for h in range(4):
                d1.append((wg_s[0:16, (h // 2) * 128 + (h % 2) * 64:(h // 2) * 128 + (h % 2) * 64 + 48], w_gg[l, :, 48 * h:48 * h + 48], "wg"))
            stages.append((d1, stage1))
            stages.append(([(lambda s: wv(s, 384), win_cols(l, SEG["gv"], 384), None)], stage2))
            stages.append(([(lambda s: wv(s, 384), win_cols(l, SEG["gr"], 384), None)], stage3))
        else:
            stages.append(([(lambda s: wv(s, 512), win_cols(l, SEG["hq"], 512), None)], stage1))
            stages.append(([(lambda s: wv(s, 512), win_cols(l, SEG["hi"], 512), None)], stage2))
            stages.append((None, lambda slot, wkey: stage3(st["s2slot"], st["s2key"])))
            old2 = stages[-2][1]

            def s2wrap(slot, wkey, old2=old2):
                st["s2slot"], st["s2key"] = slot, wkey
                old2(slot, wkey)
            stages[-2] = (stages[-2][0], s2wrap)

    def ml_branch(l, half):
        bs0 = half * NSEQ
        R0, R1, R2, R3 = rows[:, 0, :], rows[:, 1, :], rows[:, 2, :], rows[:, 3, :]
        RK = ["big_t0", "big_t1", "big_t2"]
        sextv = sext

        def stageA(slot, wkey):
            U = wv(slot, 392)
            ld("sp", m0s[0:4, :], d_smm[l, bs0:bs0 + NSEQ, :].rearrange("b h -> h b"), "m0s", w=["m0s"], nonc=True)
            for h_ in range(4):
                ld("sp", nin[0:96, h_, :], d_smn[l, bs0:bs0 + NSEQ, h_].rearrange("b d -> d b"), "nin", w=["nin"] if h_ == 0 else (),
                   j=() if h_ == 0 else ["nin"], nonc=True)
            ld("sp", cst[0:24, :], d_smconv[l, bs0:bs0 + NSEQ].rearrange("b j c -> (b j) c"), "cst", w=["cst"])
            mxhs = [sA, sB]
            sexts = [sA[:, 1064:1120].rearrange("p (b v) -> p b v", v=7), sB[:, 1064:1120].rearrange("p (b v) -> p b v", v=7)]
            accf = vTM.rearrange("p r c -> p (r c)").bitcast(F32)
            accP = accf[0:96, 0:NP]
            accS = accf[0:96, NP:T].rearrange("p (b s) -> p b s", s=4)
            xcbs = [xcb, kTM.rearrange("p r c -> p (r c)")[:, 0:T]]
            def do_mx(h):
                mxh = mxhs[h % 2]
                sxv = sexts[h % 2]
                mk, sk = "mxh%d" % (h % 2), "sext%d" % (h % 2)
                mfence = (S1K + SAK) if h % 2 == 0 else (S2K + SBK)

                def ev_mx(ti, c0, n, ps, pk, h=h, mxh=mxh, sxv=sxv, mk=mk, sk=sk, mfence=mfence):
                    if ti < 2:
                        A(act(mxh[0:96, 3 + c0:3 + c0 + n], ps[0:96, 0:n], AF.Identity, bias=plc(l, "b_mx", h, (0, 96))),
                          r=[pk] + C, w=([mk] + mfence) if ti == 0 else (), j=() if ti == 0 else [mk])
                    else:
                        A(act(sxv[0:96, :, 3:7], ps[0:96, 0:n].rearrange("p (b s) -> p b s", s=4), AF.Identity,
                              bias=plc(l, "b_mx", h, (0, 96))), r=[pk] + C, w=[sk])
                proj_x(lambda k, h=h: U[:, k, h * 96:(h + 1) * 96], 96, wkey, ev_mx)

            def ev_i(ti, c0, n, ps, pk):
                A(act(R0[:, c0:c0 + n], ps[0:4, 0:n], AF.Identity, bias=plc(l, "b_mi", 0, (0, 4))), r=[pk] + C,
                  w=RK if ti == 0 else (), j=() if ti == 0 else RK)
            proj_x(lambda k: U[:, k, 384:388], 4, wkey, ev_i)

            def ev_f(ti, c0, n, ps, pk):
                A(act(R1[:, c0:c0 + n], ps[0:4, 0:n], AF.Exp, bias=DPs[0:4, l, 2:3], scale=-1.0), r=[pk, "dp"], j=RK)
            proj_x(lambda k: U[:, k, 388:392], 4, wkey, ev_f)
            A(act(R1, R1, AF.Ln, bias=1.0), r=RK, w=RK)
            do_mx(0)
            do_mx(1)
            V(lambda e: e.tensor_tensor_scan(out=R2, data0=rmask[0:4, :], data1=R1, initial=0.0, op0=ALU.mult, op1=ALU.add),
              r=RK + CB, w=RK)
            V(ts(R1, R1, -1.0, ALU.mult), r=RK, w=RK)
            if half == 0:
                V(lambda e: e.memset(mi0, 0.0), w=["mi0"])
            else:
                V(lambda e: e.tensor_copy(out=mi0, in_=msave[l]), r=["msave%d" % l], w=["mi0"])
            V(lambda e: e.tensor_tensor_scan(out=R3[:, 0:NP], data0=R1[:, 0:NP], data1=R0[:, 0:NP], initial=mi0[0:4, 0:1],
                                             op0=ALU.add, op1=ALU.max), r=RK + ["mi0"], w=RK)
            for b_ in range(NSEQ):
                s0 = NP + 4 * b_
                V(lambda e, s0=s0, b_=b_: e.tensor_tensor_scan(out=R3[:, s0:s0 + 4], data0=R1[:, s0:s0 + 4], data1=R0[:, s0:s0 + 4],
                                                                initial=m0s[0:4, b_:b_ + 1], op0=ALU.add, op1=ALU.max),
                  r=RK + ["m0s"], w=RK)
            V(lambda e: e.tensor_copy(out=msave[l], in_=R3[:, NP - 1:NP]), r=RK, w=["msave%d" % l])
            R3s = R3[:, NP:T].rearrange("p (b s) -> p b s", s=4)
            V(lambda e: e.tensor_copy(out=msout, in_=R3s[:, :, 3]), r=RK, w=["msout"])
            ld("sp", o_mm_s[l, bs0:bs0 + NSEQ, :].rearrange("b h -> h b"), msout[0:4, :], "msout", r=["msout"], nonc=True)
            if half == 1:
                ld("sp", o_mm_p[l, 0:1, :].rearrange("b h -> h b"), msave[l][0:4, 0:1], "msave", r=["msave%d" % l], nonc=True)
            V(tt(R1, R3, R2, ALU.add), r=RK, w=RK)
            V(tt(R0, R0, R2, ALU.add), r=RK, w=RK)
            R1p = R1[:, 0:NP].rearrange("p (c s) -> p c s", s=64)
            R1s = R1[:, NP:T].rearrange("p (b s) -> p b s", s=4)
            R3p = R3[:, 0:NP].rearrange("p (c s) -> p c s", s=64)
            V(tt(crow[:, 1:16], R3p[:, 0:15, 63], R1p[:, 1:16, 63], ALU.subtract), r=RK, w=["crow"])
            V(tt(crow[:, 0:1], mi0[0:4, 0:1], R1[:, 63:64], ALU.subtract), r=RK + ["mi0"], j=["crow"])
            V(tt(crow[:, 16:24], m0s[0:4, :], R1s[:, :, 3], ALU.subtract), r=RK + ["m0s"], j=["crow"])
            A(act(crow2, crow, AF.Exp), r=["crow"], w=["crow2"])
            ps, pk = PS.next()
            for h in range(4):
                PE(mm(ps[0:96, h * 24:(h + 1) * 24], sel4[0:4, h, 0:96], crow2[0:4, :]), r=["crow2"] + C,
                   w=[pk] if h == 0 else (), j=() if h == 0 else [pk])
            A(act(carry[0:96, :, :], ps[0:96, 0:96].rearrange("p (h c) -> p h c", h=4), AF.Copy), r=[pk], w=["carry"])
            R2p = R2[:, 0:NP].rearrange("p (c s) -> p c s", s=64)
            R2s = R2[:, NP:T].rearrange("p (b s) -> p b s", s=4)
            R0p = R0[:, 0:NP].rearrange("p (c s) -> p c s", s=64)
            R0s = R0[:, NP:T].rearrange("p (b s) -> p b s", s=4)
            V(tt(R2p, R1p[:, :, 63:64].to_broadcast([4, 16, 64]), R1p, ALU.subtract), r=RK, w=RK)
            V(tt(R2s, R1s[:, :, 3:4].to_broadcast([4, 8, 4]), R1s, ALU.subtract), r=RK, w=RK)
            V(tt(R0p, R0p, R1p[:, :, 63:64].to_broadcast([4, 16, 64]), ALU.subtract), r=RK, w=RK)
            V(tt(R0s, R0s, R1s[:, :, 3:4].to_broadcast([4, 8, 4]), ALU.subtract), r=RK, w=RK)
            A(act(R2, R2, AF.Exp), r=RK, w=RK)
            A(act(R0, R0, AF.Exp), r=RK, w=RK)
            bias_bc, bbk = tmpF.next()
            ld("sp", bias_bc[:, 0:384], b_in[l:l + 1, SEG["mx"]:SEG["mx"] + 384].partition_broadcast(128), "bbc", w=[bbk])
            ps, pk = PS.next()
            for k in range(8):
                PE(mm(ps[0:32, 0:384], xb[:, k, NP:T], U[:, k, 0:384], k == 0, k == 7), r=[wkey, "xb_t2"],
                   w=[pk] if k == 0 else (), j=() if k == 0 else [pk])
            cTM, ctk = tmpF.next()
            V(tt(cTM[0:32, 0:384], ps[0:32, 0:384], bias_bc[0:32, 0:384], ALU.add), r=[pk, bbk], w=[ctk])
            for b_ in range(NSEQ):
                ld("sp", o_mconv_s[l, bs0 + b_], cTM[4 * b_ + 1:4 * b_ + 4, 0:384], ctk, r=[ctk])
            if half == 1:
                ps, pk = PS.next()
                for k in range(8):
                    PE(mm(ps[0:3, 0:384], xb[:, k, NP - 3:NP], U[:, k, 0:384], k == 0, k == 7), r=[wkey, "xb_t1"],
                       w=[pk] if k == 0 else (), j=() if k == 0 else [pk])
                cTM, ctk = tmpF.next()
                V(tt(cTM[0:3, 0:384], ps[0:3, 0:384], bias_bc[0:3, 0:384], ALU.add), r=[pk, bbk], w=[ctk])
                ld("sp", o_mconv_p[l, 0], cTM[0:3, 0:384], ctk, r=[ctk])
            def do_rest(h):
                mxh = mxhs[h % 2]
                sxv = sexts[h % 2]
                xc = xcbs[h % 2]
                mk, sk, xk = "mxh%d" % (h % 2), "sext%d" % (h % 2), "xcb%d" % (h % 2)
                if half == 0:
                    V(lambda e, mxh=mxh: e.memset(mxh[0:96, 0:3], 0.0), j=[mk])
                else:
                    V(lambda e, h=h, mxh=mxh: e.tensor_copy(out=mxh[0:96, 0:3], in_=ctail[l][0:96, h, :]), r=["ctail%d" % l], j=[mk])
                ps, pk = PS.next()
                PE(tr(ps[0:96, 0:24], cst[0:24, h * 96:(h + 1) * 96], identF[0:24, 0:24]), r=["cst"] + C, w=[pk])
                A(act(sxv[0:96, :, 0:3], ps[0:96, 0:24].rearrange("p (b j) -> p b j", j=3), AF.Copy), r=[pk], j=[sk])
                cwc = lambda j_, h=h: PLs[0:96, l, PLC["cw"] + 4 * h + j_:PLC["cw"] + 4 * h + j_ + 1]
                cbc = plc(l, "cb", h, (0, 96))
                V(ts(accP, mxh[0:96, 3:3 + NP], cwc(3), ALU.mult, cbc, ALU.add), r=[mk] + C, w=["acc"] + VTK)
                V(ts(accS, sxv[0:96, :, 3:7], cwc(3), ALU.mult, cbc, ALU.add), r=[sk] + C, j=["acc"])
                for j_ in range(3):
                    V(stt(accP, mxh[0:96, j_:j_ + NP], cwc(j_), accP, ALU.mult, ALU.add), r=[mk, "acc"] + C, w=["acc"])
                    V(stt(accS, sxv[0:96, :, j_:j_ + 4], cwc(j_), accS, ALU.mult, ALU.add), r=[sk, "acc"] + C, w=["acc"])
                if half == 0:
                    V(lambda e, h=h, mxh=mxh: e.tensor_copy(out=ctail[l][0:96, h, :], in_=mxh[0:96, NP:NP + 3]), r=[mk],
                      w=["ctail%d" % l] if h == 0 else (), j=() if h == 0 else ["ctail%d" % l])
                A(act(xc[0:96, :], accf[0:96, 0:T], AF.Silu), r=["acc"], w=[xk] + (KTK if h % 2 == 1 else []))

            def do_qk(h):
                xc = xcbs[h % 2]
                xk = "xcb%d" % (h % 2)
                for ti, (c0, n) in enumerate(TTS):
                    ps1, pk1 = PS.next()
                    PE(mm(ps1[0:96, 0:n], sel4[0:4, h, 0:96], R2[0:4, c0:c0 + n]), r=RK + C, w=[pk1])
                    f1, f1k = tmpF.next()
                    A(act(f1[0:96, 0:n], ps1[0:96, 0:n], AF.Copy), r=[pk1], w=[f1k])
                    ps2, pk2 = PS.next()
                    PE(mm(ps2[0:96, 0:n], sel4[0:4, h, 0:96], R0[0:4, c0:c0 + n]), r=RK + C, w=[pk2])
                    f2, f2k = tmpF.next()
                    A(act(f2[0:96, 0:n], ps2[0:96, 0:n], AF.Copy), r=[pk2], w=[f2k])
                    psq, pqk = PS.next()
                    PE(mm(psq[0:96, 0:n], wq_s[0:96, h, :], xc[0:96, c0:c0 + n]), r=["wq", xk], w=[pqk])
                    V(tt(qbuf[0:96, h, c0:c0 + n], psq[0:96, 0:n], f1[0:96, 0:n], ALU.mult), r=[pqk, f1k], w=["q%d_%d" % (h, ti)])
                    psk_, pkk = PS.next()
                    PE(mm(psk_[0:96, 0:n], wk_s[0:96, h, :], xc[0:96, c0:c0 + n]), r=["wk", xk], w=[pkk])
                    V(stt(kbuf[0:96, h, c0:c0 + n], psk_[0:96, 0:n], 96.0 ** -0.5, f2[0:96, 0:n], ALU.mult, ALU.mult),
                      r=[pkk, f2k], w=["k%d_%d" % (h, ti)])

            do_rest(0)
            for h in range(4):
                if h + 2 < 4:
                    do_mx(h + 2)
                if h + 1 < 4:
                    do_rest(h + 1)
                do_qk(h)
            for r, (c0, w_) in enumerate(R128):
                ps, pk = PS.next()
                psb = ps.bitcast(BF16)
                ti = min(c0 // 512, 2)
                for h in range(4):
                    PE(tr(psb[0:w_, h * 96:(h + 1) * 96], kbuf[0:96, h, c0:c0 + w_], identB[0:96, 0:96]),
                       r=["k%d_%d" % (h, ti)] + CB, w=[pk] if h == 0 else (), j=() if h == 0 else [pk])
                V(lambda e, psb=psb, r=r, w_=w_: e.tensor_copy(out=kTM[0:w_, r, 0:384], in_=psb[0:w_, 0:384]), r=[pk],
                  w=["kT%d" % r] + (["xcb1"] if r == 0 else []))

        def stageB(slot, wkey):
            U = wv(slot, 384)
            bias_bc, bbk = tmpF.next()
            ld("sp", bias_bc[:, 0:384], b_in[l:l + 1, SEG["mv"]:SEG["mv"] + 384].partition_broadcast(128), "bbc", w=[bbk])
            vT4 = vTM.rearrange("p r (h v) -> p r h v", h=4)
            for r, (c0, w_) in enumerate(R128):
                ps, pk = PS.next()
                ti = min(c0 // 512, 2)
                for k in range(8):
                    PE(mm(ps[0:w_, 0:384], xb[:, k, c0:c0 + w_], U[:, k, 0:384], k == 0, k == 7),
                       r=[wkey, xkeys("xb", ti)], w=[pk] if k == 0 else (), j=() if k == 0 else [pk])
                V(tt(vT4[0:w_, r, :, 0:96], ps[0:w_, 0:384].rearrange("p (h v) -> p h v", h=4),
                     bias_bc[0:w_, 0:384].rearrange("p (h v) -> p h v", h=4), ALU.add), r=[pk, bbk], w=["vT%d" % r] + (["acc"] if r == 0 else []))
                V(lambda e, r=r, w_=w_: e.memset(vT4[0:w_, r, :, 96:97], 1.0), j=["vT%d" % r])
            Cst = Cm[l]
            if half == 0:
                V(lambda e: e.memset(Cst, 0.0), w=["S_ml"])
            Cb2 = [Sbf[:, 0:388].rearrange("p (h v) -> p h v", h=4), Sbf2[:, 0:388].rearrange("p (h v) -> p h v", h=4)]
            okeys = ["big_t0", "big_t1", "big_t2"]
            V(tt(Cb2[0][0:96, :, :], Cst[0:96, :, :], carry[0:96, :, 0:1].to_broadcast([96, 4, 97]), ALU.mult), r=["S_ml", "carry"], w=["Sbf0"])
            for c in range(NCH):
                c0 = c * 64
                r = c // 2
                p0 = (c % 2) * 64
                ti = c // 8
                Cbv = Cb2[c % 2]
                sbk = "Sbf%d" % (c % 2)
                psa, pak = PS.next()
                for h in range(4):
                    PE(mm(psa[p0:p0 + 64, h * 64:(h + 1) * 64], kbuf[0:96, h, c0:c0 + 64], qbuf[0:96, h, c0:c0 + 64]),
                       r=["k%d_%d" % (h, ti), "q%d_%d" % (h, ti)], w=[pak] if h == 0 else (), j=() if h == 0 else [pak])
                pss, psk = PS.next()
                for h in range(4):
                    PE(mm(pss[0:96, h * 97:(h + 1) * 97], kTM[p0:p0 + 64, r, h * 96:(h + 1) * 96], vTM[p0:p0 + 64, r, h * 97:(h + 1) * 97]),
                       r=["kT%d" % r, "vT%d" % r], w=[psk] if h == 0 else (), j=() if h == 0 else [psk])
                ab, abk = attb.next()
                V(tt(ab[p0:p0 + 64, :], psa[p0:p0 + 64, 0:256], maskP[p0:p0 + 64, :], ALU.mult), r=[pak] + CB, w=[abk])
                pso, pok = PS.next()
                order = []
                for h in range(4):
                    order.append((pso[0:97, h * 64:(h + 1) * 64], Cbv[0:96, h, :], qbuf[0:96, h, c0:c0 + 64], 0, "all",
                                  [sbk, "q%d_%d" % (h, ti)], False))
                for h in range(4):
                    order.append((pso[0:97, h * 64:(h + 1) * 64], vTM[p0:p0 + 64, r, h * 97:(h + 1) * 97], ab[p0:p0 + 64, h * 64:(h + 1) * 64],
                                  p0, "all", ["vT%d" % r, abk], True))
                emit_bank(order, pok)
                A(act(obuf[0:97, :, c0:c0 + 64], pso[0:97, 0:256].rearrange("p (h t) -> p h t", h=4), AF.Copy),
                  r=[pok], w=okeys if c == 0 else (), j=() if c == 0 else [okeys[ti]])
                V(tt(Cst[0:96, :, :], Cst[0:96, :, :], carry[0:96, :, c:c + 1].to_broadcast([96, 4, 97]), ALU.mult),
                  r=["carry"], w=["S_ml"])
                V(tt(Cst[0:96, :, :], Cst[0:96, :, :], pss[0:96, 0:388].rearrange("p (h v) -> p h v", h=4), ALU.add),
                  r=[psk], w=["S_ml"])
                if c + 1 < NCH:
                    V(tt(Cb2[(c + 1) % 2][0:96, :, :], Cst[0:96, :, :], carry[0:96, :, c + 1:c + 2].to_broadcast([96, 4, 97]), ALU.mult),
                      r=["S_ml", "carry"], w=["Sbf%d" % ((c + 1) % 2)])
            c0 = NP
            psa, pak = PS.next()
            for h in range(4):
                PE(mm(psa[0:32, h * 32:(h + 1) * 32], kbuf[0:96, h, c0:c0 + 32], qbuf[0:96, h, c0:c0 + 32]),
                   r=["k%d_2" % h, "q%d_2" % h], w=[pak] if h == 0 else (), j=() if h == 0 else [pak])
            ab, abk = attb.next()
            V(tt(ab[0:32, 0:128], psa[0:32, 0:128], maskS[:, :], ALU.mult), r=[pak] + CB, w=[abk])
            def mload(h):
                ld("sp", sstR[h % 2][0:96, :, 0:96], d_smC[l, bs0:bs0 + NSEQ, h].rearrange("b d e -> d b e"), "sst%d" % (h % 2), w=["sst%d" % (h % 2)])
            mload(0)
            for h in range(4):
                if h + 1 < 4:
                    mload(h + 1)
                sst = sstR[h % 2]
                ssk = "sst%d" % (h % 2)
                pso, pok = PS.next()
                oo = pso[0:97, h * 32:(h + 1) * 32]
                V(lambda e, sst=sst, h=h: e.tensor_copy(out=sst[0:96, :, 96], in_=nin[0:96, h, :]), r=["nin"], j=[ssk])
                V(tt(sst[0:96, :, :], sst[0:96, :, :], carry[0:96, h, 16:24].unsqueeze(2).to_broadcast([96, 8, 97]), ALU.mult),
                  r=["carry"], w=[ssk])
                V(lambda e, sst=sst: e.tensor_copy(out=sstb[0:96, :, :], in_=sst[0:96, :, :]), r=[ssk], w=["sstb"])
                PE(mm(oo, vTM[0:32, 8, h * 97:(h + 1) * 97], ab[0:32, h * 32:(h + 1) * 32], True, False),
                   r=["vT8", abk], w=[pok])
                for b_ in range(NSEQ):
                    PE(mm(pso[0:97, h * 32 + 4 * b_:h * 32 + 4 * b_ + 4], sstb[0:96, b_, :], qbuf[0:96, h, c0 + 4 * b_:c0 + 4 * b_ + 4],
                          False, b_ == NSEQ - 1), r=["sstb", "q%d_2" % h], j=[pok])
                A(act(obuf[0:97, h, c0:c0 + 32], oo, AF.Copy), r=[pok], j=["big_t2"])
                V(tt(vblk[:, :, :], vTM[0:32, 8, h * 97:(h + 1) * 97].unsqueeze(1).to_broadcast([32, 8, 97]),
                     blkS[:, :].unsqueeze(2).to_broadcast([32, 8, 97]), ALU.mult), r=["vT8"] + C, w=["vblk"])
                for g in range(2):
                    psu, puk = PS.next()
                    PE(mm(psu[0:96, 0:388], kTM[0:32, 8, h * 96:(h + 1) * 96], vblk[:, 4 * g:4 * g + 4, :]), r=["kT8", "vblk"], w=[puk])
                    V(tt(sst[0:96, 4 * g:4 * g + 4, :], sst[0:96, 4 * g:4 * g + 4, :],
                         psu[0:96, 0:388].rearrange("p (b v) -> p b v", b=4), ALU.add), r=[puk, "sstb"], w=[ssk])
                ld("sp", o_mC_s[l, bs0:bs0 + NSEQ, h].rearrange("b d e -> d b e"), sst[0:96, :, 0:96], ssk, r=[ssk])
                V(lambda e, sst=sst, h=h: e.tensor_copy(out=nout[0:96, h, :], in_=sst[0:96, :, 96]), r=[ssk],
                  w=["nout"] if h == 0 else (), j=() if h == 0 else ["nout"])
            for h_ in range(4):
                ld("sp", o_mn_s[l, bs0:bs0 + NSEQ, h_].rearrange("b d -> d b"), nout[0:96, h_, :], "nout", r=["nout"], nonc=True)
            if half == 1:
                ld("sp", o_mC_p[l, 0].rearrange("h d e -> d h e"), Cst[0:96, :, 0:96], "S_out_ml", r=["S_ml"])
                ld("sp", o_mn_p[l, 0].rearrange("h d -> d h"), Cst[0:96, :, 96], "S_out_ml", r=["S_ml"], nonc=True)

        def stageC(slot, wkey):
            U = wv(slot, 384)
            t2buf, t2key = tmpF.next()
            S1s = [sA[0:97, 0:4 * 512].rearrange("p (u n) -> p u n", u=4), sB[0:97, 0:4 * 512].rearrange("p (u n) -> p u n", u=4),
                   t2buf[0:97, 0:4 * 32].rearrange("p (u n) -> p u n", u=4)]
            fence = [SAK, SBK, [t2key]]
            SK = lambda ti, h: "S1_%d" % h if ti == 0 else ("S2_%d" % h if ti == 1 else t2key)
            SKall = lambda ti: [SK(ti, h) for h in range(4)] if ti < 2 else [t2key]
            TT3 = list(enumerate(TTS))
            for ti, (c0, n) in TT3:
                bk = "big_t%d" % ti
                S1 = S1s[ti]
                for h in range(4):
                    ps, pk = PS.next()
                    PE(mm(ps[0:97, 0:n], e96[0:97, 0:97], obuf[0:97, h, c0:c0 + n]), r=[bk] + C, w=[pk])
                    A(act(S1[:, h, 0:n], ps[0:97, 0:n], AF.Abs), r=[pk], w=([SK(ti, h)] if ti < 2 else []) + (fence[ti] if h == 0 else []),
                      j=[t2key] if (ti == 2 and h > 0) else ())
            for ti, (c0, n) in TT3:
                bk = "big_t%d" % ti
                S1 = S1s[ti]
                osl = obuf[0:97, :, c0:c0 + n]
                V(ts(S1[:, :, 0:n], S1[:, :, 0:n], 1.0, ALU.max), r=SKall(ti), w=SKall(ti))
                V(lambda e, S1=S1, n=n: e.reciprocal(out=S1[:, :, 0:n], in_=S1[:, :, 0:n]), r=SKall(ti), w=SKall(ti))
                V(tt(osl, osl, S1[:, :, 0:n], ALU.mult), r=SKall(ti), w=[bk])
            for ti, (c0, n) in TT3:
                bk = "big_t%d" % ti
                pss_ = []
                for h in range(4):
                    ps1, pk1 = PS.next()
                    PE(mm(ps1[0:97, 0:n], emean[0:97, 0:97], obuf[0:97, h, c0:c0 + n]), r=[bk] + C, w=[pk1])
                    pss_.append((ps1, pk1))
                for h in range(4):
                    ps1, pk1 = pss_[h]
                    V(tt(obuf[0:97, h, c0:c0 + n], obuf[0:97, h, c0:c0 + n], ps1[0:97, 0:n], ALU.subtract), r=[pk1],
                      w=[bk] if h == 0 else (), j=() if h == 0 else [bk])
            for ti, (c0, n) in TT3:
                bk = "big_t%d" % ti
                A(act(S1s[ti][:, :, 0:n], obuf[0:97, :, c0:c0 + n], AF.Square), r=[bk], w=SKall(ti))
            for ti, (c0, n) in TT3:
                S1 = S1s[ti]
                pss_ = []
                for h in range(4):
                    ps2, pk2 = PS.next()
                    PE(mm(ps2[0:97, 0:n], emean[0:97, 0:97], S1[:, h, 0:n]), r=[SK(ti, h)] + C, w=[pk2])
                    pss_.append((ps2, pk2))
                for h in range(4):
                    ps2, pk2 = pss_[h]
                    A(act(S1[:, h, 0:n], ps2[0:97, 0:n], AF.Ln, bias=NORM_EPS), r=[pk2], w=[SK(ti, h)] if ti < 2 else (),
                      j=() if ti < 2 else [t2key])
            for ti, (c0, n) in TT3:
                bk = "big_t%d" % ti
                S1 = S1s[ti]
                osl = obuf[0:97, :, c0:c0 + n]
                A(act(S1[:, :, 0:n], S1[:, :, 0:n], AF.Exp, scale=-0.5), r=SKall(ti), w=SKall(ti))
                V(tt(osl, osl, S1[:, :, 0:n], ALU.mult), r=SKall(ti), w=[bk])
            for ti, (c0, n) in TT3:
                S1 = S1s[ti]
                pss_ = []
                for h in range(4):
                    ps3, pk3 = PS.next()
                    for k in range(8):
                        PE(mm(ps3[0:96, 0:n], U[:, k, h * 96:(h + 1) * 96], xb[:, k, c0:c0 + n], k == 0, k == 7),
                           r=[wkey, xkeys("xb", ti)], w=[pk3] if k == 0 else (), j=() if k == 0 else [pk3])
                    pss_.append((ps3, pk3))
                for h in range(4):
                    ps3, pk3 = pss_[h]
                    A(act(S1[0:96, h, 0:n], ps3[0:96, 0:n], AF.Sigmoid, bias=plc(l, "b_mo", h, (0, 96))), r=[pk3] + C,
                      w=[SK(ti, h)] if ti < 2 else (), j=() if ti < 2 else [t2key])
            for ti, (c0, n) in TT3:
                bk = "big_t%d" % ti
                S1 = S1s[ti]
                for h in range(4):
                    V(stt(hM[0:96, h, c0:c0 + n], obuf[0:96, h, c0:c0 + n], plc(l, "mng", h, (0, 96)), S1[0:96, h, 0:n], ALU.mult, ALU.mult),
                      r=[bk, SK(ti, h)] + C, w=["hml_%d_%d" % (h, ti)])

        dA = [(lambda s: wv(s, 392)[:, :, 0:384], win_cols(l, SEG["mx"], 384), None),
              (lambda s: wv(s, 392)[:, :, 384:392], win_cols(l, SEG["mi"], 8), None),
              (wq_s[0:96, :, :], w_mq[l].rearrange("h d e -> d h e"), "wq"),
              (wk_s[0:96, :, :], w_mk[l].rearrange("h d e -> d h e"), "wk")]
        stages.append((dA, stageA))
        stages.append(([(lambda s: wv(s, 384), win_cols(l, SEG["mv"], 384), None)], stageB))
        stages.append(([(lambda s: wv(s, 384), win_cols(l, SEG["mo"], 384), None)], stageC))

    def layernorm(l, gname, bname):
        st = {}

        def stats(ti):
            c0, n = TTS[ti]
            xk, bk = xkeys("xf", ti), xkeys("xb", ti)
            psm, pmk = PS.next()
            pss, psk = PS.next()
            st[ti] = (psm, pmk, pss, psk)
            for k in range(8):
                A(act(xb[:, k, c0:c0 + n], xf[:, k, c0:c0 + n], AF.Copy), r=[xk], w=[bk] if k == 0 else (), j=() if k == 0 else [bk])
            for k in range(8):
                PE(mm(psm[:, 0:n], ones128b[:, :], xb[:, k, c0:c0 + n], k == 0, k == 7), r=[bk] + CB,
                   w=[pmk] if k == 0 else (), j=() if k == 0 else [pmk])
            for k in range(8):
                sq, sqk = tmpF.next()
                sqb = sq.bitcast(BF16)
                A(act(sqb[:, 0:n], xf[:, k, c0:c0 + n], AF.Square), r=[xk], w=[sqk])
                PE(mm(pss[:, 0:n], ones128b[:, :], sqb[:, 0:n], k == 0, k == 7), r=[sqk] + CB,
                   w=[psk] if k == 0 else (), j=() if k == 0 else [psk])
            m2, m2k = tmpF.next()
            A(act(m2[:, 0:n], psm[:, 0:n], AF.Square), r=[pmk], w=[m2k])
            st[ti] = (psm, pmk, pss, psk, m2, m2k)

        def statsB(ti):
            c0, n = TTS[ti]
            psm, pmk, pss, psk, m2, m2k = st[ti]
            V(tt(pss[:, 0:n], pss[:, 0:n], m2[:, 0:n], ALU.subtract), r=[m2k], w=[psk])
            A(act(pss[:, 0:n], pss[:, 0:n], AF.Ln, bias=LN_EPS), r=[psk], w=[psk])
            A(act(pss[:, 0:n], pss[:, 0:n], AF.Exp, scale=-0.5), r=[psk], w=[psk])

        def norm(ti):
            c0, n = TTS[ti]
            xk, bk = xkeys("xf", ti), xkeys("xb", ti)
            psm, pmk, pss, psk = st[ti][0:4]
            for k in range(8):
                V(tt(xf[:, k, c0:c0 + n], xf[:, k, c0:c0 + n], psm[:, 0:n], ALU.subtract), r=[pmk], w=[xk] if k == 0 else (), j=() if k == 0 else [xk])
            for k in range(8):
                V(tt(xf[:, k, c0:c0 + n], xf[:, k, c0:c0 + n], pss[:, 0:n], ALU.mult), r=[psk, xk], w=[xk] if k == 0 else (), j=() if k == 0 else [xk])
            for k in range(8):
                V(ts(xf[:, k, c0:c0 + n], xf[:, k, c0:c0 + n], plc(l, gname, k), ALU.mult, plc(l, bname, k), ALU.add),
                  r=[xk] + C, w=[xk] if k == 0 else (), j=() if k == 0 else [xk])

        def fcopy(ti):
            c0, n = TTS[ti]
            xk, bk = xkeys("xf", ti), xkeys("xb", ti)
            for k in range(8):
                A(act(xb[:, k, c0:c0 + n], xf[:, k, c0:c0 + n], AF.Copy), r=[xk], w=[bk] if k == 0 else (), j=() if k == 0 else [bk])

        stats(0)
        statsB(0)
        stats(1)
        norm(0)
        statsB(1)
        stats(2)
        fcopy(0)
        norm(1)
        statsB(2)
        fcopy(1)
        norm(2)
        fcopy(2)

    def mix_out(l):
        hkeys = lambda br, nu, ti: ["h%s_%d_%d" % (br, u, ti) for u in range(nu)]
        for f in range(8):
            def mstage(slot, wkey, f=f):
                U = slot[:, 0:34 * 128].rearrange("p (b n) -> p b n", b=34)
                specs = [("gla", 4, 96, hG, 0), ("ml", 4, 96, hM, 4), ("hg", 2, 128, hH, 8)]
                for ti, (c0, n) in enumerate(TTS):
                    accb, acck = tmpF.next()
                    gs = []
                    for bi in range(3):
                        psg, pgk = PS.next()
                        for k in range(8):
                            PE(mm(psg[:, 0:n], U[:, 10 + bi * 8 + k, :], xb[:, k, c0:c0 + n], k == 0, k == 7),
                               r=[wkey, xkeys("xb", ti)], w=[pgk] if k == 0 else (), j=() if k == 0 else [pgk])
                        g, gk = tmpF.next()
                        A(act(g[:, 0:n], psg[:, 0:n], AF.Sigmoid, bias=plc(l, "b_mg", bi * 8 + f)), r=[pgk] + C, w=[gk])
                        gs.append((g, gk))
                    for bi, (br, nu, kr, hb, ub) in enumerate(specs):
                        g, gk = gs[bi]
                        psu, puk = PS.next()
                        for u in range(nu):
                            PE(mm(psu[:, 0:n], U[0:kr, ub + u, :], hb[0:kr, u, c0:c0 + n], u == 0, u == nu - 1),
                               r=[wkey] + hkeys(br, nu, ti), w=[puk] if u == 0 else (), j=() if u == 0 else [puk])
                        if bi == 0:
                            V(tt(accb[:, 0:n], psu[:, 0:n], g[:, 0:n], ALU.mult), r=[puk, gk], w=[acck])
                        else:
                            V(tt(g[:, 0:n], psu[:, 0:n], g[:, 0:n], ALU.mult), r=[puk, gk], w=[gk])
                            if bi == 1:
                                V(tt(accb[:, 0:n], accb[:, 0:n], g[:, 0:n], ALU.add), r=[gk, acck], w=[acck])
                            else:
                                V(tt(big[:, f, c0:c0 + n], accb[:, 0:n], g[:, 0:n], ALU.add), r=[gk, acck], w=["mg%d_%d" % (f, ti)], j=["big_t%d" % ti])
            blk = lambda s: s[:, 0:34 * 128].rearrange("p (b n) -> p b n", b=34)
            dm = [(lambda s: blk(s)[0:96, 0:4, :], w_upg[l].rearrange("(h p) n -> p h n", p=96)[:, :, f * 128:(f + 1) * 128], None),
                  (lambda s: blk(s)[0:96, 4:8, :], w_upm[l].rearrange("(h p) n -> p h n", p=96)[:, :, f * 128:(f + 1) * 128], None),
                  (lambda s: blk(s)[:, 8:10, :], w_uph[l].rearrange("(h p) n -> p h n", p=128)[:, :, f * 128:(f + 1) * 128], None)]
            for bi in range(3):
                dm.append((lambda s, bi=bi: blk(s)[:, 10 + bi * 8:18 + bi * 8, :],
                           win_cols(l, SEG["mg"] + bi * 1024 + f * 128, 128), None))
            stages.append((dm, mstage))
        for part in range(2):
            def ostage(slot, wkey, part=part):
                U = wv(slot, 512)
                loop = [(fo, ti) for fo in range(4) for ti in range(3)] if part == 0 else [(fo, ti) for ti in range(3) for fo in range(4)]
                for fo, ti in loop:
                    f = part * 4 + fo
                    c0, n = TTS[ti]
                    if True:
                        ps, pk = PS.next()
                        for k in range(8):
                            PE(mm(ps[:, 0:n], U[:, k, fo * 128:(fo + 1) * 128], big[:, k, c0:c0 + n], k == 0, k == 7),
                               r=[wkey, "mg%d_%d" % (k, ti), "big_t%d" % ti], w=[pk] if k == 0 else (), j=() if k == 0 else [pk])
                        V(stt(xf[:, f, c0:c0 + n], xf[:, f, c0:c0 + n], DN_ALPHA, ps[:, 0:n], ALU.mult, ALU.add),
                          r=[pk, xkeys("xb", ti)], w=[xkeys("xf", ti)] if f == 0 else (), j=() if f == 0 else [xkeys("xf", ti)])
                if part == 1:
                    layernorm(l, "ln1g", "ln1b")
            stages.append(([(lambda s: wv(s, 512), w_out[l].rearrange("(k p) n -> p k n", p=128)[:, :, part * 512:(part + 1) * 512], None)], ostage))

    def ffn(l):
        for jb in range(4):
            for part in range(2):
                def ustage(slot, wkey, jb=jb, part=part):
                    U = wv(slot, 512)
                    first = (jb == 0 and part == 0)
                    loop = [(fo, ti) for ti in range(3) for fo in range(4)] if first else [(fo, ti) for fo in range(4) for ti in range(3)]
                    for fo, ti in loop:
                        fh = part * 4 + fo
                        c0, n = TTS[ti]
                        if True:
                            ps, pk = PS.next()
                            for k in range(8):
                                PE(mm(ps[:, 0:n], U[:, k, fo * 128:(fo + 1) * 128], xb[:, k, c0:c0 + n], k == 0, k == 7),
                                   r=[wkey, xkeys("xb", ti)], w=[pk] if k == 0 else (), j=() if k == 0 else [pk])
                            rl, rlk = tmpF.next()
                            A(act(rl[:, 0:n], ps[:, 0:n], AF.Relu), r=[pk], w=[rlk])
                            V(tt(big[:, fh, c0:c0 + n], rl[:, 0:n], rl[:, 0:n], ALU.mult), r=[rlk], w=["hid%d_%d" % (fh, ti)], j=["big_t%d" % ti])
                stages.append(([(lambda s: wv(s, 512),
                                 w_ffu[l].rearrange("(k p) n -> p k n", p=128)[:, :, jb * 1024 + part * 512:jb * 1024 + (part + 1) * 512], None)], ustage))
            for part in range(2):
                def dstage(slot, wkey, jb=jb, part=part):
                    U = wv(slot, 512)
                    last = (jb == 3 and part == 1)
                    loop = [(fo, ti) for ti in range(3) for fo in range(4)] if last else [(fo, ti) for fo in range(4) for ti in range(3)]
                    for fo, ti in loop:
                        f = part * 4 + fo
                        c0, n = TTS[ti]
                        if True:
                            ps, pk = PS.next()
                            for k in range(8):
                                PE(mm(ps[:, 0:n], U[:, k, fo * 128:(fo + 1) * 128], big[:, k, c0:c0 + n], k == 0, k == 7),
                                   r=[wkey, "hid%d_%d" % (k, ti), "big_t%d" % ti], w=[pk] if k == 0 else (), j=() if k == 0 else [pk])
                            if jb == 0:
                                V(stt(xf[:, f, c0:c0 + n], xf[:, f, c0:c0 + n], DN_ALPHA, ps[:, 0:n], ALU.mult, ALU.add),
                                  r=[pk], w=[xkeys("xf", ti)] if f == 0 else (), j=() if f == 0 else [xkeys("xf", ti)])
                            else:
                                V(tt(xf[:, f, c0:c0 + n], xf[:, f, c0:c0 + n], ps[:, 0:n], ALU.add),
                                  r=[pk], w=[xkeys("xf", ti)] if f == 0 else (), j=() if f == 0 else [xkeys("xf", ti)])
                    if jb == 3 and part == 1:
                        layernorm(l, "ln2g", "ln2b")
                stages.append(([(lambda s: wv(s, 512),
                                 w_ffd[l].rearrange("(k p) n -> p k n", p=128)[jb * 8:(jb + 1) * 8].rearrange("k p n -> p k n")[:, :, part * 512:(part + 1) * 512]
                                 if False else
                                 w_ffd[l, jb * 1024:(jb + 1) * 1024, :].rearrange("(k p) n -> p k n", p=128)[:, :, part * 512:(part + 1) * 512], None)], dstage))

    for half in halves:
        def load_x(slot, wkey, half=half):
            for ti, (c0, n) in enumerate(TTS):
                ld("sp", xf[:, :, c0:c0 + n], xT[half, :, :, c0:c0 + n], "xin%d" % ti, w=[xkeys("xf", ti)])
                for k in range(8):
                    A(act(xb[:, k, c0:c0 + n], xf[:, k, c0:c0 + n], AF.Copy), r=[xkeys("xf", ti)],
                      w=[xkeys("xb", ti)] if k == 0 else (), j=() if k == 0 else [xkeys("xb", ti)])
        stages.append((None, load_x))
        for l in range(n_layers):
            pair_branch(l, half, "gla")
            ml_branch(l, half)
            pair_branch(l, half, "hg")
            mix_out(l)
            ffn(l)

        def store_x(slot, wkey, half=half):
            for ti, (c0, n) in enumerate(TTS):
                ld("sp", yT[half, :, :, c0:c0 + n], xf[:, :, c0:c0 + n], "xin%d" % ti, r=[xkeys("xf", ti)])
        stages.append((None, store_x))

    if os.environ.get("MAXST"):
        stages = stages[:int(os.environ["MAXST"])]
    loaded = {}
    LOOK = 2
    for i in range(len(stages)):
        for j in range(i, min(i + LOOK + 1, len(stages))):
            if j not in loaded:
                loaded[j] = wload(stages[j][0]) if stages[j][0] is not None else (None, None)
        slot, key = loaded.pop(i)
        STAGE_LOG.append((i, getattr(stages[i][1], "__name__", "?"), len(P.eng_ops["pe"]), len(P.eng_ops["act"]), len(P.eng_ops["dve"])))
        stages[i][1](slot, key)
    P.emit()
    return nc, dbg_outs


def _consts():
    c = {}
    c["c_identF"] = np.eye(128, dtype=np.float32)
    c["c_ones128"] = np.full((128, 128), 1.0 / 1024.0, np.float32)
    o96 = np.zeros((128, 96), np.float32)
    o96[0:96, :] = 1.0 / 96.0
    c["c_ones96"] = o96
    o64 = np.zeros((128, 128), np.float32)
    o64[0:64, 0:64] = 1.0 / 64.0
    o64[64:128, 64:128] = 1.0 / 64.0
    c["c_ones64b"] = o64
    em = np.zeros((128, 97), np.float32)
    em[0:96, :] = 1.0 / 96.0
    c["c_emean"] = em
    e96 = np.zeros((128, 97), np.float32)
    e96[96, :] = 1.0
    c["c_e96"] = e96
    s = np.arange(64)
    causal = (s[:, None] <= s[None, :]).astype(np.float32)
    c["c_maskP"] = np.tile(np.tile(causal, (1, 4)), (2, 1))
    i = np.arange(32)
    ms = ((i[:, None] // 4 == i[None, :] // 4) & (i[:, None] <= i[None, :])).astype(np.float32)
    c["c_maskS"] = np.tile(ms, (1, 4))
    c["c_blkS"] = (i[:, None] // 4 == np.arange(8)[None, :]).astype(np.float32)
    sel = np.zeros((4, 4, 97), np.float32)
    for h in range(4):
        sel[h, h, :] = 1.0
    c["c_sel4"] = sel.reshape(4, 4 * 97)
    rm = np.ones((128, T), np.float32)
    rm[:, 0:NP:64] = 0.0
    rm[:, NP:T:4] = 0.0
    c["c_rmask"] = rm
    return c


def _pack_params(inp):
    PL = np.zeros((DEPTH, 128, NPC), np.float32)
    b_in = np.asarray(inp["b_in"], np.float32)

    def put(name, i, vec, r0=0):
        PL[:, r0:r0 + vec.shape[1], PLC[name] + i] = vec

    put("b_ga", 0, b_in[:, SEG["ga"]:SEG["ga"] + 16])
    bg = np.asarray(inp["b_gla_gate"], np.float32)
    for h in range(4):
        put("b_gq", h // 2, b_in[:, SEG["gq"] + 48 * h:SEG["gq"] + 48 * h + 48], (h % 2) * 64)
        put("b_gk", h // 2, b_in[:, SEG["gk"] + 48 * h:SEG["gk"] + 48 * h + 48], (h % 2) * 64)
        put("bg", h // 2, bg[:, 48 * h:48 * h + 48], (h % 2) * 64)
        put("b_gr", h, b_in[:, SEG["gr"] + 96 * h:SEG["gr"] + 96 * h + 96])
        put("b_mx", h, b_in[:, SEG["mx"] + 96 * h:SEG["mx"] + 96 * h + 96])
        put("b_mo", h, b_in[:, SEG["mo"] + 96 * h:SEG["mo"] + 96 * h + 96])
        put("mng", h, np.asarray(inp["ml_norm_g"], np.float32)[:, 96 * h:96 * h + 96])
        put("cb", h, np.asarray(inp["ml_conv_b"], np.float32)[:, 96 * h:96 * h + 96])
        for j in range(4):
            put("cw", 4 * h + j, np.asarray(inp["ml_conv_w"], np.float32)[:, j, 96 * h:96 * h + 96])
    put("gng", 0, np.asarray(inp["gla_norm_g"], np.float32))
    put("b_mi", 0, b_in[:, SEG["mi"]:SEG["mi"] + 4])
    put("b_mf", 0, b_in[:, SEG["mf"]:SEG["mf"] + 4])
    put("bmlf", 0, np.asarray(inp["b_ml_f"], np.float32))
    for j in range(2):
        put("b_hq", j, b_in[:, SEG["hq"] + 128 * j:SEG["hq"] + 128 * j + 128])
        put("b_hf", j, b_in[:, SEG["hf"] + 128 * j:SEG["hf"] + 128 * j + 128])
        put("b_hgt", j, b_in[:, SEG["hgt"] + 128 * j:SEG["hgt"] + 128 * j + 128])
    hn = np.asarray(inp["hg_norm_g"], np.float32)
    put("hng", 0, np.concatenate([hn, hn], axis=1))
    for bi in range(3):
        for f in range(8):
            put("b_mg", bi * 8 + f, b_in[:, SEG["mg"] + bi * 1024 + f * 128:SEG["mg"] + bi * 1024 + f * 128 + 128])
    for nm in ("ln1g", "ln1b", "ln2g", "ln2b"):
        src = np.asarray(inp[{"ln1g": "ln1_g", "ln1b": "ln1_b", "ln2g": "ln2_g", "ln2b": "ln2_b"}[nm]], np.float32)
        for k in range(8):
            put(nm, k, src[:, 128 * k:128 * k + 128])
    hglog = np.ascontiguousarray(np.asarray(inp["hg_lb_logits"], np.float32).reshape(DEPTH, 2, 128).transpose(2, 1, 0))
    return PL, hglog


def make_in_maps(inp):
    f = lambda k: np.ascontiguousarray(np.asarray(inp[k], np.float32))
    PL, hglog = _pack_params(inp)
    consts = _consts()
    shared = {"w_in": f("w_in"), "b_in": f("b_in"), "w_out": f("w_out"), "w_ff_up": f("w_ff_up"), "w_ff_down": f("w_ff_down"),
              "w_up_gla": f("w_up_gla"), "w_up_ml": f("w_up_ml"), "w_up_hg": f("w_up_hg"), "w_gla_gate": f("w_gla_gate"),
              "w_ml_q": f("w_ml_q"), "w_ml_k": f("w_ml_k"), "PL": PL, "hglog": hglog}
    shared.update(consts)
    xp, xs = f("x_prompt"), f("x_sample")
    st = {k: f(k) for k in ("state_gla", "state_mlstm_C", "state_mlstm_n", "state_mlstm_m", "state_mlstm_conv", "state_hgrn")}
    maps = []
    for c in range(NCORES):
        xt = np.empty((2, T, D), np.float32)
        for h in range(2):
            xt[h, 0:NP] = xp[c, h * NP:(h + 1) * NP]
            xt[h, NP:T] = xs[c * 16 + h * NSEQ:c * 16 + (h + 1) * NSEQ].reshape(NS, D)
        xTc = np.ascontiguousarray(xt.reshape(2, T, 8, 128).transpose(0, 3, 2, 1))
        m = dict(shared)
        m["xT"] = xTc
        sl = slice(c * 16, (c + 1) * 16)
        m["sgla"] = np.ascontiguousarray(st["state_gla"][:, sl])
        m["smC"] = np.ascontiguousarray(st["state_mlstm_C"][:, sl])
        m["smn"] = np.ascontiguousarray(st["state_mlstm_n"][:, sl])
        m["smm"] = np.ascontiguousarray(st["state_mlstm_m"][:, sl])
        m["smconv"] = np.ascontiguousarray(st["state_mlstm_conv"][:, sl])
        m["shg"] = np.ascontiguousarray(st["state_hgrn"][:, sl])
        maps.append(m)
    return maps


def assemble(results):
    yp = np.empty((NCORES, 2 * NP, D), np.float32)
    ys = np.empty((NCORES * 16, 4, D), np.float32)
    for c, r in enumerate(results):
        y = np.asarray(r["yT"]).transpose(0, 3, 2, 1).reshape(2, T, D)
        for h in range(2):
            yp[c, h * NP:(h + 1) * NP] = y[h, 0:NP]
            ys[c * 16 + h * NSEQ:c * 16 + (h + 1) * NSEQ] = y[h, NP:T].reshape(NSEQ, 4, D)
    cat = lambda k: np.ascontiguousarray(np.concatenate([np.asarray(r[k]) for r in results], axis=1)).astype(np.float32)
    return (yp, ys, cat("gla_p"), cat("mC_p"), cat("mn_p"), cat("mm_p"), cat("mconv_p"), cat("hg_p"),
            cat("gla_s"), cat("mC_s"), cat("mn_s"), cat("mm_s"), cat("mconv_s"), cat("hg_s"))


def kernel(**inputs):
    nc, _ = build_program()
    in_maps = make_in_maps(inputs)
    res = run_bass_kernel_spmd(nc, in_maps, core_ids=list(range(NCORES)))
    return assemble(res.results)
```
